# Optimizing a Trainium2 kernel written in Bass

```python
import math
import jax, jax.numpy as jnp
from jax import lax
import numpy as np

D_MODEL = 1024
BATCH = 8
SEQ = 2048
DEPTH = 4

GRID_W = 64
CTX_LEN = 256
N_MIXERS = 2
DA_HEADS = 8
DA_HEAD_DIM = D_MODEL // DA_HEADS // 2
DA_V_DIM = 2 * DA_HEAD_DIM
Q_BLOCK = 128
ROPE_THETA = 10000.0
CHUNK = 128
SG_WIDTH = D_MODEL
SG_GROUPS = 8
SG_GROUP_DIM = SG_WIDTH // SG_GROUPS
D_FF = 2816
N_MOD = 9
RMS_EPS = 1e-6
LN_EPS = 1e-5
N_A_LAYERS = (DEPTH + 1) // 2
N_B_LAYERS = DEPTH // 2

kernel_name = "hybrid_diffattn_sgmlp_macaron_ctxprefix"


def rms_norm(x, g, eps=RMS_EPS):
    xf = x.astype(jnp.float32)
    y = xf * lax.rsqrt(jnp.mean(xf * xf, axis=-1, keepdims=True) + eps)
    return (y * g.astype(jnp.float32)).astype(x.dtype)


def layer_norm(x, g, b, eps=LN_EPS):
    xf = x.astype(jnp.float32)
    mu = jnp.mean(xf, axis=-1, keepdims=True)
    xc = xf - mu
    y = xc * lax.rsqrt(jnp.mean(xc * xc, axis=-1, keepdims=True) + eps)
    return (y * g.astype(jnp.float32) + b.astype(jnp.float32)).astype(x.dtype)


def modulate(x, g, shift, scale):
    return rms_norm(x, g) * (1 + scale) + shift


def ffn_sublayer(h, shift, scale, gate, g, w_gu, w_down):
    xn = modulate(h, g, shift, scale)
    gu = xn @ w_gu
    a, u = gu[..., :D_FF], gu[..., D_FF:]
    y = (jax.nn.silu(a) * u) @ w_down
    return h + 0.5 * gate * y


def axial_rope_tables(n_tokens, dtype):
    rows_n = n_tokens // GRID_W
    row = jnp.repeat(jnp.arange(rows_n), GRID_W)
    col = jnp.tile(jnp.arange(GRID_W), rows_n)
    n_freq = DA_HEAD_DIM // 4
    inv = ROPE_THETA ** (-jnp.arange(n_freq, dtype=jnp.float32) / n_freq)
    pos = jnp.stack([row, col], axis=-1).astype(jnp.float32)
    ang = pos[:, :, None] * inv
    return jnp.cos(ang).astype(dtype), jnp.sin(ang).astype(dtype)


def apply_axial_rope(x, cos, sin):
    shp = x.shape
    xr = x.reshape(shp[:-1] + (2, 2, DA_HEAD_DIM // 4))
    x1, x2 = xr[..., 0, :], xr[..., 1, :]
    cb, sb = cos[:, None, None], sin[:, None, None]
    out = jnp.stack([x1 * cb - x2 * sb, x2 * cb + x1 * sb], axis=-2)
    return out.reshape(shp)


def diff_attend(q, k, v, lam):
    s = jnp.einsum('bqhrd,bkhrd->bhrqk', q, k,
                   preferred_element_type=jnp.float32) * (DA_HEAD_DIM ** -0.5)
    p = jax.nn.softmax(s, axis=-1)
    w = p[:, :, 0] - lam * p[:, :, 1]
    return jnp.einsum('bhqk,bkhe->bqhe', w.astype(v.dtype), v)


def diff_head_out(o, subln_g, lam_init, w_out):
    o = rms_norm(o, subln_g) * (1.0 - lam_init)
    return o.reshape(o.shape[:2] + (D_MODEL,)) @ w_out


def diff_attention_mixer(xl, xc, w_in, w_out, lam_vecs, subln_g, lam_init, cos, sin, ctx_out):
    B, S, _ = xl.shape
    C = xc.shape[1]
    D = D_MODEL
    lv = lam_vecs.astype(jnp.float32)
    lam = (jnp.exp(jnp.sum(lv[0] * lv[1])) - jnp.exp(jnp.sum(lv[2] * lv[3])) + lam_init)
    qkv = xl @ w_in
    q = apply_axial_rope(qkv[..., :D].reshape(B, S, DA_HEADS, 2, DA_HEAD_DIM), cos, sin)
    k = apply_axial_rope(qkv[..., D:2 * D].reshape(B, S, DA_HEADS, 2, DA_HEAD_DIM), cos, sin)
    v = qkv[..., 2 * D:].reshape(B, S, DA_HEADS, DA_V_DIM)
    kv_c = xc @ w_in[:, D:]
    k_c = kv_c[..., :D].reshape(B, C, DA_HEADS, 2, DA_HEAD_DIM)
    v_c = kv_c[..., D:].reshape(B, C, DA_HEADS, DA_V_DIM)
    k_all = jnp.concatenate([k_c, k], axis=1)
    v_all = jnp.concatenate([v_c, v], axis=1)
    nb = S // Q_BLOCK
    qb = q.reshape(B, nb, Q_BLOCK, DA_HEADS, 2, DA_HEAD_DIM).transpose(1, 0, 2, 3, 4, 5)
    ob = lax.map(lambda qi: diff_attend(qi, k_all, v_all, lam), qb)
    o = ob.transpose(1, 0, 2, 3, 4).reshape(B, S, DA_HEADS, DA_V_DIM)
    y = diff_head_out(o, subln_g, lam_init, w_out)
    if ctx_out:
        q_c = (xc @ w_in[:, :D]).reshape(B, C, DA_HEADS, 2, DA_HEAD_DIM)
        o_c = diff_attend(q_c, k_c, v_c, lam)
        y_c = diff_head_out(o_c, subln_g, lam_init, w_out)
    else:
        y_c = None
    return y, y_c


def spatial_gating_mlp(x, w_in, ln_g, ln_b, w_s, b_s, w_out):
    B, L, _ = x.shape
    z = jax.nn.gelu(x @ w_in)
    u, v = z[..., :SG_WIDTH], z[..., SG_WIDTH:]
    v = layer_norm(v, ln_g, ln_b)
    nc = L // CHUNK
    vg = v.reshape(B, nc, CHUNK, SG_GROUPS, SG_GROUP_DIM)
    mixed = jnp.einsum('gts,bnsgc->bntgc', w_s, vg) + b_s.T[None, None, :, :, None]
    return (u * mixed.reshape(B, L, SG_WIDTH)) @ w_out


def setup_inputs(seed: int = 0) -> dict:
    key = jax.random.key(seed)
    ks = jax.random.split(key, 24)
    f32 = jnp.float32
    D = D_MODEL
    nrm = lambda k, shp, s: jax.random.normal(k, shp, f32) * s
    return {
        "x": nrm(ks[0], (BATCH, SEQ, D), 1.0),
        "c": nrm(ks[1], (BATCH, D), 1.0),
        "ctx": nrm(ks[2], (BATCH, CTX_LEN, D), 1.0),
        "c_ctx": nrm(ks[3], (D,), 1.0),
        "w_mod": nrm(ks[4], (DEPTH, D, N_MOD * D), 0.5 * D ** -0.5),
        "b_mod": nrm(ks[5], (DEPTH, N_MOD * D), 0.01),
        "norm_g": 1.0 + nrm(ks[6], (DEPTH, 3, D), 0.01),
        "w_ffn_gu": nrm(ks[7], (DEPTH, 2, D, 2 * D_FF), D ** -0.5),
        "w_ffn_down": nrm(ks[8], (DEPTH, 2, D_FF, D), D_FF ** -0.5),
        "da_w_in": nrm(ks[9], (N_A_LAYERS, D, 3 * D), D ** -0.5),
        "da_w_out": nrm(ks[10], (N_A_LAYERS, D, D), D ** -0.5),
        "da_lambda": nrm(ks[11], (N_A_LAYERS, 4, DA_HEAD_DIM), 0.1),
        "da_subln_g": 1.0 + nrm(ks[12], (N_A_LAYERS, DA_V_DIM), 0.01),
        "sg_w_in": nrm(ks[13], (N_B_LAYERS, D, 2 * SG_WIDTH), D ** -0.5),
        "sg_ln_g": 1.0 + nrm(ks[14], (N_B_LAYERS, SG_WIDTH), 0.01),
        "sg_ln_b": nrm(ks[15], (N_B_LAYERS, SG_WIDTH), 0.01),
        "sg_w_s": nrm(ks[16], (N_B_LAYERS, SG_GROUPS, CHUNK, CHUNK), CHUNK ** -0.5),
        "sg_b_s": 1.0 + nrm(ks[17], (N_B_LAYERS, SG_GROUPS, CHUNK), 0.01),
        "sg_w_out": nrm(ks[18], (N_B_LAYERS, SG_WIDTH, D), SG_WIDTH ** -0.5),
        "final_g": 1.0 + nrm(ks[19], (D,), 0.01),
    }


def reference(x, c, ctx, c_ctx, w_mod, b_mod, norm_g, w_ffn_gu, w_ffn_down,
              da_w_in, da_w_out, da_lambda, da_subln_g,
              sg_w_in, sg_ln_g, sg_ln_b, sg_w_s, sg_b_s, sg_w_out, final_g):
    B, S, _ = x.shape
    cos, sin = axial_rope_tables(S, x.dtype)
    last_ctx_layer = max(i for i in range(DEPTH) if i % N_MIXERS == 0)
    sc = jax.nn.silu(c)
    scc = jax.nn.silu(c_ctx)
    h, hc = x, ctx
    for i in range(DEPTH):
        j = i // N_MIXERS
        mode = 'full' if i < last_ctx_layer else ('kv' if i == last_ctx_layer else 'none')
        mx = (sc @ w_mod[i] + b_mod[i]).reshape(B, N_MOD, 1, D_MODEL)
        mx = [mx[:, k] for k in range(N_MOD)]
        mc = None
        if mode != 'none':
            mc = (scc @ w_mod[i] + b_mod[i]).reshape(N_MOD, D_MODEL)
        h = ffn_sublayer(h, mx[0], mx[1], mx[2], norm_g[i, 0], w_ffn_gu[i, 0], w_ffn_down[i, 0])
        if mode != 'none':
            hc = ffn_sublayer(hc, mc[0], mc[1], mc[2], norm_g[i, 0], w_ffn_gu[i, 0], w_ffn_down[i, 0])
        xn = modulate(h, norm_g[i, 1], mx[3], mx[4])
        xcn = modulate(hc, norm_g[i, 1], mc[3], mc[4]) if mode != 'none' else None
        if i % N_MIXERS == 0:
            lam_init = 0.8 - 0.6 * math.exp(-0.3 * i)
            y, yc = diff_attention_mixer(xn, xcn, da_w_in[j], da_w_out[j], da_lambda[j],
                                         da_subln_g[j], lam_init, cos, sin, mode == 'full')
        else:
            y = spatial_gating_mlp(xn, sg_w_in[j], sg_ln_g[j], sg_ln_b[j], sg_w_s[j], sg_b_s[j], sg_w_out[j])
            yc = (spatial_gating_mlp(xcn, sg_w_in[j], sg_ln_g[j], sg_ln_b[j], sg_w_s[j], sg_b_s[j], sg_w_out[j])
                  if mode == 'full' else None)
        h = h + mx[5] * y
        if mode == 'full':
            hc = hc + mc[5] * yc
        h = ffn_sublayer(h, mx[6], mx[7], mx[8], norm_g[i, 2], w_ffn_gu[i, 1], w_ffn_down[i, 1])
        if mode == 'full':
            hc = ffn_sublayer(hc, mc[6], mc[7], mc[8], norm_g[i, 2], w_ffn_gu[i, 1], w_ffn_down[i, 1])
    return rms_norm(h, final_g)
```

```python
import math
from contextlib import ExitStack
import numpy as np
import concourse.bass as bass
import concourse.mybir as mybir
from concourse.bass_utils import run_bass_kernel_spmd

F32 = mybir.dt.float32
BF16 = mybir.dt.bfloat16
AF = mybir.ActivationFunctionType
ALU = mybir.AluOpType
AX = mybir.AxisListType

D = 1024
S = 2048
C = 256
T = S + C
DFF = 2816
NFC = DFF // 128
DEPTH = 4
NMOD = 9
NCORES = 8
GROUPS = [(0, 512), (512, 512), (1024, 512), (1536, 512), (2048, 256)]
LAT = [0, 1, 2, 3]
CTXG = 4
RMS_EPS = 1e-6
LN_EPS = 1e-5
NF = 2
NPIECE = NFC // NF
ARENA_WORDS = 32900


class Buf:
    __slots__ = ("name", "lw", "rd")

    def __init__(self, name):
        self.name = name
        self.lw = None
        self.rd = {}


class Op:
    __slots__ = ("eng", "fn", "deps", "needs_inc", "sig", "dsem", "idx")


class Sched:
    def __init__(self, nc, stack):
        self.nc = nc
        self.stack = stack
        self.ops = []
        self.E = {"pe": nc.tensor, "act": nc.scalar, "dve": nc.vector, "pool": nc.gpsimd, "sp": nc.sync}
        self.esem = {e: stack.enter_context(nc.semaphore("es_" + e)) for e in ("pe", "act", "dve", "pool")}
        self.last = {}
        self.pending_bar = {}
        self.dsems = []
        self.free_ds = []
        self.phase_ds = []

    def new_dsem(self, name, persistent=False):
        if self.free_ds:
            s = self.free_ds.pop()
        else:
            s = [self.stack.enter_context(self.nc.semaphore("ds%d" % len(self.dsems))), 0, None]
            self.dsems.append(s)
        if not persistent:
            self.phase_ds.append(s)
        return s

    def op(self, eng, fn, reads=(), writes=(), dsem=None):
        o = Op()
        o.eng = eng
        o.fn = fn
        o.needs_inc = False
        o.dsem = dsem
        o.sig = None
        o.idx = len(self.ops)
        deps = {}
        for b in reads:
            if b.lw is not None:
                deps[b.lw.idx] = b.lw
        for b in writes:
            if b.lw is not None:
                deps[b.lw.idx] = b.lw
            for r in b.rd.values():
                deps[r.idx] = r
        if eng in self.pending_bar:
            for d in self.pending_bar.pop(eng):
                deps[d.idx] = d
        dl = []
        for d in deps.values():
            if d.eng == "pe" and eng == "pe" and d.dsem is None:
                continue
            d.needs_inc = True
            dl.append(d)
        o.deps = dl
        key = eng if dsem is None else ("d", id(dsem))
        for b in reads:
            b.rd[key] = o
        for b in writes:
            b.lw = o
            b.rd = {}
        if dsem is not None:
            dsem[1] += 16
            o.sig = (dsem[0], dsem[1])
            dsem[2] = o
        else:
            self.last[eng] = o
        self.ops.append(o)
        return o

    def barrier(self):
        deps = list(self.last.values()) + [d[2] for d in self.dsems if d[2] is not None]
        for d in deps:
            d.needs_inc = True
        for e in self.E:
            self.pending_bar[e] = list(deps)
        self.free_ds.extend(self.phase_ds)
        self.phase_ds = []

    def emit(self):
        cnt = {e: 0 for e in self.esem}
        for o in self.ops:
            if o.dsem is None and o.needs_inc:
                cnt[o.eng] += 1
                o.sig = (self.esem[o.eng], cnt[o.eng])
        seen = {e: {} for e in self.E}
        nwait = 0
        for o in self.ops:
            need = {}
            for d in o.deps:
                sem, val = d.sig
                k = id(sem)
                if k not in need or need[k][1] < val:
                    need[k] = (sem, val)
            eng = self.E[o.eng]
            sn = seen[o.eng]
            for k, (sem, val) in need.items():
                if sn.get(k, 0) < val:
                    eng.wait_ge(sem, val)
                    sn[k] = val
                    nwait += 1
            ins = o.fn()
            if o.dsem is not None:
                ins.then_inc(o.sig[0], 16)
            elif o.needs_inc:
                ins.then_inc(o.sig[0], 1)
        return nwait

    def final_wait(self, eng, ops):
        e = self.E[eng]
        need = {}
        for d in ops:
            sem, val = d.sig
            if id(sem) not in need or need[id(sem)][1] < val:
                need[id(sem)] = (sem, val)
        for sem, val in need.values():
            e.wait_ge(sem, val)


class Arena:
    def __init__(self, ap, words):
        self.ap = ap
        self.words = words
        self.off = 0

    def reset(self):
        self.off = 0

    def alloc(self, shape, dtype):
        n = 1
        for s in shape:
            n *= s
        w = n if dtype == F32 else (n + 1) // 2
        assert self.off + w <= self.words, ("arena overflow", self.off, w, self.words)
        a = self.ap[:, self.off:self.off + w]
        self.off += w
        if dtype != F32:
            a = a.bitcast(dtype)
        if len(shape) == 2:
            a = a.rearrange("p (a b) -> p a b", a=shape[0])
        elif len(shape) == 3:
            a = a.rearrange("p (a b c) -> p a b c", a=shape[0], b=shape[1])
        return a


def lam_init_of(i):
    return 0.8 - 0.6 * math.exp(-0.3 * i)


def build_program(nlayers=DEPTH, debug_dump=False, stop_stage=None, dbg_attn=False):
    nc = bass.Bass("TRN2", target_bir_lowering=False)
    dt_in = lambda name, shape: nc.dram_tensor(name, list(shape), F32, kind="ExternalInput").ap()
    x_d = dt_in("x", (S, D))
    ctx_d = dt_in("ctx", (C, D))
    cT_d = dt_in("cT", (128, 16))
    wmod_d = dt_in("w_mod", (DEPTH, D, NMOD * D))
    bmodT_d = dt_in("b_modT", (128, DEPTH * 72))
    ngT_d = dt_in("norm_gT", (128, DEPTH * 24))
    wgu_d = dt_in("w_ffn_gu", (DEPTH, 2, D, 2 * DFF))
    wdn_d = dt_in("w_ffn_down", (DEPTH, 2, DFF, D))
    dawin_d = dt_in("da_w_in", (2, D, 3 * D))
    dawout_d = dt_in("da_w_out", (2, D, D))
    dalam_d = dt_in("da_lambda_b", (128, 512))
    dasg_d = dt_in("da_subln_gT", (128, 2))
    sgwin_d = dt_in("sg_w_in", (2, D, 2 * D))
    sglng_d = dt_in("sg_ln_gT", (128, 16))
    sglnb_d = dt_in("sg_ln_bT", (128, 16))
    sgws_d = dt_in("sg_w_sT", (2, 128, 8 * 128))
    sgbs_d = dt_in("sg_b_s_b", (128, 2 * 1024))
    sgwout_d = dt_in("sg_w_out", (2, D, D))
    fg_d = dt_in("final_g_b", (128, D))
    ropeC_d = dt_in("ropeC", (128, S))
    ropeT_d = dt_in("ropeT", (128, S))
    nrows_out = T if debug_dump else S
    out_d = nc.dram_tensor("out", [nrows_out, D], F32, kind="ExternalOutput").ap()

    stack = ExitStack()
    with stack:
        sb = lambda name, shape, dt: stack.enter_context(nc.sbuf_tensor(name, list(shape), dt))
        XT = sb("XT", (128, 8, T), F32)
        MODT = sb("MODT", (128, DEPTH * 72 * 2), F32)
        SMALL = sb("SMALL", (128, 16 + 288 + 96 + 2 + 16 + 16), F32)
        PRM = sb("PRM", (128, 12 * 48), F32)
        CONST = sb("CONST", (128, 8), F32)
        ONES = sb("ONES", (128, 128), BF16)
        IDENT = sb("IDENT", (128, 128), F32)
        SCT = sb("SCT", (128, 16), BF16)
        ARENA_T = sb("ARENA", (128, ARENA_WORDS), F32)
        banks = [stack.enter_context(nc.psum_tensor("bk%d" % i, [128, 512], F32)) for i in range(8)]
        BK = [Buf("bk%d" % i) for i in range(8)]
        SC = Sched(nc, stack)
        AR = Arena(ARENA_T, ARENA_WORDS)

        def MM(out, lhsT, rhs, start, stop, reads, writes):
            return SC.op("pe", lambda: nc.tensor.matmul(out, lhsT, rhs, start=start, stop=stop), reads, writes)

        def TR(out, in_, reads, writes):
            return SC.op("pe", lambda: nc.tensor.transpose(out, in_, IDENT[:]), reads, writes)

        def ACT(out, in_, func, reads, writes, bias=None, scale=None):
            kw = {}
            if bias is not None:
                kw["bias"] = bias
            if scale is not None:
                kw["scale"] = scale
            return SC.op("act", lambda: nc.scalar.activation(out, in_, func, **kw), reads, writes)

        def TT(out, in0, in1, op, reads, writes, eng="dve"):
            e = SC.E[eng]
            return SC.op(eng, lambda: e.tensor_tensor(out, in0, in1, op), reads, writes)

        def TS(out, in0, s1, op0, reads, writes, s2=None, op1=None, eng="dve"):
            e = SC.E[eng]
            if op1 is None:
                return SC.op(eng, lambda: e.tensor_scalar(out, in0, s1, None, op0), reads, writes)
            return SC.op(eng, lambda: e.tensor_scalar(out, in0, s1, s2, op0, op1), reads, writes)

        def STT(out, in0, scalar, in1, op0, op1, reads, writes, eng="dve"):
            e = SC.E[eng]
            return SC.op(eng, lambda: e.scalar_tensor_tensor(out, in0, scalar, in1, op0, op1), reads, writes)

        def RECIP(out, in_, reads, writes):
            return SC.op("dve", lambda: nc.vector.reciprocal(out, in_), reads, writes)

        def RSUM(out, in_, reads, writes):
            return SC.op("dve", lambda: nc.vector.reduce_sum(out, in_, AX.X), reads, writes)

        def COPY(out, in_, reads, writes, eng="dve"):
            if eng == "act":
                return SC.op("act", lambda: nc.scalar.copy(out, in_), reads, writes)
            e = SC.E[eng]
            return SC.op(eng, lambda: e.tensor_copy(out, in_), reads, writes)

        def DMA(queue, out, in_, reads, writes, dsem):
            e = SC.E[queue]
            return SC.op(queue, lambda: e.dma_start(out=out, in_=in_), reads, writes, dsem=dsem)

        class Rot:
            def __init__(self, name, n, shape, dtype, dma=False):
                self.t = [AR.alloc(shape, dtype) for _ in range(n)]
                self.b = [Buf("%s%d" % (name, i)) for i in range(n)]
                self.ds = [SC.new_dsem("%s%d" % (name, i)) for i in range(n)] if dma else None
                self.i = -1
                self.n = n

            def next(self):
                self.i = (self.i + 1) % self.n
                if self.ds:
                    return self.t[self.i], self.b[self.i], self.ds[self.i]
                return self.t[self.i], self.b[self.i]

        XTb = [[Buf("xt%d_%d" % (d, g)) for g in range(5)] for d in range(8)]
        MODb = [Buf("mod%d" % i) for i in range(DEPTH)]
        SMALLb = Buf("small")
        CONSTb = Buf("const")
        LAMb = Buf("lam")
        ONESb = Buf("ones")
        IDENTb = Buf("ident")
        SCTb = Buf("sct")
        PRMb = [Buf("prm%d" % i) for i in range(12)]
        small_ds = SC.new_dsem("small")

        cT = SMALL[:, 0:16]
        bmodT = SMALL[:, 16:304]
        ngT = SMALL[:, 304:400]
        dasg = SMALL[:, 400:402]
        sglng = SMALL[:, 402:418]
        sglnb = SMALL[:, 418:434]
        for dst, src in ((cT, cT_d), (bmodT, bmodT_d), (ngT, ngT_d), (dasg, dasg_d), (sglng, sglng_d), (sglnb, sglnb_d)):
            DMA("sp", dst, src[:, :], (), (SMALLb,), small_ds)
        LAMB = AR.alloc((512,), F32)
        LTMP = AR.alloc((160,), F32)
        DMA("sp", LAMB, dalam_d[:, :], (), (LAMb,), small_ds)

        SC.op("dve", lambda: nc.vector.memset(CONST[:, 0:1], RMS_EPS), (), (CONSTb,))
        SC.op("dve", lambda: nc.vector.memset(CONST[:, 1:2], LN_EPS), (), (CONSTb,))
        SC.op("dve", lambda: nc.vector.memset(ONES[:], 1.0), (), (ONESb,))
        SC.op("pool", lambda: nc.gpsimd.memset(IDENT[:], 0.0), (), (IDENTb,))
        SC.op("pool", lambda: nc.gpsimd.affine_select(out=IDENT[:], in_=IDENT[:], pattern=[[-1, 128]],
                                                      compare_op=ALU.not_equal, fill=1.0, base=0,
                                                      channel_multiplier=1), (), (IDENTb,))
        EPS_RMS = CONST[:, 0:1]
        EPS_LN = CONST[:, 1:2]

        for j in range(2):
            li = lam_init_of(2 * j)
            lb = LAMB[:, j * 256:(j + 1) * 256]
            TT(LTMP[:, 0:64], lb[:, 0:64], lb[:, 64:128], ALU.mult, (LAMb,), (LAMb,))
            TT(LTMP[:, 64:128], lb[:, 128:192], lb[:, 192:256], ALU.mult, (LAMb,), (LAMb,))
            RSUM(LTMP[:, 128:129], LTMP[:, 0:64], (LAMb,), (LAMb,))
            RSUM(LTMP[:, 129:130], LTMP[:, 64:128], (LAMb,), (LAMb,))
            ACT(LTMP[:, 130:132], LTMP[:, 128:130], AF.Exp, (LAMb,), (LAMb,))
            TT(LTMP[:, 132:133], LTMP[:, 131:132], LTMP[:, 130:131], ALU.subtract, (LAMb,), (LAMb,))
            TS(CONST[:, 4 + j:5 + j], LTMP[:, 132:133], -li, ALU.add, (LAMb,), (CONSTb,))
            TS(CONST[:, 6 + j:7 + j], dasg[:, j:j + 1], 1.0 - li, ALU.mult, (SMALLb,), (CONSTb,))

        stage = Rot("stage", 2, (D,), F32, dma=True)
        for c in range(T // 128):
            st, stb, sds = stage.next()
            src = x_d[c * 128:(c + 1) * 128, :] if c < 16 else ctx_d[(c - 16) * 128:(c - 15) * 128, :]
            DMA("sp", st, src, (), (stb,), sds)
            g = min(c // 4, 4)
            for half in range(2):
                bk = (c % 2) * 2 + half
                for dd in range(4):
                    d = half * 4 + dd
                    TR(banks[bk][:, dd * 128:(dd + 1) * 128], st[:, d * 128:(d + 1) * 128], (stb, IDENTb), (BK[bk],))
                COPY(XT[:, half * 4:half * 4 + 4, c * 128:(c + 1) * 128],
                     banks[bk][:, :].rearrange("p (a b) -> p a b", a=4),
                     (BK[bk],), [XTb[half * 4 + dd][g] for dd in range(4)], eng=("act" if half else "dve"))

        ACT(SCT[:, :].rearrange("p (k c) -> p k c", c=2), cT.rearrange("p (c k) -> p k c", c=2), AF.Silu,
            (SMALLb,), (SCTb,))
        wm = Rot("wm", 2, (8, 1024), BF16, dma=True)
        for i in range(nlayers):
            mbk = 4 + (i % 2)
            for q in range(9):
                wt, wb, wds = wm.next()
                DMA("pool", wt, wmod_d[i].rearrange("(k p) f -> p k f", p=128)[:, :, q * 1024:(q + 1) * 1024],
                    (), (wb,), wds)
                for m in range(8):
                    mm_ = q * 8 + m
                    for k in range(8):
                        MM(banks[mbk][:, mm_ * 2:mm_ * 2 + 2], wt[:, k, m * 128:(m + 1) * 128],
                           SCT[:, k * 2:k * 2 + 2], k == 0, k == 7, (wb, SCTb), (BK[mbk],))
            for col in range(2):
                TT(MODT[:, i * 144:(i + 1) * 144].rearrange("p (m c) -> p m c", c=2)[:, :, col],
                   banks[mbk][:, 0:144].rearrange("p (m c) -> p m c", c=2)[:, :, col],
                   bmodT[:, i * 72:(i + 1) * 72], ALU.add, (BK[mbk], SMALLb), (MODb[i],))

        def mod_ap(i, kmod, col):
            return MODT[:, i * 144 + kmod * 16:i * 144 + kmod * 16 + 16].rearrange("p (d c) -> p d c", c=2)[:, :, col]

        def prep_params(i, s, gate_mul):
            slot = i * 3 + s
            base = slot * 48
            for col in range(2):
                A = PRM[:, base + col * 8:base + col * 8 + 8]
                SH = PRM[:, base + 16 + col * 8:base + 16 + col * 8 + 8]
                G = PRM[:, base + 32 + col * 8:base + 32 + col * 8 + 8]
                STT(A, mod_ap(i, 3 * s + 1, col), 1.0, ngT[:, i * 24 + s * 8:i * 24 + s * 8 + 8], ALU.add, ALU.mult,
                    (MODb[i], SMALLb), (PRMb[slot],))
                COPY(SH, mod_ap(i, 3 * s, col), (MODb[i],), (PRMb[slot],))
                TS(G, mod_ap(i, 3 * s + 2, col), gate_mul, ALU.mult, (MODb[i],), (PRMb[slot],))
            return slot

        def prm(slot, which, g, d):
            col = 1 if g == CTXG else 0
            o = slot * 48 + which * 16 + col * 8 + d
            return PRM[:, o:o + 1]

        def prenorm(groups, slot, XN, XNb, ssbanks, FR):
            SQ = Rot("sq", 1, (8, 512), BF16)
            RS = Rot("rs", 1, (512,), F32)
            TMP = FR
            for n, g in enumerate(groups):
                g0, gs = GROUPS[g]
                sq, sqb = SQ.next()
                bk = ssbanks[n % len(ssbanks)]
                for d in range(8):
                    ACT(sq[:, d, 0:gs], XT[:, d, g0:g0 + gs], AF.Square, (XTb[d][g],), (sqb,))
                for d in range(8):
                    MM(banks[bk][:, 0:gs], ONES[:], sq[:, d, 0:gs], d == 0, d == 7, (ONESb, sqb), (BK[bk],))
                rs, rsb = RS.next()
                ACT(rs[:, 0:gs], banks[bk][:, 0:gs], AF.Sqrt, (BK[bk], CONSTb), (rsb,), bias=EPS_RMS, scale=1.0 / D)
                RECIP(rs[:, 0:gs], rs[:, 0:gs], (rsb,), (rsb,))
                for d in range(8):
                    tp, tpb = TMP.next()
                    STT(tp[:, 0:gs], XT[:, d, g0:g0 + gs], prm(slot, 0, g, d), rs[:, 0:gs], ALU.mult, ALU.mult,
                        (XTb[d][g], PRMb[slot], rsb), (tpb,))
                    ACT(XN[:, d, g0:g0 + gs], tp[:, 0:gs], AF.Identity, (tpb, PRMb[slot]), (XNb[d][g],),
                        bias=prm(slot, 1, g, d))

        def ffn_phase(i, which, groups):
            SC.barrier()
            AR.reset()
            slot = prep_params(i, 0 if which == 0 else 2, 0.5)
            XN = AR.alloc((8, T), BF16)
            XNb = [[Buf("xn%d_%d" % (d, g)) for g in range(5)] for d in range(8)]
            WG = Rot("wg", 3, (8, NF * 128), BF16, dma=True)
            WU = Rot("wu", 3, (8, NF * 128), BF16, dma=True)
            WD = Rot("wd", 3, (NF, D), BF16, dma=True)
            FR = Rot("fr", 8, (512,), F32)
            SG = FR
            HT = Rot("ht", 4, (512,), BF16)
            wguv = wgu_d[i, which].rearrange("(k p) f -> p k f", p=128)
            wdnv = wdn_d[i, which].rearrange("(f p) d -> p f d", p=128)
            loaded = {}

            def load_piece(p):
                f0 = p * NF * 128
                wg = WG.next()
                wu = WU.next()
                wd = WD.next()
                DMA("pool", wg[0], wguv[:, :, f0:f0 + NF * 128], (), (wg[1],), wg[2])
                DMA("pool", wu[0], wguv[:, :, DFF + f0:DFF + f0 + NF * 128], (), (wu[1],), wu[2])
                DMA("pool", wd[0], wdnv[:, p * NF:(p + 1) * NF, :], (), (wd[1],), wd[2])
                loaded[p] = (wg, wu, wd)

            load_piece(0)
            load_piece(1)
            prenorm(groups, slot, XN, XNb, [7], FR)
            items = [(p, g) for p in range(NPIECE) for g in groups]
            ybanks = [4, 5, 6]
            state = {"y": 0}

            def emit_gu(p, g):
                g0, gs = GROUPS[g]
                wg, wu, wd = loaded[p]
                hts = []
                for fi in range(NF):
                    bg, bu = (0, 1) if fi % 2 == 0 else (2, 3)
                    for k in range(8):
                        MM(banks[bg][:, 0:gs], wg[0][:, k, fi * 128:(fi + 1) * 128], XN[:, k, g0:g0 + gs], k == 0,
                           k == 7, (wg[1], XNb[k][g]), (BK[bg],))
                    for k in range(8):
                        MM(banks[bu][:, 0:gs], wu[0][:, k, fi * 128:(fi + 1) * 128], XN[:, k, g0:g0 + gs], k == 0,
                           k == 7, (wu[1], XNb[k][g]), (BK[bu],))
                    sg, sgb = SG.next()
                    ht, htb = HT.next()
                    ACT(sg[:, 0:gs], banks[bg][:, 0:gs], AF.Silu, (BK[bg],), (sgb,))
                    TT(ht[:, 0:gs], sg[:, 0:gs], banks[bu][:, 0:gs], ALU.mult, (sgb, BK[bu]), (htb,))
                    hts.append((ht, htb))
                return hts

            def emit_down(p, g, hts):
                g0, gs = GROUPS[g]
                wg, wu, wd = loaded[p]
                for d in range(8):
                    yb = ybanks[state["y"] % 3]
                    state["y"] += 1
                    for fi in range(NF):
                        MM(banks[yb][:, 0:gs], wd[0][:, fi, d * 128:(d + 1) * 128], hts[fi][0][:, 0:gs], fi == 0,
                           fi == NF - 1, (wd[1], hts[fi][1]), (BK[yb],))
                    STT(XT[:, d, g0:g0 + gs], banks[yb][:, 0:gs], prm(slot, 2, g, d), XT[:, d, g0:g0 + gs], ALU.mult,
                        ALU.add, (BK[yb], PRMb[slot], XTb[d][g]), (XTb[d][g],))

            prev = None
            for n, (p, g) in enumerate(items):
                hts = emit_gu(p, g)
                if prev is not None:
                    emit_down(*prev)
                if g == groups[0] and p + 2 < NPIECE:
                    load_piece(p + 2)
                prev = (p, g, hts)
            emit_down(*prev)

        def attn_phase(i, j, qgroups, kvgroups):
            SC.barrier()
            AR.reset()
            slot = prep_params(i, 1, 1.0)
            XN = AR.alloc((8, T), BF16)
            XNb = [[Buf("xn%d_%d" % (d, g)) for g in range(5)] for d in range(8)]
            ROPC = AR.alloc((S,), F32)
            ROPT = AR.alloc((S,), F32)
            ROPb = Buf("rope")
            rds = SC.new_dsem("rope%d" % i)
            DMA("sp", ROPC, ropeC_d[:, :], (), (ROPb,), rds)
            DMA("sp", ROPT, ropeT_d[:, :], (), (ROPb,), rds)
            WQ = Rot("wq", 2, (8, 128), BF16, dma=True)
            WK = Rot("wk", 2, (8, 128), BF16, dma=True)
            WV = Rot("wv", 2, (8, 128), BF16, dma=True)
            WO = Rot("wo", 3, (D,), BF16, dma=True)
            QTA = AR.alloc((T,), BF16)
            QTB = AR.alloc((T,), BF16)
            QTb = [Buf("qt%d" % g) for g in range(5)]
            SC.op("dve", lambda: nc.vector.memset(QTA[64:128, :], 0.0), (), QTb)
            SC.op("dve", lambda: nc.vector.memset(QTB[0:64, :], 0.0), (), QTb)
            KT = Rot("kt", 2, (T,), BF16)
            VV = Rot("vv", 2, (18, 128), BF16)
            PP = [Rot("pp%d" % r, 2, (512,), BF16) for r in range(2)]
            FR = Rot("fr", 6, (512,), F32)
            RA = RB = R1 = R2 = T1 = T2 = OO = ZR = ZI = FR
            HR = Rot("hr", 5, (512,), BF16)
            OSQ = ON = HR
            winv = dawin_d[j].rearrange("(k p) f -> p k f", p=128)
            woutv = dawout_d[j].rearrange("(h p) d -> p h d", p=128)
            NLAM = CONST[:, 4 + j:5 + j]
            SGC = CONST[:, 6 + j:7 + j]
            misc = [6, 7]
            mstate = {"m": 0}

            def mbank():
                b = misc[mstate["m"] % 2]
                mstate["m"] += 1
                return b

            prenorm(kvgroups, slot, XN, XNb, [6, 7], FR)
            hw = {}

            def load_head(h):
                wq = WQ.next()
                wk = WK.next()
                wv = WV.next()
                wo = WO.next()
                DMA("pool", wq[0], winv[:, :, h * 128:(h + 1) * 128], (), (wq[1],), wq[2])
                DMA("pool", wk[0], winv[:, :, D + h * 128:D + (h + 1) * 128], (), (wk[1],), wk[2])
                DMA("pool", wv[0], winv[:, :, 2 * D + h * 128:2 * D + (h + 1) * 128], (), (wv[1],), wv[2])
                DMA("pool", wo[0], woutv[:, h, :], (), (wo[1],), wo[2])
                hw[h] = (wq, wk, wv, wo)

            def rope_parts(bk, g0, gs):
                qs, qsb = FR.next()
                COPY(qs[:, 0:gs], banks[bk][:, 0:gs], (BK[bk],), (qsb,), eng="act")
                ra, rab = RA.next()
                rb, rbb = RB.next()
                TT(ra[:, 0:gs], qs[:, 0:gs], ROPC[:, g0:g0 + gs], ALU.mult, (qsb, ROPb), (rab,))
                for blk in range(4):
                    sp_ = blk ^ 1
                    TT(rb[blk * 32:(blk + 1) * 32, 0:gs], qs[sp_ * 32:(sp_ + 1) * 32, 0:gs],
                       ROPT[sp_ * 32:(sp_ + 1) * 32, g0:g0 + gs], ALU.mult, (qsb, ROPb), (rbb,))
                return ra, rab, rb, rbb

            proj = {}

            def project_q(h):
                wq = hw[h][0]
                for g in qgroups:
                    g0, gs = GROUPS[g]
                    bk = mbank()
                    for k in range(8):
                        MM(banks[bk][:, 0:gs], wq[0][:, k, :], XN[:, k, g0:g0 + gs], k == 0, k == 7,
                           (wq[1], XNb[k][g]), (BK[bk],))
                    if g == CTXG:
                        COPY(QTA[0:64, g0:g0 + gs], banks[bk][0:64, 0:gs], (BK[bk],), (QTb[g],), eng="act")
                        COPY(QTB[64:128, g0:g0 + gs], banks[bk][64:128, 0:gs], (BK[bk],), (QTb[g],), eng="act")
                    else:
                        ra, rab, rb, rbb = rope_parts(bk, g0, gs)
                        TT(QTA[0:64, g0:g0 + gs], ra[0:64, 0:gs], rb[0:64, 0:gs], ALU.add, (rab, rbb), (QTb[g],))
                        TT(QTB[64:128, g0:g0 + gs], ra[64:128, 0:gs], rb[64:128, 0:gs], ALU.add, (rab, rbb),
                           (QTb[g],))

            def project_kv(h):
                wq, wk, wv, wo = hw[h]
                kt, ktb = KT.next()
                vv, vvb = VV.next()
                for g in kvgroups:
                    g0, gs = GROUPS[g]
                    bk = mbank()
                    for k in range(8):
                        MM(banks[bk][:, 0:gs], wk[0][:, k, :], XN[:, k, g0:g0 + gs], k == 0, k == 7,
                           (wk[1], XNb[k][g]), (BK[bk],))
                    if g == CTXG:
                        COPY(kt[:, g0:g0 + gs], banks[bk][:, 0:gs], (BK[bk],), (ktb,), eng="act")
                    else:
                        ra, rab, rb, rbb = rope_parts(bk, g0, gs)
                        TT(kt[:, g0:g0 + gs], ra[:, 0:gs], rb[:, 0:gs], ALU.add, (rab, rbb), (ktb,))
                for g in kvgroups:
                    g0, gs = GROUPS[g]
                    bk = mbank()
                    nch = gs // 128
                    for cc in range(nch):
                        t0 = g0 + cc * 128
                        for k in range(8):
                            MM(banks[bk][:, cc * 128:(cc + 1) * 128], XN[:, k, t0:t0 + 128], wv[0][:, k, :], k == 0,
                               k == 7, (wv[1], XNb[k][g]), (BK[bk],))
                    COPY(vv[:, g0 // 128:g0 // 128 + nch, :], banks[bk][:, 0:gs].rearrange("p (a b) -> p a b", a=nch),
                         (BK[bk],), (vvb,), eng="act")
                proj[h] = (kt, ktb, vv, vvb)

            def key_loop(h, g):
                kt, ktb, vv, vvb = proj[h]
                g0, gs = GROUPS[g]
                chunks = list(range(18)) if g != CTXG else [16, 17]
                pend = None
                nk = len(chunks)

                def pv(ci, c, p1, p2):
                    for r, (pt, ptb) in enumerate((p1, p2)):
                        MM(banks[2 + r][:, 0:gs], vv[:, c, :], pt[:, 0:gs], ci == 0, ci == nk - 1, (vvb, ptb),
                           (BK[2 + r],))
                        MM(banks[4 + r][:, 0:gs], ONES[:], pt[:, 0:gs], ci == 0, ci == nk - 1, (ONESb, ptb),
                           (BK[4 + r],))

                for ci, c in enumerate(chunks):
                    ps = []
                    for r in range(2):
                        MM(banks[r][:, 0:gs], kt[:, c * 128:(c + 1) * 128], (QTA, QTB)[r][:, g0:g0 + gs], True, True,
                           (ktb, QTb[g]), (BK[r],))
                        pt, ptb = PP[r].next()
                        ACT(pt[:, 0:gs], banks[r][:, 0:gs], AF.Exp, (BK[r],), (ptb,), scale=0.125)
                        ps.append((pt, ptb))
                    if pend is not None:
                        pv(*pend)
                    pend = (ci, c, ps[0], ps[1])
                pv(*pend)

            def norm_chain(h, g):
                g0, gs = GROUPS[g]
                r1, r1b = R1.next()
                r2, r2b = R2.next()
                t1, t1b = T1.next()
                t2, t2b = T2.next()
                oo, oob = OO.next()
                COPY(r1[:, 0:gs], banks[4][:, 0:gs], (BK[4],), (r1b,), eng="act")
                COPY(r2[:, 0:gs], banks[5][:, 0:gs], (BK[5],), (r2b,), eng="act")
                RECIP(r1[:, 0:gs], r1[:, 0:gs], (r1b,), (r1b,))
                RECIP(r2[:, 0:gs], r2[:, 0:gs], (r2b,), (r2b,))
                TT(t1[:, 0:gs], banks[2][:, 0:gs], r1[:, 0:gs], ALU.mult, (BK[2], r1b), (t1b,))
                STT(t2[:, 0:gs], banks[3][:, 0:gs], NLAM, r2[:, 0:gs], ALU.mult, ALU.mult, (BK[3], CONSTb, r2b),
                    (t2b,))
                TT(oo[:, 0:gs], t1[:, 0:gs], t2[:, 0:gs], ALU.add, (t1b, t2b), (oob,))
                osq, osqb = OSQ.next()
                ACT(osq[:, 0:gs], oo[:, 0:gs], AF.Square, (oob,), (osqb,))
                bk = mbank()
                MM(banks[bk][:, 0:gs], ONES[:], osq[:, 0:gs], True, True, (ONESb, osqb), (BK[bk],))
                zi, zib = ZI.next()
                ACT(zi[:, 0:gs], banks[bk][:, 0:gs], AF.Sqrt, (BK[bk], CONSTb), (zib,), bias=EPS_RMS, scale=1.0 / 128)
                RECIP(zi[:, 0:gs], zi[:, 0:gs], (zib,), (zib,))
                on, onb = ON.next()
                STT(on[:, 0:gs], oo[:, 0:gs], SGC, zi[:, 0:gs], ALU.mult, ALU.mult, (oob, CONSTb, zib), (onb,))
                if dbg_attn and h == 0:
                    for dd, (src, srcb) in enumerate(((r1, r1b), (t1, t1b), (t2, t2b), (oo, oob), (zi, zib), (on, onb))):
                        COPY(XT[:, dd, g0:g0 + gs], src[:, 0:gs], (srcb,), (XTb[dd][g],))
                    COPY(XT[0:64, 6, g0:g0 + gs], QTA[0:64, g0:g0 + gs], (QTb[g],), (XTb[6][g],))
                    COPY(XT[64:128, 6, g0:g0 + gs], QTB[64:128, g0:g0 + gs], (QTb[g],), (XTb[6][g],))
                    COPY(XT[:, 7, g0:g0 + gs], proj[0][0][:, g0:g0 + gs], (proj[0][1],), (XTb[7][g],))
                return on, onb

            def out_proj(h, g, on, onb):
                if dbg_attn:
                    return
                g0, gs = GROUPS[g]
                wo = hw[h][3]
                for d in range(8):
                    bk = mbank()
                    MM(banks[bk][:, 0:gs], wo[0][:, d * 128:(d + 1) * 128], on[:, 0:gs], True, True, (wo[1], onb),
                       (BK[bk],))
                    STT(XT[:, d, g0:g0 + gs], banks[bk][:, 0:gs], prm(slot, 2, g, d), XT[:, d, g0:g0 + gs], ALU.mult,
                        ALU.add, (BK[bk], PRMb[slot], XTb[d][g]), (XTb[d][g],))

            load_head(0)
            load_head(1)
            project_kv(0)
            prev = None
            for h in range(8):
                project_q(h)
                if h + 1 < 8:
                    project_kv(h + 1)
                for g in qgroups:
                    key_loop(h, g)
                    on, onb = norm_chain(h, g)
                    if prev is not None:
                        out_proj(*prev)
                    prev = (h, g, on, onb)
                if h + 2 < 8:
                    load_head(h + 2)
            out_proj(*prev)

        def sgmlp_phase(i, j, groups):
            SC.barrier()
            AR.reset()
            slot = prep_params(i, 1, 1.0)
            XN = AR.alloc((8, T), BF16)
            XNb = [[Buf("xn%d_%d" % (d, g)) for g in range(5)] for d in range(8)]
            WIN = AR.alloc((8, 2 * D), BF16)
            WOUT = AR.alloc((8, D), BF16)
            WST = AR.alloc((8, 128), BF16)
            BIAS = AR.alloc((8, 128), F32)
            WINb, WOUTb, WSTb, BIASb = Buf("win"), Buf("wout"), Buf("wst"), Buf("bias")
            ds1, ds2, ds3, ds4 = (SC.new_dsem("sg%d_%d" % (i, n)) for n in range(4))
            winv = sgwin_d[j].rearrange("(k p) f -> p k f", p=128)
            DMA("pool", WIN[:, :, 0:D], winv[:, :, 0:D], (), (WINb,), ds1)
            DMA("pool", WIN[:, :, D:2 * D], winv[:, :, D:2 * D], (), (WINb,), ds1)
            DMA("pool", WOUT, sgwout_d[j].rearrange("(k p) f -> p k f", p=128), (), (WOUTb,), ds2)
            DMA("pool", WST, sgws_d[j].rearrange("p (g t) -> p g t", g=8), (), (WSTb,), ds3)
            DMA("sp", BIAS, sgbs_d[:, j * 1024:(j + 1) * 1024].rearrange("p (g t) -> p g t", g=8), (), (BIASb,), ds4)
            FR = Rot("fr", 4, (512,), F32)
            prenorm(groups, slot, XN, XNb, [7], FR)
            for half in range(2):
                MM(banks[half][:, :], ONES[:], WST[:, half * 4:half * 4 + 4, :].rearrange("p a b -> p (a b)"), True,
                   True, (ONESb, WSTb), (BK[half],))
            for gq in range(8):
                STT(BIAS[:, gq, :], banks[gq // 4][:, (gq % 4) * 128:(gq % 4 + 1) * 128],
                    sglnb[:, j * 8 + gq:j * 8 + gq + 1], BIAS[:, gq, :], ALU.mult, ALU.add,
                    (BK[gq // 4], SMALLb, BIASb), (BIASb,))
            SUB = 256
            UT = Rot("ut", 1, (8, SUB), BF16)
            GT = Rot("gt", 1, (8, SUB), BF16)
            VG = Rot("vg", 1, (D,), F32)
            VSQ = Rot("vsq", 1, (D,), F32)
            VH = Rot("vh", 1, (D,), BF16)
            ST = Rot("st", 4, (8,), F32)
            MT = Rot("mt", 3, (128,), F32)
            nsub = 0
            for g in groups:
                gg0, ggs = GROUPS[g]
                for sub in range(ggs // SUB):
                    g0 = gg0 + sub * SUB
                    gs = SUB
                    ut, utb = UT.next()
                    gt, gtb = GT.next()
                    for cc in range(8):
                        bk = cc % 2
                        for k in range(8):
                            MM(banks[bk][:, 0:gs], WIN[:, k, cc * 128:(cc + 1) * 128], XN[:, k, g0:g0 + gs], k == 0,
                               k == 7, (WINb, XNb[k][g]), (BK[bk],))
                        ACT(ut[:, cc, 0:gs], banks[bk][:, 0:gs], AF.Gelu_apprx_tanh, (BK[bk],), (utb,))
                    for cpos in range(gs // 128):
                        t0 = g0 + cpos * 128
                        for half in range(2):
                            for k in range(8):
                                MM(banks[2 + half][:, :], XN[:, k, t0:t0 + 128],
                                   WIN[:, k, D + half * 512:D + (half + 1) * 512], k == 0, k == 7,
                                   (WINb, XNb[k][g]), (BK[2 + half],))
                        vg, vgb = VG.next()
                        vsq, vsqb = VSQ.next()
                        vh, vhb = VH.next()
                        st, stb = ST.next()
                        for half in range(2):
                            ACT(vg[:, half * 512:(half + 1) * 512], banks[2 + half][:, :], AF.Gelu_apprx_tanh,
                                (BK[2 + half],), (vgb,))
                        ACT(vsq[:, :], vg[:, :], AF.Square, (vgb,), (vsqb,))
                        RSUM(st[:, 0:1], vg[:, :], (vgb,), (stb,))
                        RSUM(st[:, 1:2], vsq[:, :], (vsqb,), (stb,))
                        TS(st[:, 2:3], st[:, 0:1], 1.0 / D, ALU.mult, (stb,), (stb,))
                        TT(st[:, 3:4], st[:, 2:3], st[:, 2:3], ALU.mult, (stb,), (stb,))
                        STT(st[:, 4:5], st[:, 1:2], 1.0 / D, st[:, 3:4], ALU.mult, ALU.subtract, (stb,), (stb,))
                        ACT(st[:, 5:6], st[:, 4:5], AF.Sqrt, (stb, CONSTb), (stb,), bias=EPS_LN)
                        RECIP(st[:, 6:7], st[:, 5:6], (stb,), (stb,))
                        TS(vh[:, :], vg[:, :], st[:, 2:3], ALU.subtract, (vgb, stb), (vhb,), s2=st[:, 6:7],
                           op1=ALU.mult)
                        mb = 4 + 2 * (nsub % 2)
                        nsub += 1
                        for gq in range(8):
                            bk = mb + gq // 4
                            MM(banks[bk][:, (gq % 4) * 128:(gq % 4 + 1) * 128], vh[:, gq * 128:(gq + 1) * 128],
                               WST[:, gq, :], True, True, (vhb, WSTb), (BK[bk],))
                        for gq in range(8):
                            bk = mb + gq // 4
                            mt, mtb = MT.next()
                            STT(mt[:, :], banks[bk][:, (gq % 4) * 128:(gq % 4 + 1) * 128],
                                sglng[:, j * 8 + gq:j * 8 + gq + 1], BIAS[:, gq, :], ALU.mult, ALU.add,
                                (BK[bk], SMALLb, BIASb), (mtb,))
                            TT(gt[:, gq, cpos * 128:(cpos + 1) * 128], mt[:, :],
                               ut[:, gq, cpos * 128:(cpos + 1) * 128], ALU.mult, (mtb, utb), (gtb,))
                    for d in range(8):
                        bk = d % 2
                        for cc in range(8):
                            MM(banks[bk][:, 0:gs], WOUT[:, cc, d * 128:(d + 1) * 128], gt[:, cc, 0:gs], cc == 0,
                               cc == 7, (WOUTb, gtb), (BK[bk],))
                        STT(XT[:, d, g0:g0 + gs], banks[bk][:, 0:gs], prm(slot, 2, g, d), XT[:, d, g0:g0 + gs],
                            ALU.mult, ALU.add, (BK[bk], PRMb[slot], XTb[d][g]), (XTb[d][g],))

        last_ctx_layer = 2
        done = False
        for i in range(nlayers):
            mode = "full" if i < last_ctx_layer else ("kv" if i == last_ctx_layer else "none")
            j = i // 2
            ffn_phase(i, 0, LAT + ([CTXG] if mode != "none" else []))
            if stop_stage == (i, 0):
                break
            if i % 2 == 0:
                attn_phase(i, j, LAT + ([CTXG] if mode == "full" else []), LAT + [CTXG])
            else:
                sgmlp_phase(i, j, LAT + ([CTXG] if mode == "full" else []))
            if stop_stage == (i, 1):
                break
            ffn_phase(i, 1, LAT + ([CTXG] if mode == "full" else []))

        SC.barrier()
        AR.reset()
        FG = AR.alloc((D,), F32)
        FGb = Buf("fg")
        fds = SC.new_dsem("fg")
        DMA("sp", FG, fg_d[:, :], (), (FGb,), fds)
        OST = Rot("ost", 2, (D,), F32, dma=True)
        FSQ = Rot("fsq", 2, (D,), F32)
        FS = Rot("fs", 4, (4,), F32)
        OUTb = Buf("outdram")
        out_ops = []
        for c in range(nrows_out // 128):
            g = min(c // 4, 4)
            for half in range(2):
                bk = (c % 2) * 2 + half
                for dd in range(4):
                    d = half * 4 + dd
                    TR(banks[bk][:, dd * 128:(dd + 1) * 128], XT[:, d, c * 128:(c + 1) * 128], (XTb[d][g], IDENTb),
                       (BK[bk],))
            ost, ostb, ods = OST.next()
            b0 = (c % 2) * 2
            if debug_dump:
                for half in range(2):
                    COPY(ost[:, half * 512:(half + 1) * 512], banks[b0 + half][:, :], (BK[b0 + half],), (ostb,),
                         eng=("act" if half else "dve"))
            else:
                fsq, fsqb = FSQ.next()
                fs, fsb = FS.next()
                for half in range(2):
                    ACT(fsq[:, half * 512:(half + 1) * 512], banks[b0 + half][:, :], AF.Square, (BK[b0 + half],),
                        (fsqb,))
                RSUM(fs[:, 0:1], fsq[:, :], (fsqb,), (fsb,))
                ACT(fs[:, 1:2], fs[:, 0:1], AF.Sqrt, (fsb, CONSTb), (fsb,), bias=EPS_RMS, scale=1.0 / D)
                RECIP(fs[:, 2:3], fs[:, 1:2], (fsb,), (fsb,))
                for half in range(2):
                    STT(ost[:, half * 512:(half + 1) * 512], banks[b0 + half][:, :], fs[:, 2:3],
                        FG[:, half * 512:(half + 1) * 512], ALU.mult, ALU.mult, (BK[b0 + half], fsb, FGb), (ostb,))
            o = DMA("sp", out_d[c * 128:(c + 1) * 128, :], ost, (ostb,), (), ods)
            out_ops.append(o)

        nwait = SC.emit()
        SC.final_wait("sp", out_ops)
        print("ops", len(SC.ops), "waits", nwait)
    return nc


def _rope_tables():
    n_freq = 16
    inv = (10000.0 ** (-np.arange(n_freq, dtype=np.float32) / n_freq)).astype(np.float32)
    t = np.arange(S)
    pos = np.stack([t // 64, t % 64], axis=-1).astype(np.float32)
    ang = pos[:, :, None] * inv
    cos = np.cos(ang).astype(np.float32).reshape(S, 32).T
    sin = np.sin(ang).astype(np.float32).reshape(S, 32).T
    ropeC = np.tile(cos, (4, 1))
    ropeT = np.concatenate([sin, -sin, sin, -sin], axis=0)
    return np.ascontiguousarray(ropeC), np.ascontiguousarray(ropeT)


def _head_perm():
    perm = np.zeros(128, dtype=np.int64)
    for r in range(2):
        for half in range(2):
            for axis in range(2):
                for f in range(16):
                    perm[r * 64 + half * 32 + axis * 16 + f] = r * 64 + axis * 32 + half * 16 + f
    return perm


def _fm(v):
    return np.ascontiguousarray(v.reshape(-1, 8, 128).transpose(2, 0, 1).reshape(128, -1))


_PROG_CACHE = {}


def prepare_inputs(inputs):
    f = lambda a: np.ascontiguousarray(np.asarray(a, dtype=np.float32))
    x = f(inputs["x"])
    c = f(inputs["c"])
    ctx = f(inputs["ctx"])
    c_ctx = f(inputs["c_ctx"])
    perm = _head_perm()
    da_w_in = f(inputs["da_w_in"]).copy()
    cols = np.arange(3 * D)
    for blk in range(2):
        for h in range(8):
            base = blk * D + h * 128
            cols[base:base + 128] = base + perm
    da_w_in = np.ascontiguousarray(da_w_in[:, :, cols])
    ropeC, ropeT = _rope_tables()
    b_mod = f(inputs["b_mod"])
    b_modT = np.ascontiguousarray(b_mod.reshape(DEPTH, 72, 128).transpose(2, 0, 1).reshape(128, DEPTH * 72))
    norm_g = f(inputs["norm_g"])
    norm_gT = np.ascontiguousarray(norm_g.reshape(DEPTH, 3, 8, 128).transpose(3, 0, 1, 2).reshape(128, DEPTH * 24))
    da_lambda_b = np.ascontiguousarray(np.tile(f(inputs["da_lambda"]).reshape(1, 512), (128, 1)))
    da_subln_gT = np.ascontiguousarray(f(inputs["da_subln_g"]).T)
    sg_ln_gT = np.ascontiguousarray(f(inputs["sg_ln_g"]).reshape(2, 8, 128).transpose(2, 0, 1).reshape(128, 16))
    sg_ln_bT = np.ascontiguousarray(f(inputs["sg_ln_b"]).reshape(2, 8, 128).transpose(2, 0, 1).reshape(128, 16))
    sg_w_sT = np.ascontiguousarray(f(inputs["sg_w_s"]).transpose(0, 3, 1, 2).reshape(2, 128, 8 * 128))
    sg_b_s_b = np.ascontiguousarray(np.tile(f(inputs["sg_b_s"]).reshape(1, 2 * 1024), (128, 1)))
    final_g_b = np.ascontiguousarray(np.tile(f(inputs["final_g"]).reshape(1, D), (128, 1)))
    shared = {
        "w_mod": f(inputs["w_mod"]), "b_modT": b_modT, "norm_gT": norm_gT,
        "w_ffn_gu": f(inputs["w_ffn_gu"]), "w_ffn_down": f(inputs["w_ffn_down"]),
        "da_w_in": da_w_in, "da_w_out": f(inputs["da_w_out"]), "da_lambda_b": da_lambda_b,
        "da_subln_gT": da_subln_gT, "sg_w_in": f(inputs["sg_w_in"]), "sg_ln_gT": sg_ln_gT, "sg_ln_bT": sg_ln_bT,
        "sg_w_sT": sg_w_sT, "sg_b_s_b": sg_b_s_b, "sg_w_out": f(inputs["sg_w_out"]), "final_g_b": final_g_b,
        "ropeC": ropeC, "ropeT": ropeT,
    }
    in_maps = []
    for b in range(NCORES):
        cT = np.concatenate([c[b].reshape(8, 128).T, c_ctx.reshape(8, 128).T], axis=1)
        m = dict(shared)
        m["x"] = x[b]
        m["ctx"] = ctx[b]
        m["cT"] = np.ascontiguousarray(cT)
        in_maps.append(m)
    return in_maps


def kernel(**inputs):
    in_maps = prepare_inputs(inputs)
    if "full" not in _PROG_CACHE:
        _PROG_CACHE["full"] = build_program()
    nc = _PROG_CACHE["full"]
    res = run_bass_kernel_spmd(nc, in_maps, core_ids=list(range(NCORES)))
    return np.stack([np.asarray(r["out"]) for r in res.results], axis=0).astype(np.float32)
```

```python
import math
from contextlib import ExitStack
import numpy as np
import concourse.bass as bass
import concourse.mybir as mybir
from concourse.bass_utils import run_bass_kernel_spmd

F32 = mybir.dt.float32
BF16 = mybir.dt.bfloat16
AF = mybir.ActivationFunctionType
ALU = mybir.AluOpType
AX = mybir.AxisListType

D = 1024
S = 2048
C = 256
T = S + C
DFF = 2816
NFC = DFF // 128
DEPTH = 4
NMOD = 9
NCORES = 8
GROUPS = [(0, 512), (512, 512), (1024, 512), (1536, 512), (2048, 256)]
LAT = [0, 1, 2, 3]
CTXG = 4
RMS_EPS = 1e-6
LN_EPS = 1e-5
NF = 2
NPIECE = NFC // NF
ARENA_WORDS = 32900


class Buf:
    __slots__ = ("name", "lw", "rd")

    def __init__(self, name):
        self.name = name
        self.lw = None
        self.rd = {}


class Op:
    __slots__ = ("eng", "fn", "deps", "needs_inc", "sig", "dsem", "idx")


class Sched:
    def __init__(self, nc, stack):
        self.nc = nc
        self.stack = stack
        self.ops = []
        self.E = {"pe": nc.tensor, "act": nc.scalar, "dve": nc.vector, "pool": nc.gpsimd, "sp": nc.sync}
        self.esem = {e: stack.enter_context(nc.semaphore("es_" + e)) for e in ("pe", "act", "dve", "pool")}
        self.last = {}
        self.pending_bar = {}
        self.dsems = []
        self.free_ds = []
        self.phase_ds = []

    def new_dsem(self, name, persistent=False):
        if self.free_ds:
            s = self.free_ds.pop()
        else:
            s = [self.stack.enter_context(self.nc.semaphore("ds%d" % len(self.dsems))), 0, None]
            self.dsems.append(s)
        if not persistent:
            self.phase_ds.append(s)
        return s

    def op(self, eng, fn, reads=(), writes=(), dsem=None):
        o = Op()
        o.eng = eng
        o.fn = fn
        o.needs_inc = False
        o.dsem = dsem
        o.sig = None
        o.idx = len(self.ops)
        deps = {}
        for b in reads:
            if b.lw is not None:
                deps[b.lw.idx] = b.lw
        for b in writes:
            if b.lw is not None:
                deps[b.lw.idx] = b.lw
            for r in b.rd.values():
                deps[r.idx] = r
        if eng in self.pending_bar:
            for d in self.pending_bar.pop(eng):
                deps[d.idx] = d
        dl = []
        for d in deps.values():
            if d.eng == "pe" and eng == "pe" and d.dsem is None:
                continue
            d.needs_inc = True
            dl.append(d)
        o.deps = dl
        key = eng if dsem is None else ("d", id(dsem))
        for b in reads:
            b.rd[key] = o
        for b in writes:
            b.lw = o
            b.rd = {}
        if dsem is not None:
            dsem[1] += 16
            o.sig = (dsem[0], dsem[1])
            dsem[2] = o
        else:
            self.last[eng] = o
        self.ops.append(o)
        return o

    def barrier(self):
        deps = list(self.last.values()) + [d[2] for d in self.dsems if d[2] is not None]
        for d in deps:
            d.needs_inc = True
        for e in self.E:
            self.pending_bar[e] = list(deps)
        self.free_ds.extend(self.phase_ds)
        self.phase_ds = []

    def emit(self):
        cnt = {e: 0 for e in self.esem}
        for o in self.ops:
            if o.dsem is None and o.needs_inc:
                cnt[o.eng] += 1
                o.sig = (self.esem[o.eng], cnt[o.eng])
        seen = {e: {} for e in self.E}
        nwait = 0
        for o in self.ops:
            need = {}
            for d in o.deps:
                sem, val = d.sig
                k = id(sem)
                if k not in need or need[k][1] < val:
                    need[k] = (sem, val)
            eng = self.E[o.eng]
            sn = seen[o.eng]
            for k, (sem, val) in need.items():
                if sn.get(k, 0) < val:
                    eng.wait_ge(sem, val)
                    sn[k] = val
                    nwait += 1
            ins = o.fn()
            if o.dsem is not None:
                ins.then_inc(o.sig[0], 16)
            elif o.needs_inc:
                ins.then_inc(o.sig[0], 1)
        return nwait

    def final_wait(self, eng, ops):
        e = self.E[eng]
        need = {}
        for d in ops:
            sem, val = d.sig
            if id(sem) not in need or need[id(sem)][1] < val:
                need[id(sem)] = (sem, val)
        for sem, val in need.values():
            e.wait_ge(sem, val)


class Arena:
    def __init__(self, ap, words):
        self.ap = ap
        self.words = words
        self.off = 0

    def reset(self):
        self.off = 0

    def alloc(self, shape, dtype):
        n = 1
        for s in shape:
            n *= s
        w = n if dtype == F32 else (n + 1) // 2
        assert self.off + w <= self.words, ("arena overflow", self.off, w, self.words)
        a = self.ap[:, self.off:self.off + w]
        self.off += w
        if dtype != F32:
            a = a.bitcast(dtype)
        if len(shape) == 2:
            a = a.rearrange("p (a b) -> p a b", a=shape[0])
        elif len(shape) == 3:
            a = a.rearrange("p (a b c) -> p a b c", a=shape[0], b=shape[1])
        return a


def lam_init_of(i):
    return 0.8 - 0.6 * math.exp(-0.3 * i)


def build_program(nlayers=DEPTH, debug_dump=False, stop_stage=None, dbg_attn=False):
    nc = bass.Bass("TRN2", target_bir_lowering=False)
    dt_in = lambda name, shape: nc.dram_tensor(name, list(shape), F32, kind="ExternalInput").ap()
    x_d = dt_in("x", (S, D))
    ctx_d = dt_in("ctx", (C, D))
    cT_d = dt_in("cT", (128, 16))
    wmod_d = dt_in("w_mod", (DEPTH, D, NMOD * D))
    bmodT_d = dt_in("b_modT", (128, DEPTH * 72))
    ngT_d = dt_in("norm_gT", (128, DEPTH * 24))
    wgu_d = dt_in("w_ffn_gu", (DEPTH, 2, D, 2 * DFF))
    wdn_d = dt_in("w_ffn_down", (DEPTH, 2, DFF, D))
    dawin_d = dt_in("da_w_in", (2, D, 3 * D))
    dawout_d = dt_in("da_w_out", (2, D, D))
    dalam_d = dt_in("da_lambda_b", (128, 512))
    dasg_d = dt_in("da_subln_gT", (128, 2))
    sgwin_d = dt_in("sg_w_in", (2, D, 2 * D))
    sglng_d = dt_in("sg_ln_gT", (128, 16))
    sglnb_d = dt_in("sg_ln_bT", (128, 16))
    sgws_d = dt_in("sg_w_sT", (2, 128, 8 * 128))
    sgbs_d = dt_in("sg_b_s_b", (128, 2 * 1024))
    sgwout_d = dt_in("sg_w_out", (2, D, D))
    fg_d = dt_in("final_g_b", (128, D))
    ropeC_d = dt_in("ropeC", (128, S))
    ropeT_d = dt_in("ropeT", (128, S))
    nrows_out = T if debug_dump else S
    out_d = nc.dram_tensor("out", [nrows_out, D], F32, kind="ExternalOutput").ap()

    stack = ExitStack()
    with stack:
        sb = lambda name, shape, dt: stack.enter_context(nc.sbuf_tensor(name, list(shape), dt))
        XT = sb("XT", (128, 8, T), F32)
        MODT = sb("MODT", (128, DEPTH * 72 * 2), F32)
        SMALL = sb("SMALL", (128, 16 + 288 + 96 + 2 + 16 + 16), F32)
        PRM = sb("PRM", (128, 12 * 48), F32)
        CONST = sb("CONST", (128, 8), F32)
        ONES = sb("ONES", (128, 128), BF16)
        IDENT = sb("IDENT", (128, 128), F32)
        SCT = sb("SCT", (128, 16), BF16)
        ARENA_T = sb("ARENA", (128, ARENA_WORDS), F32)
        banks = [stack.enter_context(nc.psum_tensor("bk%d" % i, [128, 512], F32)) for i in range(8)]
        BK = [Buf("bk%d" % i) for i in range(8)]
        SC = Sched(nc, stack)
        AR = Arena(ARENA_T, ARENA_WORDS)

        def MM(out, lhsT, rhs, start, stop, reads, writes):
            return SC.op("pe", lambda: nc.tensor.matmul(out, lhsT, rhs, start=start, stop=stop), reads, writes)

        def TR(out, in_, reads, writes):
            return SC.op("pe", lambda: nc.tensor.transpose(out, in_, IDENT[:]), reads, writes)

        def ACT(out, in_, func, reads, writes, bias=None, scale=None):
            kw = {}
            if bias is not None:
                kw["bias"] = bias
            if scale is not None:
                kw["scale"] = scale
            return SC.op("act", lambda: nc.scalar.activation(out, in_, func, **kw), reads, writes)

        def TT(out, in0, in1, op, reads, writes, eng="dve"):
            e = SC.E[eng]
            return SC.op(eng, lambda: e.tensor_tensor(out, in0, in1, op), reads, writes)

        def TS(out, in0, s1, op0, reads, writes, s2=None, op1=None, eng="dve"):
            e = SC.E[eng]
            if op1 is None:
                return SC.op(eng, lambda: e.tensor_scalar(out, in0, s1, None, op0), reads, writes)
            return SC.op(eng, lambda: e.tensor_scalar(out, in0, s1, s2, op0, op1), reads, writes)

        def STT(out, in0, scalar, in1, op0, op1, reads, writes, eng="dve"):
            e = SC.E[eng]
            return SC.op(eng, lambda: e.scalar_tensor_tensor(out, in0, scalar, in1, op0, op1), reads, writes)

        def RECIP(out, in_, reads, writes):
            return SC.op("dve", lambda: nc.vector.reciprocal(out, in_), reads, writes)

        def RSUM(out, in_, reads, writes):
            return SC.op("dve", lambda: nc.vector.reduce_sum(out, in_, AX.X), reads, writes)

        def COPY(out, in_, reads, writes, eng="dve"):
            if eng == "act":
                return SC.op("act", lambda: nc.scalar.copy(out, in_), reads, writes)
            e = SC.E[eng]
            return SC.op(eng, lambda: e.tensor_copy(out, in_), reads, writes)

        def DMA(queue, out, in_, reads, writes, dsem):
            e = SC.E[queue]
            return SC.op(queue, lambda: e.dma_start(out=out, in_=in_), reads, writes, dsem=dsem)

        class Rot:
            def __init__(self, name, n, shape, dtype, dma=False):
                self.t = [AR.alloc(shape, dtype) for _ in range(n)]
                self.b = [Buf("%s%d" % (name, i)) for i in range(n)]
                self.ds = [SC.new_dsem("%s%d" % (name, i)) for i in range(n)] if dma else None
                self.i = -1
                self.n = n

            def next(self):
                self.i = (self.i + 1) % self.n
                if self.ds:
                    return self.t[self.i], self.b[self.i], self.ds[self.i]
                return self.t[self.i], self.b[self.i]

        XTb = [[Buf("xt%d_%d" % (d, g)) for g in range(5)] for d in range(8)]
        MODb = [Buf("mod%d" % i) for i in range(DEPTH)]
        SMALLb = Buf("small")
        CONSTb = Buf("const")
        LAMb = Buf("lam")
        ONESb = Buf("ones")
        IDENTb = Buf("ident")
        SCTb = Buf("sct")
        PRMb = [Buf("prm%d" % i) for i in range(12)]
        small_ds = SC.new_dsem("small")

        cT = SMALL[:, 0:16]
        bmodT = SMALL[:, 16:304]
        ngT = SMALL[:, 304:400]
        dasg = SMALL[:, 400:402]
        sglng = SMALL[:, 402:418]
        sglnb = SMALL[:, 418:434]
        for dst, src in ((cT, cT_d), (bmodT, bmodT_d), (ngT, ngT_d), (dasg, dasg_d), (sglng, sglng_d), (sglnb, sglnb_d)):
            DMA("sp", dst, src[:, :], (), (SMALLb,), small_ds)
        LAMB = AR.alloc((512,), F32)
        LTMP = AR.alloc((160,), F32)
        DMA("sp", LAMB, dalam_d[:, :], (), (LAMb,), small_ds)

        SC.op("dve", lambda: nc.vector.memset(CONST[:, 0:1], RMS_EPS), (), (CONSTb,))
        SC.op("dve", lambda: nc.vector.memset(CONST[:, 1:2], LN_EPS), (), (CONSTb,))
        SC.op("dve", lambda: nc.vector.memset(ONES[:], 1.0), (), (ONESb,))
        SC.op("pool", lambda: nc.gpsimd.memset(IDENT[:], 0.0), (), (IDENTb,))
        SC.op("pool", lambda: nc.gpsimd.affine_select(out=IDENT[:], in_=IDENT[:], pattern=[[-1, 128]],
                                                      compare_op=ALU.not_equal, fill=1.0, base=0,
                                                      channel_multiplier=1), (), (IDENTb,))
        EPS_RMS = CONST[:, 0:1]
        EPS_LN = CONST[:, 1:2]

        for j in range(2):
            li = lam_init_of(2 * j)
            lb = LAMB[:, j * 256:(j + 1) * 256]
            TT(LTMP[:, 0:64], lb[:, 0:64], lb[:, 64:128], ALU.mult, (LAMb,), (LAMb,))
            TT(LTMP[:, 64:128], lb[:, 128:192], lb[:, 192:256], ALU.mult, (LAMb,), (LAMb,))
            RSUM(LTMP[:, 128:129], LTMP[:, 0:64], (LAMb,), (LAMb,))
            RSUM(LTMP[:, 129:130], LTMP[:, 64:128], (LAMb,), (LAMb,))
            ACT(LTMP[:, 130:132], LTMP[:, 128:130], AF.Exp, (LAMb,), (LAMb,))
            TT(LTMP[:, 132:133], LTMP[:, 131:132], LTMP[:, 130:131], ALU.subtract, (LAMb,), (LAMb,))
            TS(CONST[:, 4 + j:5 + j], LTMP[:, 132:133], -li, ALU.add, (LAMb,), (CONSTb,))
            TS(CONST[:, 6 + j:7 + j], dasg[:, j:j + 1], 1.0 - li, ALU.mult, (SMALLb,), (CONSTb,))

        stage = Rot("stage", 2, (D,), F32, dma=True)
        for c in range(T // 128):
            st, stb, sds = stage.next()
            src = x_d[c * 128:(c + 1) * 128, :] if c < 16 else ctx_d[(c - 16) * 128:(c - 15) * 128, :]
            DMA("sp", st, src, (), (stb,), sds)
            g = min(c // 4, 4)
            for half in range(2):
                bk = (c % 2) * 2 + half
                for dd in range(4):
                    d = half * 4 + dd
                    TR(banks[bk][:, dd * 128:(dd + 1) * 128], st[:, d * 128:(d + 1) * 128], (stb, IDENTb), (BK[bk],))
                COPY(XT[:, half * 4:half * 4 + 4, c * 128:(c + 1) * 128],
                     banks[bk][:, :].rearrange("p (a b) -> p a b", a=4),
                     (BK[bk],), [XTb[half * 4 + dd][g] for dd in range(4)], eng=("act" if half else "dve"))

        ACT(SCT[:, :].rearrange("p (k c) -> p k c", c=2), cT.rearrange("p (c k) -> p k c", c=2), AF.Silu,
            (SMALLb,), (SCTb,))
        wm = Rot("wm", 2, (8, 1024), BF16, dma=True)
        for i in range(nlayers):
            mbk = 4 + (i % 2)
            for q in range(9):
                wt, wb, wds = wm.next()
                DMA("pool", wt, wmod_d[i].rearrange("(k p) f -> p k f", p=128)[:, :, q * 1024:(q + 1) * 1024],
                    (), (wb,), wds)
                for m in range(8):
                    mm_ = q * 8 + m
                    for k in range(8):
                        MM(banks[mbk][:, mm_ * 2:mm_ * 2 + 2], wt[:, k, m * 128:(m + 1) * 128],
                           SCT[:, k * 2:k * 2 + 2], k == 0, k == 7, (wb, SCTb), (BK[mbk],))
            for col in range(2):
                TT(MODT[:, i * 144:(i + 1) * 144].rearrange("p (m c) -> p m c", c=2)[:, :, col],
                   banks[mbk][:, 0:144].rearrange("p (m c) -> p m c", c=2)[:, :, col],
                   bmodT[:, i * 72:(i + 1) * 72], ALU.add, (BK[mbk], SMALLb), (MODb[i],))

        def mod_ap(i, kmod, col):
            return MODT[:, i * 144 + kmod * 16:i * 144 + kmod * 16 + 16].rearrange("p (d c) -> p d c", c=2)[:, :, col]

        def prep_params(i, s, gate_mul):
            slot = i * 3 + s
            base = slot * 48
            for col in range(2):
                A = PRM[:, base + col * 8:base + col * 8 + 8]
                SH = PRM[:, base + 16 + col * 8:base + 16 + col * 8 + 8]
                G = PRM[:, base + 32 + col * 8:base + 32 + col * 8 + 8]
                STT(A, mod_ap(i, 3 * s + 1, col), 1.0, ngT[:, i * 24 + s * 8:i * 24 + s * 8 + 8], ALU.add, ALU.mult,
                    (MODb[i], SMALLb), (PRMb[slot],))
                COPY(SH, mod_ap(i, 3 * s, col), (MODb[i],), (PRMb[slot],))
                TS(G, mod_ap(i, 3 * s + 2, col), gate_mul, ALU.mult, (MODb[i],), (PRMb[slot],))
            return slot

        def prm(slot, which, g, d):
            col = 1 if g == CTXG else 0
            o = slot * 48 + which * 16 + col * 8 + d
            return PRM[:, o:o + 1]

        def prenorm(groups, slot, XN, XNb, ssbanks, FR):
            SQ = Rot("sq", 1, (8, 512), BF16)
            RS = Rot("rs", 1, (512,), F32)
            TMP = FR
            for n, g in enumerate(groups):
                g0, gs = GROUPS[g]
                sq, sqb = SQ.next()
                bk = ssbanks[n % len(ssbanks)]
                for d in range(8):
                    ACT(sq[:, d, 0:gs], XT[:, d, g0:g0 + gs], AF.Square, (XTb[d][g],), (sqb,))
                for d in range(8):
                    MM(banks[bk][:, 0:gs], ONES[:], sq[:, d, 0:gs], d == 0, d == 7, (ONESb, sqb), (BK[bk],))
                rs, rsb = RS.next()
                ACT(rs[:, 0:gs], banks[bk][:, 0:gs], AF.Ln, (BK[bk], CONSTb), (rsb,), bias=EPS_RMS, scale=1.0 / D)
                ACT(rs[:, 0:gs], rs[:, 0:gs], AF.Exp, (rsb,), (rsb,), scale=-0.5)
                for d in range(8):
                    tp, tpb = TMP.next()
                    STT(tp[:, 0:gs], XT[:, d, g0:g0 + gs], prm(slot, 0, g, d), rs[:, 0:gs], ALU.mult, ALU.mult,
                        (XTb[d][g], PRMb[slot], rsb), (tpb,))
                    ACT(XN[:, d, g0:g0 + gs], tp[:, 0:gs], AF.Identity, (tpb, PRMb[slot]), (XNb[d][g],),
                        bias=prm(slot, 1, g, d))

        def ffn_phase(i, which, groups):
            SC.barrier()
            AR.reset()
            slot = prep_params(i, 0 if which == 0 else 2, 0.5)
            XN = AR.alloc((8, T), BF16)
            XNb = [[Buf("xn%d_%d" % (d, g)) for g in range(5)] for d in range(8)]
            WG = Rot("wg", 3, (8, NF * 128), BF16, dma=True)
            WU = Rot("wu", 3, (8, NF * 128), BF16, dma=True)
            WD = Rot("wd", 3, (NF, D), BF16, dma=True)
            FR = Rot("fr", 8, (512,), F32)
            SG = FR
            HT = Rot("ht", 4, (512,), BF16)
            wguv = wgu_d[i, which].rearrange("(k p) f -> p k f", p=128)
            wdnv = wdn_d[i, which].rearrange("(f p) d -> p f d", p=128)
            loaded = {}

            def load_piece(p):
                f0 = p * NF * 128
                wg = WG.next()
                wu = WU.next()
                wd = WD.next()
                DMA("pool", wg[0], wguv[:, :, f0:f0 + NF * 128], (), (wg[1],), wg[2])
                DMA("pool", wu[0], wguv[:, :, DFF + f0:DFF + f0 + NF * 128], (), (wu[1],), wu[2])
                DMA("pool", wd[0], wdnv[:, p * NF:(p + 1) * NF, :], (), (wd[1],), wd[2])
                loaded[p] = (wg, wu, wd)

            load_piece(0)
            load_piece(1)
            prenorm(groups, slot, XN, XNb, [7], FR)
            items = [(p, g) for p in range(NPIECE) for g in groups]
            ybanks = [4, 5, 6]
            state = {"y": 0}

            def emit_gu(p, g):
                g0, gs = GROUPS[g]
                wg, wu, wd = loaded[p]
                hts = []
                for fi in range(NF):
                    bg, bu = (0, 1) if fi % 2 == 0 else (2, 3)
                    for k in range(8):
                        MM(banks[bg][:, 0:gs], wg[0][:, k, fi * 128:(fi + 1) * 128], XN[:, k, g0:g0 + gs], k == 0,
                           k == 7, (wg[1], XNb[k][g]), (BK[bg],))
                    for k in range(8):
                        MM(banks[bu][:, 0:gs], wu[0][:, k, fi * 128:(fi + 1) * 128], XN[:, k, g0:g0 + gs], k == 0,
                           k == 7, (wu[1], XNb[k][g]), (BK[bu],))
                    sg, sgb = SG.next()
                    ht, htb = HT.next()
                    ACT(sg[:, 0:gs], banks[bg][:, 0:gs], AF.Silu, (BK[bg],), (sgb,))
                    TT(ht[:, 0:gs], sg[:, 0:gs], banks[bu][:, 0:gs], ALU.mult, (sgb, BK[bu]), (htb,))
                    hts.append((ht, htb))
                return hts

            def emit_down(p, g, hts):
                g0, gs = GROUPS[g]
                wg, wu, wd = loaded[p]
                for d in range(8):
                    yb = ybanks[state["y"] % 3]
                    state["y"] += 1
                    for fi in range(NF):
                        MM(banks[yb][:, 0:gs], wd[0][:, fi, d * 128:(d + 1) * 128], hts[fi][0][:, 0:gs], fi == 0,
                           fi == NF - 1, (wd[1], hts[fi][1]), (BK[yb],))
                    STT(XT[:, d, g0:g0 + gs], banks[yb][:, 0:gs], prm(slot, 2, g, d), XT[:, d, g0:g0 + gs], ALU.mult,
                        ALU.add, (BK[yb], PRMb[slot], XTb[d][g]), (XTb[d][g],))

            prev = None
            for n, (p, g) in enumerate(items):
                hts = emit_gu(p, g)
                if prev is not None:
                    emit_down(*prev)
                if g == groups[0] and p + 2 < NPIECE:
                    load_piece(p + 2)
                prev = (p, g, hts)
            emit_down(*prev)

        def attn_phase(i, j, qgroups, kvgroups):
            SC.barrier()
            AR.reset()
            slot = prep_params(i, 1, 1.0)
            XN = AR.alloc((8, T), BF16)
            XNb = [[Buf("xn%d_%d" % (d, g)) for g in range(5)] for d in range(8)]
            ROPC = AR.alloc((S,), F32)
            ROPT = AR.alloc((S,), F32)
            ROPb = Buf("rope")
            rds = SC.new_dsem("rope%d" % i)
            DMA("sp", ROPC, ropeC_d[:, :], (), (ROPb,), rds)
            DMA("sp", ROPT, ropeT_d[:, :], (), (ROPb,), rds)
            WQ = Rot("wq", 2, (8, 128), BF16, dma=True)
            WK = Rot("wk", 2, (8, 128), BF16, dma=True)
            WV = Rot("wv", 2, (8, 128), BF16, dma=True)
            WO = Rot("wo", 3, (D,), BF16, dma=True)
            QTA = AR.alloc((T,), BF16)
            QTB = AR.alloc((T,), BF16)
            QTb = [Buf("qt%d" % g) for g in range(5)]
            SC.op("dve", lambda: nc.vector.memset(QTA[64:128, :], 0.0), (), QTb)
            SC.op("dve", lambda: nc.vector.memset(QTB[0:64, :], 0.0), (), QTb)
            KT = Rot("kt", 2, (T,), BF16)
            VV = Rot("vv", 2, (18, 128), BF16)
            PP = [Rot("pp%d" % r, 2, (512,), BF16) for r in range(2)]
            FR = Rot("fr", 2, (512,), F32)
            NR = Rot("nr", 4, (512,), F32)
            HR = Rot("hr", 5, (512,), BF16)
            OSQ = ON = HR
            winv = dawin_d[j].rearrange("(k p) f -> p k f", p=128)
            woutv = dawout_d[j].rearrange("(h p) d -> p h d", p=128)
            NLAM = CONST[:, 4 + j:5 + j]
            SGC = CONST[:, 6 + j:7 + j]
            misc = [6, 7]
            mstate = {"m": 0}

            def mbank():
                b = misc[mstate["m"] % 2]
                mstate["m"] += 1
                return b

            prenorm(kvgroups, slot, XN, XNb, [6, 7], FR)
            hw = {}

            def load_head(h):
                wq = WQ.next()
                wk = WK.next()
                wv = WV.next()
                wo = WO.next()
                DMA("pool", wq[0], winv[:, :, h * 128:(h + 1) * 128], (), (wq[1],), wq[2])
                DMA("pool", wk[0], winv[:, :, D + h * 128:D + (h + 1) * 128], (), (wk[1],), wk[2])
                DMA("pool", wv[0], winv[:, :, 2 * D + h * 128:2 * D + (h + 1) * 128], (), (wv[1],), wv[2])
                DMA("pool", wo[0], woutv[:, h, :], (), (wo[1],), wo[2])
                hw[h] = (wq, wk, wv, wo)

            def rope_parts(bk, g0, gs):
                qs, qsb = FR.next()
                COPY(qs[:, 0:gs], banks[bk][:, 0:gs], (BK[bk],), (qsb,), eng="act")
                rb, rbb = FR.next()
                for blk in range(4):
                    sp_ = blk ^ 1
                    TT(rb[blk * 32:(blk + 1) * 32, 0:gs], qs[sp_ * 32:(sp_ + 1) * 32, 0:gs],
                       ROPT[sp_ * 32:(sp_ + 1) * 32, g0:g0 + gs], ALU.mult, (qsb, ROPb), (rbb,))
                TT(qs[:, 0:gs], qs[:, 0:gs], ROPC[:, g0:g0 + gs], ALU.mult, (qsb, ROPb), (qsb,))
                return qs, qsb, rb, rbb

            proj = {}

            def project_q(h):
                wq = hw[h][0]
                for g in qgroups:
                    g0, gs = GROUPS[g]
                    bk = mbank()
                    for k in range(8):
                        MM(banks[bk][:, 0:gs], wq[0][:, k, :], XN[:, k, g0:g0 + gs], k == 0, k == 7,
                           (wq[1], XNb[k][g]), (BK[bk],))
                    if g == CTXG:
                        COPY(QTA[0:64, g0:g0 + gs], banks[bk][0:64, 0:gs], (BK[bk],), (QTb[g],), eng="act")
                        COPY(QTB[64:128, g0:g0 + gs], banks[bk][64:128, 0:gs], (BK[bk],), (QTb[g],), eng="act")
                    else:
                        ra, rab, rb, rbb = rope_parts(bk, g0, gs)
                        TT(QTA[0:64, g0:g0 + gs], ra[0:64, 0:gs], rb[0:64, 0:gs], ALU.add, (rab, rbb), (QTb[g],))
                        TT(QTB[64:128, g0:g0 + gs], ra[64:128, 0:gs], rb[64:128, 0:gs], ALU.add, (rab, rbb),
                           (QTb[g],))

            def project_kv(h):
                wq, wk, wv, wo = hw[h]
                kt, ktb = KT.next()
                vv, vvb = VV.next()
                for g in kvgroups:
                    g0, gs = GROUPS[g]
                    bk = mbank()
                    for k in range(8):
                        MM(banks[bk][:, 0:gs], wk[0][:, k, :], XN[:, k, g0:g0 + gs], k == 0, k == 7,
                           (wk[1], XNb[k][g]), (BK[bk],))
                    if g == CTXG:
                        COPY(kt[:, g0:g0 + gs], banks[bk][:, 0:gs], (BK[bk],), (ktb,), eng="act")
                    else:
                        ra, rab, rb, rbb = rope_parts(bk, g0, gs)
                        TT(kt[:, g0:g0 + gs], ra[:, 0:gs], rb[:, 0:gs], ALU.add, (rab, rbb), (ktb,))
                for g in kvgroups:
                    g0, gs = GROUPS[g]
                    bk = mbank()
                    nch = gs // 128
                    for cc in range(nch):
                        t0 = g0 + cc * 128
                        for k in range(8):
                            MM(banks[bk][:, cc * 128:(cc + 1) * 128], XN[:, k, t0:t0 + 128], wv[0][:, k, :], k == 0,
                               k == 7, (wv[1], XNb[k][g]), (BK[bk],))
                    COPY(vv[:, g0 // 128:g0 // 128 + nch, :], banks[bk][:, 0:gs].rearrange("p (a b) -> p a b", a=nch),
                         (BK[bk],), (vvb,), eng="act")
                proj[h] = (kt, ktb, vv, vvb)

            def key_loop(h, g, inject=()):
                inject = list(inject)
                kt, ktb, vv, vvb = proj[h]
                g0, gs = GROUPS[g]
                chunks = list(range(18)) if g != CTXG else [16, 17]
                pend = None
                nk = len(chunks)

                def pv(ci, c, p1, p2):
                    for r, (pt, ptb) in enumerate((p1, p2)):
                        MM(banks[2 + r][:, 0:gs], vv[:, c, :], pt[:, 0:gs], ci == 0, ci == nk - 1, (vvb, ptb),
                           (BK[2 + r],))
                        MM(banks[4 + r][:, 0:gs], ONES[:], pt[:, 0:gs], ci == 0, ci == nk - 1, (ONESb, ptb),
                           (BK[4 + r],))

                for ci, c in enumerate(chunks):
                    ps = []
                    for r in range(2):
                        MM(banks[r][:, 0:gs], kt[:, c * 128:(c + 1) * 128], (QTA, QTB)[r][:, g0:g0 + gs], True, True,
                           (ktb, QTb[g]), (BK[r],))
                        pt, ptb = PP[r].next()
                        ACT(pt[:, 0:gs], banks[r][:, 0:gs], AF.Exp, (BK[r],), (ptb,), scale=0.125)
                        ps.append((pt, ptb))
                    if pend is not None:
                        pv(*pend)
                    pend = (ci, c, ps[0], ps[1])
                    while inject and inject[0][0] <= ci:
                        inject.pop(0)[1]()
                pv(*pend)
                while inject:
                    inject.pop(0)[1]()

            def norm_chain(h, g):
                g0, gs = GROUPS[g]
                r1, r1b = NR.next()
                r2, r2b = NR.next()
                t1, t1b = NR.next()
                t2, t2b = NR.next()
                ACT(r1[:, 0:gs], banks[4][:, 0:gs], AF.Ln, (BK[4],), (r1b,))
                COPY(t1[:, 0:gs], banks[2][:, 0:gs], (BK[2],), (t1b,))
                ACT(r2[:, 0:gs], banks[5][:, 0:gs], AF.Ln, (BK[5],), (r2b,))
                COPY(t2[:, 0:gs], banks[3][:, 0:gs], (BK[3],), (t2b,))
                ACT(r1[:, 0:gs], r1[:, 0:gs], AF.Exp, (r1b,), (r1b,), scale=-1.0)
                ACT(r2[:, 0:gs], r2[:, 0:gs], AF.Exp, (r2b,), (r2b,), scale=-1.0)
                TT(t1[:, 0:gs], t1[:, 0:gs], r1[:, 0:gs], ALU.mult, (t1b, r1b), (t1b,))
                STT(t2[:, 0:gs], t2[:, 0:gs], NLAM, r2[:, 0:gs], ALU.mult, ALU.mult, (t2b, CONSTb, r2b), (t2b,))
                oo, oob = r1, r1b
                TT(oo[:, 0:gs], t1[:, 0:gs], t2[:, 0:gs], ALU.add, (t1b, t2b), (oob,))
                osq, osqb = OSQ.next()
                TT(osq[:, 0:gs], oo[:, 0:gs], oo[:, 0:gs], ALU.mult, (oob,), (osqb,))
                return (g, gs, oo, oob, osq, osqb, r2, r2b)

            def norm_chain_b(st):
                g, gs, oo, oob, osq, osqb, r2, r2b = st
                bk = mbank()
                MM(banks[bk][:, 0:gs], ONES[:], osq[:, 0:gs], True, True, (ONESb, osqb), (BK[bk],))
                zi, zib = r2, r2b
                ACT(zi[:, 0:gs], banks[bk][:, 0:gs], AF.Ln, (BK[bk], CONSTb), (zib,), bias=EPS_RMS, scale=1.0 / 128)
                ACT(zi[:, 0:gs], zi[:, 0:gs], AF.Exp, (zib,), (zib,), scale=-0.5)
                on, onb = ON.next()
                STT(on[:, 0:gs], oo[:, 0:gs], SGC, zi[:, 0:gs], ALU.mult, ALU.mult, (oob, CONSTb, zib), (onb,))
                return on, onb

            def out_proj(h, g, on, onb):
                if dbg_attn:
                    return
                g0, gs = GROUPS[g]
                wo = hw[h][3]
                for d in range(8):
                    bk = mbank()
                    MM(banks[bk][:, 0:gs], wo[0][:, d * 128:(d + 1) * 128], on[:, 0:gs], True, True, (wo[1], onb),
                       (BK[bk],))
                    STT(XT[:, d, g0:g0 + gs], banks[bk][:, 0:gs], prm(slot, 2, g, d), XT[:, d, g0:g0 + gs], ALU.mult,
                        ALU.add, (BK[bk], PRMb[slot], XTb[d][g]), (XTb[d][g],))

            load_head(0)
            load_head(1)
            project_kv(0)
            prev = None
            box = {}

            def part_b(p):
                box["on"] = norm_chain_b(p[2])

            def part_c(p):
                out_proj(p[0], p[1], *box["on"])

            for h in range(8):
                project_q(h)
                if h + 1 < 8:
                    project_kv(h + 1)
                for g in qgroups:
                    inj = []
                    if prev is not None:
                        inj = [(3, (lambda p=prev: part_b(p))), (9, (lambda p=prev: part_c(p)))]
                    key_loop(h, g, inj)
                    st = norm_chain(h, g)
                    prev = (h, g, st)
                if h + 2 < 8:
                    load_head(h + 2)
            part_b(prev)
            part_c(prev)

        def sgmlp_phase(i, j, groups):
            SC.barrier()
            AR.reset()
            slot = prep_params(i, 1, 1.0)
            XN = AR.alloc((8, T), BF16)
            XNb = [[Buf("xn%d_%d" % (d, g)) for g in range(5)] for d in range(8)]
            WIN = AR.alloc((8, 2 * D), BF16)
            WOUT = AR.alloc((8, D), BF16)
            WST = AR.alloc((8, 128), BF16)
            BIAS = AR.alloc((8, 128), F32)
            WINb, WOUTb, WSTb, BIASb = Buf("win"), Buf("wout"), Buf("wst"), Buf("bias")
            ds1, ds2, ds3, ds4 = (SC.new_dsem("sg%d_%d" % (i, n)) for n in range(4))
            winv = sgwin_d[j].rearrange("(k p) f -> p k f", p=128)
            DMA("pool", WIN[:, :, 0:D], winv[:, :, 0:D], (), (WINb,), ds1)
            DMA("pool", WIN[:, :, D:2 * D], winv[:, :, D:2 * D], (), (WINb,), ds1)
            DMA("pool", WOUT, sgwout_d[j].rearrange("(k p) f -> p k f", p=128), (), (WOUTb,), ds2)
            DMA("pool", WST, sgws_d[j].rearrange("p (g t) -> p g t", g=8), (), (WSTb,), ds3)
            DMA("sp", BIAS, sgbs_d[:, j * 1024:(j + 1) * 1024].rearrange("p (g t) -> p g t", g=8), (), (BIASb,), ds4)
            FR = Rot("fr", 4, (512,), F32)
            prenorm(groups, slot, XN, XNb, [7], FR)
            for half in range(2):
                MM(banks[half][:, :], ONES[:], WST[:, half * 4:half * 4 + 4, :].rearrange("p a b -> p (a b)"), True,
                   True, (ONESb, WSTb), (BK[half],))
            for gq in range(8):
                STT(BIAS[:, gq, :], banks[gq // 4][:, (gq % 4) * 128:(gq % 4 + 1) * 128],
                    sglnb[:, j * 8 + gq:j * 8 + gq + 1], BIAS[:, gq, :], ALU.mult, ALU.add,
                    (BK[gq // 4], SMALLb, BIASb), (BIASb,))
            SUB = 256
            UT = Rot("ut", 1, (8, SUB), BF16)
            GT = Rot("gt", 1, (8, SUB), BF16)
            VG = Rot("vg", 1, (D,), F32)
            VSQ = Rot("vsq", 1, (D,), F32)
            VH = Rot("vh", 1, (D,), BF16)
            ST = Rot("st", 4, (8,), F32)
            MT = Rot("mt", 3, (128,), F32)
            nsub = 0
            for g in groups:
                gg0, ggs = GROUPS[g]
                for sub in range(ggs // SUB):
                    g0 = gg0 + sub * SUB
                    gs = SUB
                    ut, utb = UT.next()
                    gt, gtb = GT.next()
                    for cc in range(8):
                        bk = cc % 2
                        for k in range(8):
                            MM(banks[bk][:, 0:gs], WIN[:, k, cc * 128:(cc + 1) * 128], XN[:, k, g0:g0 + gs], k == 0,
                               k == 7, (WINb, XNb[k][g]), (BK[bk],))
                        ACT(ut[:, cc, 0:gs], banks[bk][:, 0:gs], AF.Gelu_apprx_tanh, (BK[bk],), (utb,))
                    for cpos in range(gs // 128):
                        t0 = g0 + cpos * 128
                        for half in range(2):
                            for k in range(8):
                                MM(banks[2 + half][:, :], XN[:, k, t0:t0 + 128],
                                   WIN[:, k, D + half * 512:D + (half + 1) * 512], k == 0, k == 7,
                                   (WINb, XNb[k][g]), (BK[2 + half],))
                        vg, vgb = VG.next()
                        vsq, vsqb = VSQ.next()
                        vh, vhb = VH.next()
                        st, stb = ST.next()
                        for half in range(2):
                            ACT(vg[:, half * 512:(half + 1) * 512], banks[2 + half][:, :], AF.Gelu_apprx_tanh,
                                (BK[2 + half],), (vgb,))
                        ACT(vsq[:, :], vg[:, :], AF.Square, (vgb,), (vsqb,))
                        RSUM(st[:, 0:1], vg[:, :], (vgb,), (stb,))
                        RSUM(st[:, 1:2], vsq[:, :], (vsqb,), (stb,))
                        TS(st[:, 2:3], st[:, 0:1], 1.0 / D, ALU.mult, (stb,), (stb,))
                        TT(st[:, 3:4], st[:, 2:3], st[:, 2:3], ALU.mult, (stb,), (stb,))
                        STT(st[:, 4:5], st[:, 1:2], 1.0 / D, st[:, 3:4], ALU.mult, ALU.subtract, (stb,), (stb,))
                        ACT(st[:, 5:6], st[:, 4:5], AF.Sqrt, (stb, CONSTb), (stb,), bias=EPS_LN)
                        RECIP(st[:, 6:7], st[:, 5:6], (stb,), (stb,))
                        TS(vh[:, :], vg[:, :], st[:, 2:3], ALU.subtract, (vgb, stb), (vhb,), s2=st[:, 6:7],
                           op1=ALU.mult)
                        mb = 4 + 2 * (nsub % 2)
                        nsub += 1
                        for gq in range(8):
                            bk = mb + gq // 4
                            MM(banks[bk][:, (gq % 4) * 128:(gq % 4 + 1) * 128], vh[:, gq * 128:(gq + 1) * 128],
                               WST[:, gq, :], True, True, (vhb, WSTb), (BK[bk],))
                        for gq in range(8):
                            bk = mb + gq // 4
                            mt, mtb = MT.next()
                            STT(mt[:, :], banks[bk][:, (gq % 4) * 128:(gq % 4 + 1) * 128],
                                sglng[:, j * 8 + gq:j * 8 + gq + 1], BIAS[:, gq, :], ALU.mult, ALU.add,
                                (BK[bk], SMALLb, BIASb), (mtb,))
                            TT(gt[:, gq, cpos * 128:(cpos + 1) * 128], mt[:, :],
                               ut[:, gq, cpos * 128:(cpos + 1) * 128], ALU.mult, (mtb, utb), (gtb,))
                    for d in range(8):
                        bk = d % 2
                        for cc in range(8):
                            MM(banks[bk][:, 0:gs], WOUT[:, cc, d * 128:(d + 1) * 128], gt[:, cc, 0:gs], cc == 0,
                               cc == 7, (WOUTb, gtb), (BK[bk],))
                        STT(XT[:, d, g0:g0 + gs], banks[bk][:, 0:gs], prm(slot, 2, g, d), XT[:, d, g0:g0 + gs],
                            ALU.mult, ALU.add, (BK[bk], PRMb[slot], XTb[d][g]), (XTb[d][g],))

        last_ctx_layer = 2
        done = False
        for i in range(nlayers):
            mode = "full" if i < last_ctx_layer else ("kv" if i == last_ctx_layer else "none")
            j = i // 2
            ffn_phase(i, 0, LAT + ([CTXG] if mode != "none" else []))
            if stop_stage == (i, 0):
                break
            if i % 2 == 0:
                attn_phase(i, j, LAT + ([CTXG] if mode == "full" else []), LAT + [CTXG])
            else:
                sgmlp_phase(i, j, LAT + ([CTXG] if mode == "full" else []))
            if stop_stage == (i, 1):
                break
            ffn_phase(i, 1, LAT + ([CTXG] if mode == "full" else []))

        SC.barrier()
        AR.reset()
        FG = AR.alloc((D,), F32)
        FGb = Buf("fg")
        fds = SC.new_dsem("fg")
        DMA("sp", FG, fg_d[:, :], (), (FGb,), fds)
        OST = Rot("ost", 2, (D,), F32, dma=True)
        FSQ = Rot("fsq", 2, (D,), F32)
        FS = Rot("fs", 4, (4,), F32)
        OUTb = Buf("outdram")
        out_ops = []
        for c in range(nrows_out // 128):
            g = min(c // 4, 4)
            for half in range(2):
                bk = (c % 2) * 2 + half
                for dd in range(4):
                    d = half * 4 + dd
                    TR(banks[bk][:, dd * 128:(dd + 1) * 128], XT[:, d, c * 128:(c + 1) * 128], (XTb[d][g], IDENTb),
                       (BK[bk],))
            ost, ostb, ods = OST.next()
            b0 = (c % 2) * 2
            if debug_dump:
                for half in range(2):
                    COPY(ost[:, half * 512:(half + 1) * 512], banks[b0 + half][:, :], (BK[b0 + half],), (ostb,),
                         eng=("act" if half else "dve"))
            else:
                fsq, fsqb = FSQ.next()
                fs, fsb = FS.next()
                for half in range(2):
                    ACT(fsq[:, half * 512:(half + 1) * 512], banks[b0 + half][:, :], AF.Square, (BK[b0 + half],),
                        (fsqb,))
                RSUM(fs[:, 0:1], fsq[:, :], (fsqb,), (fsb,))
                ACT(fs[:, 1:2], fs[:, 0:1], AF.Sqrt, (fsb, CONSTb), (fsb,), bias=EPS_RMS, scale=1.0 / D)
                RECIP(fs[:, 2:3], fs[:, 1:2], (fsb,), (fsb,))
                for half in range(2):
                    STT(ost[:, half * 512:(half + 1) * 512], banks[b0 + half][:, :], fs[:, 2:3],
                        FG[:, half * 512:(half + 1) * 512], ALU.mult, ALU.mult, (BK[b0 + half], fsb, FGb), (ostb,))
            o = DMA("sp", out_d[c * 128:(c + 1) * 128, :], ost, (ostb,), (), ods)
            out_ops.append(o)

        nwait = SC.emit()
        SC.final_wait("sp", out_ops)
        print("ops", len(SC.ops), "waits", nwait)
    return nc


def _rope_tables():
    n_freq = 16
    inv = (10000.0 ** (-np.arange(n_freq, dtype=np.float32) / n_freq)).astype(np.float32)
    t = np.arange(S)
    pos = np.stack([t // 64, t % 64], axis=-1).astype(np.float32)
    ang = pos[:, :, None] * inv
    cos = np.cos(ang).astype(np.float32).reshape(S, 32).T
    sin = np.sin(ang).astype(np.float32).reshape(S, 32).T
    ropeC = np.tile(cos, (4, 1))
    ropeT = np.concatenate([sin, -sin, sin, -sin], axis=0)
    return np.ascontiguousarray(ropeC), np.ascontiguousarray(ropeT)


def _head_perm():
    perm = np.zeros(128, dtype=np.int64)
    for r in range(2):
        for half in range(2):
            for axis in range(2):
                for f in range(16):
                    perm[r * 64 + half * 32 + axis * 16 + f] = r * 64 + axis * 32 + half * 16 + f
    return perm


def _fm(v):
    return np.ascontiguousarray(v.reshape(-1, 8, 128).transpose(2, 0, 1).reshape(128, -1))


_PROG_CACHE = {}


def prepare_inputs(inputs):
    f = lambda a: np.ascontiguousarray(np.asarray(a, dtype=np.float32))
    x = f(inputs["x"])
    c = f(inputs["c"])
    ctx = f(inputs["ctx"])
    c_ctx = f(inputs["c_ctx"])
    perm = _head_perm()
    da_w_in = f(inputs["da_w_in"]).copy()
    cols = np.arange(3 * D)
    for blk in range(2):
        for h in range(8):
            base = blk * D + h * 128
            cols[base:base + 128] = base + perm
    da_w_in = np.ascontiguousarray(da_w_in[:, :, cols])
    ropeC, ropeT = _rope_tables()
    b_mod = f(inputs["b_mod"])
    b_modT = np.ascontiguousarray(b_mod.reshape(DEPTH, 72, 128).transpose(2, 0, 1).reshape(128, DEPTH * 72))
    norm_g = f(inputs["norm_g"])
    norm_gT = np.ascontiguousarray(norm_g.reshape(DEPTH, 3, 8, 128).transpose(3, 0, 1, 2).reshape(128, DEPTH * 24))
    da_lambda_b = np.ascontiguousarray(np.tile(f(inputs["da_lambda"]).reshape(1, 512), (128, 1)))
    da_subln_gT = np.ascontiguousarray(f(inputs["da_subln_g"]).T)
    sg_ln_gT = np.ascontiguousarray(f(inputs["sg_ln_g"]).reshape(2, 8, 128).transpose(2, 0, 1).reshape(128, 16))
    sg_ln_bT = np.ascontiguousarray(f(inputs["sg_ln_b"]).reshape(2, 8, 128).transpose(2, 0, 1).reshape(128, 16))
    sg_w_sT = np.ascontiguousarray(f(inputs["sg_w_s"]).transpose(0, 3, 1, 2).reshape(2, 128, 8 * 128))
    sg_b_s_b = np.ascontiguousarray(np.tile(f(inputs["sg_b_s"]).reshape(1, 2 * 1024), (128, 1)))
    final_g_b = np.ascontiguousarray(np.tile(f(inputs["final_g"]).reshape(1, D), (128, 1)))
    shared = {
        "w_mod": f(inputs["w_mod"]), "b_modT": b_modT, "norm_gT": norm_gT,
        "w_ffn_gu": f(inputs["w_ffn_gu"]), "w_ffn_down": f(inputs["w_ffn_down"]),
        "da_w_in": da_w_in, "da_w_out": f(inputs["da_w_out"]), "da_lambda_b": da_lambda_b,
        "da_subln_gT": da_subln_gT, "sg_w_in": f(inputs["sg_w_in"]), "sg_ln_gT": sg_ln_gT, "sg_ln_bT": sg_ln_bT,
        "sg_w_sT": sg_w_sT, "sg_b_s_b": sg_b_s_b, "sg_w_out": f(inputs["sg_w_out"]), "final_g_b": final_g_b,
        "ropeC": ropeC, "ropeT": ropeT,
    }
    in_maps = []
    for b in range(NCORES):
        cT = np.concatenate([c[b].reshape(8, 128).T, c_ctx.reshape(8, 128).T], axis=1)
        m = dict(shared)
        m["x"] = x[b]
        m["ctx"] = ctx[b]
        m["cT"] = np.ascontiguousarray(cT)
        in_maps.append(m)
    return in_maps


def kernel(**inputs):
    in_maps = prepare_inputs(inputs)
    if "full" not in _PROG_CACHE:
        _PROG_CACHE["full"] = build_program()
    nc = _PROG_CACHE["full"]
    res = run_bass_kernel_spmd(nc, in_maps, core_ids=list(range(NCORES)))
    return np.stack([np.asarray(r["out"]) for r in res.results], axis=0).astype(np.float32)
```

```python
import math
from contextlib import ExitStack
import numpy as np
import concourse.bass as bass
import concourse.mybir as mybir
from concourse.bass_utils import run_bass_kernel_spmd

F32 = mybir.dt.float32
BF16 = mybir.dt.bfloat16
AF = mybir.ActivationFunctionType
ALU = mybir.AluOpType
AX = mybir.AxisListType

D = 1024
S = 2048
C = 256
T = S + C
DFF = 2816
NFC = DFF // 128
DEPTH = 4
NMOD = 9
NCORES = 8
GROUPS = [(0, 512), (512, 512), (1024, 512), (1536, 512), (2048, 256)]
LAT = [0, 1, 2, 3]
CTXG = 4
RMS_EPS = 1e-6
LN_EPS = 1e-5
NF = 2
NPIECE = NFC // NF
ARENA_WORDS = 32900


class Buf:
    __slots__ = ("name", "lw", "rd")

    def __init__(self, name):
        self.name = name
        self.lw = None
        self.rd = {}


class Op:
    __slots__ = ("eng", "fn", "deps", "needs_inc", "sig", "dsem", "idx")


class Sched:
    def __init__(self, nc, stack):
        self.nc = nc
        self.stack = stack
        self.ops = []
        self.E = {"pe": nc.tensor, "act": nc.scalar, "dve": nc.vector, "pool": nc.gpsimd, "sp": nc.sync}
        self.esem = {e: stack.enter_context(nc.semaphore("es_" + e)) for e in ("pe", "act", "dve", "pool")}
        self.last = {}
        self.pending_bar = {}
        self.dsems = []
        self.free_ds = []
        self.phase_ds = []

    def new_dsem(self, name, persistent=False):
        if self.free_ds:
            s = self.free_ds.pop()
        else:
            s = [self.stack.enter_context(self.nc.semaphore("ds%d" % len(self.dsems))), 0, None]
            self.dsems.append(s)
        if not persistent:
            self.phase_ds.append(s)
        return s

    def op(self, eng, fn, reads=(), writes=(), dsem=None):
        o = Op()
        o.eng = eng
        o.fn = fn
        o.needs_inc = False
        o.dsem = dsem
        o.sig = None
        o.idx = len(self.ops)
        deps = {}
        for b in reads:
            if b.lw is not None:
                deps[b.lw.idx] = b.lw
        for b in writes:
            if b.lw is not None:
                deps[b.lw.idx] = b.lw
            for r in b.rd.values():
                deps[r.idx] = r
        if eng in self.pending_bar:
            for d in self.pending_bar.pop(eng):
                deps[d.idx] = d
        dl = []
        for d in deps.values():
            if d.eng == "pe" and eng == "pe" and d.dsem is None:
                continue
            d.needs_inc = True
            dl.append(d)
        o.deps = dl
        key = eng if dsem is None else ("d", id(dsem))
        for b in reads:
            b.rd[key] = o
        for b in writes:
            b.lw = o
            b.rd = {}
        if dsem is not None:
            dsem[1] += 16
            o.sig = (dsem[0], dsem[1])
            dsem[2] = o
        else:
            self.last[eng] = o
        self.ops.append(o)
        return o

    def barrier(self):
        deps = list(self.last.values()) + [d[2] for d in self.dsems if d[2] is not None]
        for d in deps:
            d.needs_inc = True
        for e in self.E:
            self.pending_bar[e] = list(deps)
        self.free_ds.extend(self.phase_ds)
        self.phase_ds = []

    def emit(self):
        cnt = {e: 0 for e in self.esem}
        for o in self.ops:
            if o.dsem is None and o.needs_inc:
                cnt[o.eng] += 1
                o.sig = (self.esem[o.eng], cnt[o.eng])
        seen = {e: {} for e in self.E}
        nwait = 0
        for o in self.ops:
            need = {}
            for d in o.deps:
                sem, val = d.sig
                k = id(sem)
                if k not in need or need[k][1] < val:
                    need[k] = (sem, val)
            eng = self.E[o.eng]
            sn = seen[o.eng]
            for k, (sem, val) in need.items():
                if sn.get(k, 0) < val:
                    eng.wait_ge(sem, val)
                    sn[k] = val
                    nwait += 1
            ins = o.fn()
            if o.dsem is not None:
                ins.then_inc(o.sig[0], 16)
            elif o.needs_inc:
                ins.then_inc(o.sig[0], 1)
        return nwait

    def final_wait(self, eng, ops):
        e = self.E[eng]
        need = {}
        for d in ops:
            sem, val = d.sig
            if id(sem) not in need or need[id(sem)][1] < val:
                need[id(sem)] = (sem, val)
        for sem, val in need.values():
            e.wait_ge(sem, val)


class Arena:
    def __init__(self, ap, words):
        self.ap = ap
        self.words = words
        self.off = 0

    def reset(self):
        self.off = 0

    def alloc(self, shape, dtype):
        n = 1
        for s in shape:
            n *= s
        w = n if dtype == F32 else (n + 1) // 2
        assert self.off + w <= self.words, ("arena overflow", self.off, w, self.words)
        a = self.ap[:, self.off:self.off + w]
        self.off += w
        if dtype != F32:
            a = a.bitcast(dtype)
        if len(shape) == 2:
            a = a.rearrange("p (a b) -> p a b", a=shape[0])
        elif len(shape) == 3:
            a = a.rearrange("p (a b c) -> p a b c", a=shape[0], b=shape[1])
        return a


def lam_init_of(i):
    return 0.8 - 0.6 * math.exp(-0.3 * i)


def build_program(nlayers=DEPTH, debug_dump=False, stop_stage=None, dbg_attn=False):
    nc = bass.Bass("TRN2", target_bir_lowering=False)
    dt_in = lambda name, shape: nc.dram_tensor(name, list(shape), F32, kind="ExternalInput").ap()
    x_d = dt_in("x", (S, D))
    ctx_d = dt_in("ctx", (C, D))
    cT_d = dt_in("cT", (128, 16))
    wmod_d = dt_in("w_mod", (DEPTH, D, NMOD * D))
    bmodT_d = dt_in("b_modT", (128, DEPTH * 72))
    ngT_d = dt_in("norm_gT", (128, DEPTH * 24))
    wgu_d = dt_in("w_ffn_gu", (DEPTH, 2, D, 2 * DFF))
    wdn_d = dt_in("w_ffn_down", (DEPTH, 2, DFF, D))
    dawin_d = dt_in("da_w_in", (2, D, 3 * D))
    dawout_d = dt_in("da_w_out", (2, D, D))
    dalam_d = dt_in("da_lambda_b", (128, 512))
    dasg_d = dt_in("da_subln_gT", (128, 2))
    sgwin_d = dt_in("sg_w_in", (2, D, 2 * D))
    sglng_d = dt_in("sg_ln_gT", (128, 16))
    sglnb_d = dt_in("sg_ln_bT", (128, 16))
    sgws_d = dt_in("sg_w_sT", (2, 128, 8 * 128))
    sgbs_d = dt_in("sg_b_s_b", (128, 2 * 1024))
    sgwout_d = dt_in("sg_w_out", (2, D, D))
    fg_d = dt_in("final_g_b", (128, D))
    ropeC_d = dt_in("ropeC", (128, S))
    ropeT_d = dt_in("ropeT", (128, S))
    nrows_out = T if debug_dump else S
    out_d = nc.dram_tensor("out", [nrows_out, D], F32, kind="ExternalOutput").ap()

    stack = ExitStack()
    with stack:
        sb = lambda name, shape, dt: stack.enter_context(nc.sbuf_tensor(name, list(shape), dt))
        XT = sb("XT", (128, 8, T), F32)
        MODT = sb("MODT", (128, DEPTH * 72 * 2), F32)
        SMALL = sb("SMALL", (128, 16 + 288 + 96 + 2 + 16 + 16), F32)
        PRM = sb("PRM", (128, 12 * 48), F32)
        CONST = sb("CONST", (128, 8), F32)
        ONES = sb("ONES", (128, 128), BF16)
        IDENT = sb("IDENT", (128, 128), F32)
        SCT = sb("SCT", (128, 16), BF16)
        ARENA_T = sb("ARENA", (128, ARENA_WORDS), F32)
        banks = [stack.enter_context(nc.psum_tensor("bk%d" % i, [128, 512], F32)) for i in range(8)]
        BK = [Buf("bk%d" % i) for i in range(8)]
        SC = Sched(nc, stack)
        AR = Arena(ARENA_T, ARENA_WORDS)

        def MM(out, lhsT, rhs, start, stop, reads, writes):
            return SC.op("pe", lambda: nc.tensor.matmul(out, lhsT, rhs, start=start, stop=stop), reads, writes)

        def TR(out, in_, reads, writes):
            return SC.op("pe", lambda: nc.tensor.transpose(out, in_, IDENT[:]), reads, writes)

        def ACT(out, in_, func, reads, writes, bias=None, scale=None):
            kw = {}
            if bias is not None:
                kw["bias"] = bias
            if scale is not None:
                kw["scale"] = scale
            return SC.op("act", lambda: nc.scalar.activation(out, in_, func, **kw), reads, writes)

        def TT(out, in0, in1, op, reads, writes, eng="dve"):
            e = SC.E[eng]
            return SC.op(eng, lambda: e.tensor_tensor(out, in0, in1, op), reads, writes)

        def TS(out, in0, s1, op0, reads, writes, s2=None, op1=None, eng="dve"):
            e = SC.E[eng]
            if op1 is None:
                return SC.op(eng, lambda: e.tensor_scalar(out, in0, s1, None, op0), reads, writes)
            return SC.op(eng, lambda: e.tensor_scalar(out, in0, s1, s2, op0, op1), reads, writes)

        def STT(out, in0, scalar, in1, op0, op1, reads, writes, eng="dve"):
            e = SC.E[eng]
            return SC.op(eng, lambda: e.scalar_tensor_tensor(out, in0, scalar, in1, op0, op1), reads, writes)

        def RECIP(out, in_, reads, writes):
            return SC.op("dve", lambda: nc.vector.reciprocal(out, in_), reads, writes)

        def RSUM(out, in_, reads, writes):
            return SC.op("dve", lambda: nc.vector.reduce_sum(out, in_, AX.X), reads, writes)

        def COPY(out, in_, reads, writes, eng="dve"):
            if eng == "act":
                return SC.op("act", lambda: nc.scalar.copy(out, in_), reads, writes)
            e = SC.E[eng]
            return SC.op(eng, lambda: e.tensor_copy(out, in_), reads, writes)

        def DMA(queue, out, in_, reads, writes, dsem):
            e = SC.E[queue]
            return SC.op(queue, lambda: e.dma_start(out=out, in_=in_), reads, writes, dsem=dsem)

        class Rot:
            def __init__(self, name, n, shape, dtype, dma=False):
                self.t = [AR.alloc(shape, dtype) for _ in range(n)]
                self.b = [Buf("%s%d" % (name, i)) for i in range(n)]
                self.ds = [SC.new_dsem("%s%d" % (name, i)) for i in range(n)] if dma else None
                self.i = -1
                self.n = n

            def next(self):
                self.i = (self.i + 1) % self.n
                if self.ds:
                    return self.t[self.i], self.b[self.i], self.ds[self.i]
                return self.t[self.i], self.b[self.i]

        XTb = [[Buf("xt%d_%d" % (d, g)) for g in range(5)] for d in range(8)]
        MODb = [Buf("mod%d" % i) for i in range(DEPTH)]
        SMALLb = Buf("small")
        CONSTb = Buf("const")
        LAMb = Buf("lam")
        ONESb = Buf("ones")
        IDENTb = Buf("ident")
        SCTb = Buf("sct")
        PRMb = [Buf("prm%d" % i) for i in range(12)]
        small_ds = SC.new_dsem("small")

        cT = SMALL[:, 0:16]
        bmodT = SMALL[:, 16:304]
        ngT = SMALL[:, 304:400]
        dasg = SMALL[:, 400:402]
        sglng = SMALL[:, 402:418]
        sglnb = SMALL[:, 418:434]
        for dst, src in ((cT, cT_d), (bmodT, bmodT_d), (ngT, ngT_d), (dasg, dasg_d), (sglng, sglng_d), (sglnb, sglnb_d)):
            DMA("sp", dst, src[:, :], (), (SMALLb,), small_ds)
        LAMB = AR.alloc((512,), F32)
        LTMP = AR.alloc((160,), F32)
        DMA("sp", LAMB, dalam_d[:, :], (), (LAMb,), small_ds)

        SC.op("dve", lambda: nc.vector.memset(CONST[:, 0:1], RMS_EPS), (), (CONSTb,))
        SC.op("dve", lambda: nc.vector.memset(CONST[:, 1:2], LN_EPS), (), (CONSTb,))
        SC.op("dve", lambda: nc.vector.memset(ONES[:], 1.0), (), (ONESb,))
        SC.op("pool", lambda: nc.gpsimd.memset(IDENT[:], 0.0), (), (IDENTb,))
        SC.op("pool", lambda: nc.gpsimd.affine_select(out=IDENT[:], in_=IDENT[:], pattern=[[-1, 128]],
                                                      compare_op=ALU.not_equal, fill=1.0, base=0,
                                                      channel_multiplier=1), (), (IDENTb,))
        EPS_RMS = CONST[:, 0:1]
        EPS_LN = CONST[:, 1:2]

        for j in range(2):
            li = lam_init_of(2 * j)
            lb = LAMB[:, j * 256:(j + 1) * 256]
            TT(LTMP[:, 0:64], lb[:, 0:64], lb[:, 64:128], ALU.mult, (LAMb,), (LAMb,))
            TT(LTMP[:, 64:128], lb[:, 128:192], lb[:, 192:256], ALU.mult, (LAMb,), (LAMb,))
            RSUM(LTMP[:, 128:129], LTMP[:, 0:64], (LAMb,), (LAMb,))
            RSUM(LTMP[:, 129:130], LTMP[:, 64:128], (LAMb,), (LAMb,))
            ACT(LTMP[:, 130:132], LTMP[:, 128:130], AF.Exp, (LAMb,), (LAMb,))
            TT(LTMP[:, 132:133], LTMP[:, 131:132], LTMP[:, 130:131], ALU.subtract, (LAMb,), (LAMb,))
            TS(CONST[:, 4 + j:5 + j], LTMP[:, 132:133], -li, ALU.add, (LAMb,), (CONSTb,))
            TS(CONST[:, 6 + j:7 + j], dasg[:, j:j + 1], 1.0 - li, ALU.mult, (SMALLb,), (CONSTb,))

        stage = Rot("stage", 2, (D,), F32, dma=True)
        for c in range(T // 128):
            st, stb, sds = stage.next()
            src = x_d[c * 128:(c + 1) * 128, :] if c < 16 else ctx_d[(c - 16) * 128:(c - 15) * 128, :]
            DMA("sp", st, src, (), (stb,), sds)
            g = min(c // 4, 4)
            for half in range(2):
                bk = (c % 2) * 2 + half
                for dd in range(4):
                    d = half * 4 + dd
                    TR(banks[bk][:, dd * 128:(dd + 1) * 128], st[:, d * 128:(d + 1) * 128], (stb, IDENTb), (BK[bk],))
                COPY(XT[:, half * 4:half * 4 + 4, c * 128:(c + 1) * 128],
                     banks[bk][:, :].rearrange("p (a b) -> p a b", a=4),
                     (BK[bk],), [XTb[half * 4 + dd][g] for dd in range(4)], eng=("act" if half else "dve"))

        ACT(SCT[:, :].rearrange("p (k c) -> p k c", c=2), cT.rearrange("p (c k) -> p k c", c=2), AF.Silu,
            (SMALLb,), (SCTb,))
        def mod_jobs(i, wm, mbk):
            jobs = []

            def piece(q):
                wt, wb, wds = wm.next()
                DMA("pool", wt, wmod_d[i].rearrange("(k p) f -> p k f", p=128)[:, :, q * 512:(q + 1) * 512],
                    (), (wb,), wds)
                for m in range(4):
                    mm_ = q * 4 + m
                    for k in range(8):
                        MM(banks[mbk][:, mm_ * 2:mm_ * 2 + 2], wt[:, k, m * 128:(m + 1) * 128],
                           SCT[:, k * 2:k * 2 + 2], k == 0, k == 7, (wb, SCTb), (BK[mbk],))

            def fin():
                for col in range(2):
                    TT(MODT[:, i * 144:(i + 1) * 144].rearrange("p (m c) -> p m c", c=2)[:, :, col],
                       banks[mbk][:, 0:144].rearrange("p (m c) -> p m c", c=2)[:, :, col],
                       bmodT[:, i * 72:(i + 1) * 72], ALU.add, (BK[mbk], SMALLb), (MODb[i],))

            for q in range(18):
                jobs.append(lambda q=q: piece(q))
            jobs.append(fin)
            return jobs

        wm0 = Rot("wm", 3, (8, 512), BF16, dma=True)
        for job in mod_jobs(0, wm0, 4):
            job()

        def mod_ap(i, kmod, col):
            return MODT[:, i * 144 + kmod * 16:i * 144 + kmod * 16 + 16].rearrange("p (d c) -> p d c", c=2)[:, :, col]

        def prep_params(i, s, gate_mul):
            slot = i * 3 + s
            base = slot * 48
            for col in range(2):
                A = PRM[:, base + col * 8:base + col * 8 + 8]
                SH = PRM[:, base + 16 + col * 8:base + 16 + col * 8 + 8]
                G = PRM[:, base + 32 + col * 8:base + 32 + col * 8 + 8]
                STT(A, mod_ap(i, 3 * s + 1, col), 1.0, ngT[:, i * 24 + s * 8:i * 24 + s * 8 + 8], ALU.add, ALU.mult,
                    (MODb[i], SMALLb), (PRMb[slot],))
                COPY(SH, mod_ap(i, 3 * s, col), (MODb[i],), (PRMb[slot],))
                TS(G, mod_ap(i, 3 * s + 2, col), gate_mul, ALU.mult, (MODb[i],), (PRMb[slot],))
            return slot

        def prm(slot, which, g, d):
            col = 1 if g == CTXG else 0
            o = slot * 48 + which * 16 + col * 8 + d
            return PRM[:, o:o + 1]

        def prenorm(groups, slot, XN, XNb, ssbanks, FR):
            SQ = Rot("sq", 1, (8, 512), BF16)
            RS = Rot("rs", 1, (512,), F32)
            TMP = FR
            for n, g in enumerate(groups):
                g0, gs = GROUPS[g]
                sq, sqb = SQ.next()
                bk = ssbanks[n % len(ssbanks)]
                for d in range(8):
                    ACT(sq[:, d, 0:gs], XT[:, d, g0:g0 + gs], AF.Square, (XTb[d][g],), (sqb,))
                for d in range(8):
                    MM(banks[bk][:, 0:gs], ONES[:], sq[:, d, 0:gs], d == 0, d == 7, (ONESb, sqb), (BK[bk],))
                rs, rsb = RS.next()
                ACT(rs[:, 0:gs], banks[bk][:, 0:gs], AF.Ln, (BK[bk], CONSTb), (rsb,), bias=EPS_RMS, scale=1.0 / D)
                ACT(rs[:, 0:gs], rs[:, 0:gs], AF.Exp, (rsb,), (rsb,), scale=-0.5)
                for d in range(8):
                    tp, tpb = TMP.next()
                    STT(tp[:, 0:gs], XT[:, d, g0:g0 + gs], prm(slot, 0, g, d), rs[:, 0:gs], ALU.mult, ALU.mult,
                        (XTb[d][g], PRMb[slot], rsb), (tpb,))
                    ACT(XN[:, d, g0:g0 + gs], tp[:, 0:gs], AF.Identity, (tpb, PRMb[slot]), (XNb[d][g],),
                        bias=prm(slot, 1, g, d))

        def ffn_phase(i, which, groups):
            SC.barrier()
            AR.reset()
            slot = prep_params(i, 0 if which == 0 else 2, 0.5)
            XN = AR.alloc((8, T), BF16)
            XNb = [[Buf("xn%d_%d" % (d, g)) for g in range(5)] for d in range(8)]
            WG = Rot("wg", 3, (8, NF * 128), BF16, dma=True)
            WU = Rot("wu", 3, (8, NF * 128), BF16, dma=True)
            WD = Rot("wd", 3, (NF, D), BF16, dma=True)
            FR = Rot("fr", 8, (512,), F32)
            SG = FR
            HT = Rot("ht", 4, (512,), BF16)
            wguv = wgu_d[i, which].rearrange("(k p) f -> p k f", p=128)
            wdnv = wdn_d[i, which].rearrange("(f p) d -> p f d", p=128)
            loaded = {}

            def load_piece(p):
                f0 = p * NF * 128
                wg = WG.next()
                wu = WU.next()
                wd = WD.next()
                DMA("pool", wg[0], wguv[:, :, f0:f0 + NF * 128], (), (wg[1],), wg[2])
                DMA("pool", wu[0], wguv[:, :, DFF + f0:DFF + f0 + NF * 128], (), (wu[1],), wu[2])
                DMA("pool", wd[0], wdnv[:, p * NF:(p + 1) * NF, :], (), (wd[1],), wd[2])
                loaded[p] = (wg, wu, wd)

            load_piece(0)
            load_piece(1)
            prenorm(groups, slot, XN, XNb, [7], FR)
            items = [(p, g) for p in range(NPIECE) for g in groups]
            ybanks = [4, 5, 6]
            state = {"y": 0}

            def emit_gu(p, g):
                g0, gs = GROUPS[g]
                wg, wu, wd = loaded[p]
                hts = []
                for fi in range(NF):
                    bg, bu = (0, 1) if fi % 2 == 0 else (2, 3)
                    for k in range(8):
                        MM(banks[bg][:, 0:gs], wg[0][:, k, fi * 128:(fi + 1) * 128], XN[:, k, g0:g0 + gs], k == 0,
                           k == 7, (wg[1], XNb[k][g]), (BK[bg],))
                    for k in range(8):
                        MM(banks[bu][:, 0:gs], wu[0][:, k, fi * 128:(fi + 1) * 128], XN[:, k, g0:g0 + gs], k == 0,
                           k == 7, (wu[1], XNb[k][g]), (BK[bu],))
                    sg, sgb = SG.next()
                    ht, htb = HT.next()
                    ACT(sg[:, 0:gs], banks[bg][:, 0:gs], AF.Silu, (BK[bg],), (sgb,))
                    TT(ht[:, 0:gs], sg[:, 0:gs], banks[bu][:, 0:gs], ALU.mult, (sgb, BK[bu]), (htb,))
                    hts.append((ht, htb))
                return hts

            def emit_down(p, g, hts):
                g0, gs = GROUPS[g]
                wg, wu, wd = loaded[p]
                for d in range(8):
                    yb = ybanks[state["y"] % 3]
                    state["y"] += 1
                    for fi in range(NF):
                        MM(banks[yb][:, 0:gs], wd[0][:, fi, d * 128:(d + 1) * 128], hts[fi][0][:, 0:gs], fi == 0,
                           fi == NF - 1, (wd[1], hts[fi][1]), (BK[yb],))
                    STT(XT[:, d, g0:g0 + gs], banks[yb][:, 0:gs], prm(slot, 2, g, d), XT[:, d, g0:g0 + gs], ALU.mult,
                        ALU.add, (BK[yb], PRMb[slot], XTb[d][g]), (XTb[d][g],))

            mjobs = []
            if which == 0 and i + 1 < nlayers:
                mjobs = mod_jobs(i + 1, Rot("wmx", 2, (8, 512), BF16, dma=True), 7)
            prev = None
            for n, (p, g) in enumerate(items):
                hts = emit_gu(p, g)
                if prev is not None:
                    emit_down(*prev)
                if g == groups[0] and p + 2 < NPIECE:
                    load_piece(p + 2)
                prev = (p, g, hts)
                if mjobs and n % 2 == 1:
                    mjobs.pop(0)()
            emit_down(*prev)
            while mjobs:
                mjobs.pop(0)()

        def attn_phase(i, j, qgroups, kvgroups):
            SC.barrier()
            AR.reset()
            slot = prep_params(i, 1, 1.0)
            XN = AR.alloc((8, T), BF16)
            XNb = [[Buf("xn%d_%d" % (d, g)) for g in range(5)] for d in range(8)]
            ROPC = AR.alloc((S,), F32)
            ROPT = AR.alloc((S,), F32)
            ROPb = Buf("rope")
            rds = SC.new_dsem("rope%d" % i)
            DMA("sp", ROPC, ropeC_d[:, :], (), (ROPb,), rds)
            DMA("sp", ROPT, ropeT_d[:, :], (), (ROPb,), rds)
            WQ = Rot("wq", 2, (8, 128), BF16, dma=True)
            WK = Rot("wk", 2, (8, 128), BF16, dma=True)
            WV = Rot("wv", 2, (8, 128), BF16, dma=True)
            WO = Rot("wo", 3, (D,), BF16, dma=True)
            QTA = AR.alloc((T,), BF16)
            QTB = AR.alloc((T,), BF16)
            QTb = [Buf("qt%d" % g) for g in range(5)]
            SC.op("dve", lambda: nc.vector.memset(QTA[64:128, :], 0.0), (), QTb)
            SC.op("dve", lambda: nc.vector.memset(QTB[0:64, :], 0.0), (), QTb)
            KT = Rot("kt", 2, (T,), BF16)
            VV = Rot("vv", 2, (18, 128), BF16)
            PP = [Rot("pp%d" % r, 2, (512,), BF16) for r in range(2)]
            FR = Rot("fr", 2, (512,), F32)
            NR = Rot("nr", 4, (512,), F32)
            HR = Rot("hr", 5, (512,), BF16)
            OSQ = ON = HR
            winv = dawin_d[j].rearrange("(k p) f -> p k f", p=128)
            woutv = dawout_d[j].rearrange("(h p) d -> p h d", p=128)
            NLAM = CONST[:, 4 + j:5 + j]
            SGC = CONST[:, 6 + j:7 + j]
            misc = [6, 7]
            mstate = {"m": 0}

            def mbank():
                b = misc[mstate["m"] % 2]
                mstate["m"] += 1
                return b

            prenorm(kvgroups, slot, XN, XNb, [6, 7], FR)
            hw = {}

            def load_head(h):
                wq = WQ.next()
                wk = WK.next()
                wv = WV.next()
                wo = WO.next()
                DMA("pool", wq[0], winv[:, :, h * 128:(h + 1) * 128], (), (wq[1],), wq[2])
                DMA("pool", wk[0], winv[:, :, D + h * 128:D + (h + 1) * 128], (), (wk[1],), wk[2])
                DMA("pool", wv[0], winv[:, :, 2 * D + h * 128:2 * D + (h + 1) * 128], (), (wv[1],), wv[2])
                DMA("pool", wo[0], woutv[:, h, :], (), (wo[1],), wo[2])
                hw[h] = (wq, wk, wv, wo)

            def rope_parts(bk, g0, gs):
                qs, qsb = FR.next()
                COPY(qs[:, 0:gs], banks[bk][:, 0:gs], (BK[bk],), (qsb,), eng="dve")
                rb, rbb = FR.next()
                for blk in range(4):
                    sp_ = blk ^ 1
                    TT(rb[blk * 32:(blk + 1) * 32, 0:gs], qs[sp_ * 32:(sp_ + 1) * 32, 0:gs],
                       ROPT[sp_ * 32:(sp_ + 1) * 32, g0:g0 + gs], ALU.mult, (qsb, ROPb), (rbb,))
                TT(qs[:, 0:gs], qs[:, 0:gs], ROPC[:, g0:g0 + gs], ALU.mult, (qsb, ROPb), (qsb,))
                return qs, qsb, rb, rbb

            proj = {}

            def project_q(h):
                wq = hw[h][0]
                for g in qgroups:
                    g0, gs = GROUPS[g]
                    bk = mbank()
                    for k in range(8):
                        MM(banks[bk][:, 0:gs], wq[0][:, k, :], XN[:, k, g0:g0 + gs], k == 0, k == 7,
                           (wq[1], XNb[k][g]), (BK[bk],))
                    if g == CTXG:
                        COPY(QTA[0:64, g0:g0 + gs], banks[bk][0:64, 0:gs], (BK[bk],), (QTb[g],), eng="act")
                        COPY(QTB[64:128, g0:g0 + gs], banks[bk][64:128, 0:gs], (BK[bk],), (QTb[g],), eng="act")
                    else:
                        ra, rab, rb, rbb = rope_parts(bk, g0, gs)
                        TT(QTA[0:64, g0:g0 + gs], ra[0:64, 0:gs], rb[0:64, 0:gs], ALU.add, (rab, rbb), (QTb[g],))
                        TT(QTB[64:128, g0:g0 + gs], ra[64:128, 0:gs], rb[64:128, 0:gs], ALU.add, (rab, rbb),
                           (QTb[g],))

            def project_kv(h):
                wq, wk, wv, wo = hw[h]
                kt, ktb = KT.next()
                vv, vvb = VV.next()
                for g in kvgroups:
                    g0, gs = GROUPS[g]
                    bk = mbank()
                    for k in range(8):
                        MM(banks[bk][:, 0:gs], wk[0][:, k, :], XN[:, k, g0:g0 + gs], k == 0, k == 7,
                           (wk[1], XNb[k][g]), (BK[bk],))
                    if g == CTXG:
                        COPY(kt[:, g0:g0 + gs], banks[bk][:, 0:gs], (BK[bk],), (ktb,), eng="act")
                    else:
                        ra, rab, rb, rbb = rope_parts(bk, g0, gs)
                        TT(kt[:, g0:g0 + gs], ra[:, 0:gs], rb[:, 0:gs], ALU.add, (rab, rbb), (ktb,))
                for g in kvgroups:
                    g0, gs = GROUPS[g]
                    bk = mbank()
                    nch = gs // 128
                    for cc in range(nch):
                        t0 = g0 + cc * 128
                        for k in range(8):
                            MM(banks[bk][:, cc * 128:(cc + 1) * 128], XN[:, k, t0:t0 + 128], wv[0][:, k, :], k == 0,
                               k == 7, (wv[1], XNb[k][g]), (BK[bk],))
                    COPY(vv[:, g0 // 128:g0 // 128 + nch, :], banks[bk][:, 0:gs].rearrange("p (a b) -> p a b", a=nch),
                         (BK[bk],), (vvb,), eng="act")
                proj[h] = (kt, ktb, vv, vvb)

            def key_loop(h, g, inject=()):
                inject = list(inject)
                kt, ktb, vv, vvb = proj[h]
                g0, gs = GROUPS[g]
                chunks = list(range(18)) if g != CTXG else [16, 17]
                pend = None
                nk = len(chunks)

                def pv1(r, ci, c, pt, ptb):
                    MM(banks[2 + r][:, 0:gs], vv[:, c, :], pt[:, 0:gs], ci == 0, ci == nk - 1, (vvb, ptb),
                       (BK[2 + r],))
                    MM(banks[4 + r][:, 0:gs], ONES[:], pt[:, 0:gs], ci == 0, ci == nk - 1, (ONESb, ptb),
                       (BK[4 + r],))

                for ci, c in enumerate(chunks):
                    ps = []
                    for r in range(2):
                        MM(banks[r][:, 0:gs], kt[:, c * 128:(c + 1) * 128], (QTA, QTB)[r][:, g0:g0 + gs], True, True,
                           (ktb, QTb[g]), (BK[r],))
                        pt, ptb = PP[r].next()
                        ACT(pt[:, 0:gs], banks[r][:, 0:gs], AF.Exp, (BK[r],), (ptb,), scale=0.125)
                        ps.append((pt, ptb))
                        if pend is not None:
                            pv1(r, pend[0], pend[1], *pend[2 + r])
                    pend = (ci, c, ps[0], ps[1])
                    while inject and inject[0][0] <= ci:
                        inject.pop(0)[1]()
                for r in range(2):
                    pv1(r, pend[0], pend[1], *pend[2 + r])
                while inject:
                    inject.pop(0)[1]()

            def norm_chain(h, g):
                g0, gs = GROUPS[g]
                r1, r1b = NR.next()
                r2, r2b = NR.next()
                t1, t1b = NR.next()
                t2, t2b = NR.next()
                ACT(r1[:, 0:gs], banks[4][:, 0:gs], AF.Ln, (BK[4],), (r1b,))
                COPY(t1[:, 0:gs], banks[2][:, 0:gs], (BK[2],), (t1b,))
                ACT(r2[:, 0:gs], banks[5][:, 0:gs], AF.Ln, (BK[5],), (r2b,))
                COPY(t2[:, 0:gs], banks[3][:, 0:gs], (BK[3],), (t2b,))
                ACT(r1[:, 0:gs], r1[:, 0:gs], AF.Exp, (r1b,), (r1b,), scale=-1.0)
                ACT(r2[:, 0:gs], r2[:, 0:gs], AF.Exp, (r2b,), (r2b,), scale=-1.0)
                TT(t1[:, 0:gs], t1[:, 0:gs], r1[:, 0:gs], ALU.mult, (t1b, r1b), (t1b,))
                STT(t2[:, 0:gs], t2[:, 0:gs], NLAM, r2[:, 0:gs], ALU.mult, ALU.mult, (t2b, CONSTb, r2b), (t2b,))
                oo, oob = r1, r1b
                TT(oo[:, 0:gs], t1[:, 0:gs], t2[:, 0:gs], ALU.add, (t1b, t2b), (oob,))
                osq, osqb = OSQ.next()
                TT(osq[:, 0:gs], oo[:, 0:gs], oo[:, 0:gs], ALU.mult, (oob,), (osqb,))
                return (g, gs, oo, oob, osq, osqb, r2, r2b)

            def norm_chain_b(st):
                g, gs, oo, oob, osq, osqb, r2, r2b = st
                bk = mbank()
                MM(banks[bk][:, 0:gs], ONES[:], osq[:, 0:gs], True, True, (ONESb, osqb), (BK[bk],))
                zi, zib = r2, r2b
                ACT(zi[:, 0:gs], banks[bk][:, 0:gs], AF.Ln, (BK[bk], CONSTb), (zib,), bias=EPS_RMS, scale=1.0 / 128)
                ACT(zi[:, 0:gs], zi[:, 0:gs], AF.Exp, (zib,), (zib,), scale=-0.5)
                on, onb = ON.next()
                STT(on[:, 0:gs], oo[:, 0:gs], SGC, zi[:, 0:gs], ALU.mult, ALU.mult, (oob, CONSTb, zib), (onb,))
                return on, onb

            def out_proj(h, g, on, onb, ds=range(8)):
                if dbg_attn:
                    return
                g0, gs = GROUPS[g]
                wo = hw[h][3]
                for d in ds:
                    bk = mbank()
                    MM(banks[bk][:, 0:gs], wo[0][:, d * 128:(d + 1) * 128], on[:, 0:gs], True, True, (wo[1], onb),
                       (BK[bk],))
                    STT(XT[:, d, g0:g0 + gs], banks[bk][:, 0:gs], prm(slot, 2, g, d), XT[:, d, g0:g0 + gs], ALU.mult,
                        ALU.add, (BK[bk], PRMb[slot], XTb[d][g]), (XTb[d][g],))

            load_head(0)
            load_head(1)
            project_kv(0)
            prev = None
            box = {}

            def part_b(p):
                box["on"] = norm_chain_b(p[2])

            def part_c(p, ds=range(8)):
                out_proj(p[0], p[1], *box["on"], ds=ds)

            for h in range(8):
                project_q(h)
                if h + 1 < 8:
                    project_kv(h + 1)
                for g in qgroups:
                    inj = []
                    if prev is not None:
                        inj = [(3, (lambda p=prev: part_b(p)))]
                        inj += [(8 + d, (lambda p=prev, d=d: part_c(p, (d,)))) for d in range(8)]
                    key_loop(h, g, inj)
                    st = norm_chain(h, g)
                    prev = (h, g, st)
                if h + 2 < 8:
                    load_head(h + 2)
            part_b(prev)
            part_c(prev)

        def sgmlp_phase(i, j, groups):
            SC.barrier()
            AR.reset()
            slot = prep_params(i, 1, 1.0)
            XN = AR.alloc((8, T), BF16)
            XNb = [[Buf("xn%d_%d" % (d, g)) for g in range(5)] for d in range(8)]
            WIN = AR.alloc((8, 2 * D), BF16)
            WOUT = AR.alloc((8, D), BF16)
            WST = AR.alloc((8, 128), BF16)
            BIAS = AR.alloc((8, 128), F32)
            WINb, WOUTb, WSTb, BIASb = Buf("win"), Buf("wout"), Buf("wst"), Buf("bias")
            ds1, ds2, ds3, ds4 = (SC.new_dsem("sg%d_%d" % (i, n)) for n in range(4))
            winv = sgwin_d[j].rearrange("(k p) f -> p k f", p=128)
            DMA("pool", WIN[:, :, 0:D], winv[:, :, 0:D], (), (WINb,), ds1)
            DMA("pool", WIN[:, :, D:2 * D], winv[:, :, D:2 * D], (), (WINb,), ds1)
            DMA("pool", WOUT, sgwout_d[j].rearrange("(k p) f -> p k f", p=128), (), (WOUTb,), ds2)
            DMA("pool", WST, sgws_d[j].rearrange("p (g t) -> p g t", g=8), (), (WSTb,), ds3)
            DMA("sp", BIAS, sgbs_d[:, j * 1024:(j + 1) * 1024].rearrange("p (g t) -> p g t", g=8), (), (BIASb,), ds4)
            FR = Rot("fr", 4, (512,), F32)
            prenorm(groups, slot, XN, XNb, [7], FR)
            for half in range(2):
                MM(banks[half][:, :], ONES[:], WST[:, half * 4:half * 4 + 4, :].rearrange("p a b -> p (a b)"), True,
                   True, (ONESb, WSTb), (BK[half],))
            for gq in range(8):
                STT(BIAS[:, gq, :], banks[gq // 4][:, (gq % 4) * 128:(gq % 4 + 1) * 128],
                    sglnb[:, j * 8 + gq:j * 8 + gq + 1], BIAS[:, gq, :], ALU.mult, ALU.add,
                    (BK[gq // 4], SMALLb, BIASb), (BIASb,))
            SUB = 256
            UT = Rot("ut", 1, (8, SUB), BF16)
            GT = Rot("gt", 1, (8, SUB), BF16)
            VG = Rot("vg", 1, (D,), F32)
            VSQ = Rot("vsq", 1, (D,), F32)
            VH = Rot("vh", 1, (D,), BF16)
            ST = Rot("st", 4, (8,), F32)
            MT = Rot("mt", 3, (128,), F32)
            nsub = 0
            for g in groups:
                gg0, ggs = GROUPS[g]
                for sub in range(ggs // SUB):
                    g0 = gg0 + sub * SUB
                    gs = SUB
                    ut, utb = UT.next()
                    gt, gtb = GT.next()
                    for cc in range(8):
                        bk = cc % 2
                        for k in range(8):
                            MM(banks[bk][:, 0:gs], WIN[:, k, cc * 128:(cc + 1) * 128], XN[:, k, g0:g0 + gs], k == 0,
                               k == 7, (WINb, XNb[k][g]), (BK[bk],))
                        ACT(ut[:, cc, 0:gs], banks[bk][:, 0:gs], AF.Gelu_apprx_tanh, (BK[bk],), (utb,))
                    for cpos in range(gs // 128):
                        t0 = g0 + cpos * 128
                        for half in range(2):
                            for k in range(8):
                                MM(banks[2 + half][:, :], XN[:, k, t0:t0 + 128],
                                   WIN[:, k, D + half * 512:D + (half + 1) * 512], k == 0, k == 7,
                                   (WINb, XNb[k][g]), (BK[2 + half],))
                        vg, vgb = VG.next()
                        vsq, vsqb = VSQ.next()
                        vh, vhb = VH.next()
                        st, stb = ST.next()
                        for half in range(2):
                            ACT(vg[:, half * 512:(half + 1) * 512], banks[2 + half][:, :], AF.Gelu_apprx_tanh,
                                (BK[2 + half],), (vgb,))
                        ACT(vsq[:, :], vg[:, :], AF.Square, (vgb,), (vsqb,))
                        RSUM(st[:, 0:1], vg[:, :], (vgb,), (stb,))
                        RSUM(st[:, 1:2], vsq[:, :], (vsqb,), (stb,))
                        TS(st[:, 2:3], st[:, 0:1], 1.0 / D, ALU.mult, (stb,), (stb,))
                        TT(st[:, 3:4], st[:, 2:3], st[:, 2:3], ALU.mult, (stb,), (stb,))
                        STT(st[:, 4:5], st[:, 1:2], 1.0 / D, st[:, 3:4], ALU.mult, ALU.subtract, (stb,), (stb,))
                        ACT(st[:, 5:6], st[:, 4:5], AF.Sqrt, (stb, CONSTb), (stb,), bias=EPS_LN)
                        RECIP(st[:, 6:7], st[:, 5:6], (stb,), (stb,))
                        TS(vh[:, :], vg[:, :], st[:, 2:3], ALU.subtract, (vgb, stb), (vhb,), s2=st[:, 6:7],
                           op1=ALU.mult)
                        mb = 4 + 2 * (nsub % 2)
                        nsub += 1
                        for gq in range(8):
                            bk = mb + gq // 4
                            MM(banks[bk][:, (gq % 4) * 128:(gq % 4 + 1) * 128], vh[:, gq * 128:(gq + 1) * 128],
                               WST[:, gq, :], True, True, (vhb, WSTb), (BK[bk],))
                        for gq in range(8):
                            bk = mb + gq // 4
                            mt, mtb = MT.next()
                            STT(mt[:, :], banks[bk][:, (gq % 4) * 128:(gq % 4 + 1) * 128],
                                sglng[:, j * 8 + gq:j * 8 + gq + 1], BIAS[:, gq, :], ALU.mult, ALU.add,
                                (BK[bk], SMALLb, BIASb), (mtb,))
                            TT(gt[:, gq, cpos * 128:(cpos + 1) * 128], mt[:, :],
                               ut[:, gq, cpos * 128:(cpos + 1) * 128], ALU.mult, (mtb, utb), (gtb,))
                    for d in range(8):
                        bk = d % 2
                        for cc in range(8):
                            MM(banks[bk][:, 0:gs], WOUT[:, cc, d * 128:(d + 1) * 128], gt[:, cc, 0:gs], cc == 0,
                               cc == 7, (WOUTb, gtb), (BK[bk],))
                        STT(XT[:, d, g0:g0 + gs], banks[bk][:, 0:gs], prm(slot, 2, g, d), XT[:, d, g0:g0 + gs],
                            ALU.mult, ALU.add, (BK[bk], PRMb[slot], XTb[d][g]), (XTb[d][g],))

        last_ctx_layer = 2
        done = False
        for i in range(nlayers):
            mode = "full" if i < last_ctx_layer else ("kv" if i == last_ctx_layer else "none")
            j = i // 2
            ffn_phase(i, 0, LAT + ([CTXG] if mode != "none" else []))
            if stop_stage == (i, 0):
                break
            if i % 2 == 0:
                attn_phase(i, j, LAT + ([CTXG] if mode == "full" else []), LAT + [CTXG])
            else:
                sgmlp_phase(i, j, LAT + ([CTXG] if mode == "full" else []))
            if stop_stage == (i, 1):
                break
            ffn_phase(i, 1, LAT + ([CTXG] if mode == "full" else []))

        SC.barrier()
        AR.reset()
        FG = AR.alloc((D,), F32)
        FGb = Buf("fg")
        fds = SC.new_dsem("fg")
        DMA("sp", FG, fg_d[:, :], (), (FGb,), fds)
        OST = Rot("ost", 2, (D,), F32, dma=True)
        FSQ = Rot("fsq", 2, (D,), F32)
        FS = Rot("fs", 4, (4,), F32)
        OUTb = Buf("outdram")
        out_ops = []
        for c in range(nrows_out // 128):
            g = min(c // 4, 4)
            for half in range(2):
                bk = (c % 2) * 2 + half
                for dd in range(4):
                    d = half * 4 + dd
                    TR(banks[bk][:, dd * 128:(dd + 1) * 128], XT[:, d, c * 128:(c + 1) * 128], (XTb[d][g], IDENTb),
                       (BK[bk],))
            ost, ostb, ods = OST.next()
            b0 = (c % 2) * 2
            if debug_dump:
                for half in range(2):
                    COPY(ost[:, half * 512:(half + 1) * 512], banks[b0 + half][:, :], (BK[b0 + half],), (ostb,),
                         eng=("act" if half else "dve"))
            else:
                fsq, fsqb = FSQ.next()
                fs, fsb = FS.next()
                for half in range(2):
                    ACT(fsq[:, half * 512:(half + 1) * 512], banks[b0 + half][:, :], AF.Square, (BK[b0 + half],),
                        (fsqb,))
                RSUM(fs[:, 0:1], fsq[:, :], (fsqb,), (fsb,))
                ACT(fs[:, 1:2], fs[:, 0:1], AF.Sqrt, (fsb, CONSTb), (fsb,), bias=EPS_RMS, scale=1.0 / D)
                RECIP(fs[:, 2:3], fs[:, 1:2], (fsb,), (fsb,))
                for half in range(2):
                    STT(ost[:, half * 512:(half + 1) * 512], banks[b0 + half][:, :], fs[:, 2:3],
                        FG[:, half * 512:(half + 1) * 512], ALU.mult, ALU.mult, (BK[b0 + half], fsb, FGb), (ostb,))
            o = DMA("sp", out_d[c * 128:(c + 1) * 128, :], ost, (ostb,), (), ods)
            out_ops.append(o)

        nwait = SC.emit()
        SC.final_wait("sp", out_ops)
        print("ops", len(SC.ops), "waits", nwait)
    return nc


def _rope_tables():
    n_freq = 16
    inv = (10000.0 ** (-np.arange(n_freq, dtype=np.float32) / n_freq)).astype(np.float32)
    t = np.arange(S)
    pos = np.stack([t // 64, t % 64], axis=-1).astype(np.float32)
    ang = pos[:, :, None] * inv
    cos = np.cos(ang).astype(np.float32).reshape(S, 32).T
    sin = np.sin(ang).astype(np.float32).reshape(S, 32).T
    ropeC = np.tile(cos, (4, 1))
    ropeT = np.concatenate([sin, -sin, sin, -sin], axis=0)
    return np.ascontiguousarray(ropeC), np.ascontiguousarray(ropeT)


def _head_perm():
    perm = np.zeros(128, dtype=np.int64)
    for r in range(2):
        for half in range(2):
            for axis in range(2):
                for f in range(16):
                    perm[r * 64 + half * 32 + axis * 16 + f] = r * 64 + axis * 32 + half * 16 + f
    return perm


def _fm(v):
    return np.ascontiguousarray(v.reshape(-1, 8, 128).transpose(2, 0, 1).reshape(128, -1))


_PROG_CACHE = {}


def prepare_inputs(inputs):
    f = lambda a: np.ascontiguousarray(np.asarray(a, dtype=np.float32))
    x = f(inputs["x"])
    c = f(inputs["c"])
    ctx = f(inputs["ctx"])
    c_ctx = f(inputs["c_ctx"])
    perm = _head_perm()
    da_w_in = f(inputs["da_w_in"]).copy()
    cols = np.arange(3 * D)
    for blk in range(2):
        for h in range(8):
            base = blk * D + h * 128
            cols[base:base + 128] = base + perm
    da_w_in = np.ascontiguousarray(da_w_in[:, :, cols])
    ropeC, ropeT = _rope_tables()
    b_mod = f(inputs["b_mod"])
    b_modT = np.ascontiguousarray(b_mod.reshape(DEPTH, 72, 128).transpose(2, 0, 1).reshape(128, DEPTH * 72))
    norm_g = f(inputs["norm_g"])
    norm_gT = np.ascontiguousarray(norm_g.reshape(DEPTH, 3, 8, 128).transpose(3, 0, 1, 2).reshape(128, DEPTH * 24))
    da_lambda_b = np.ascontiguousarray(np.tile(f(inputs["da_lambda"]).reshape(1, 512), (128, 1)))
    da_subln_gT = np.ascontiguousarray(f(inputs["da_subln_g"]).T)
    sg_ln_gT = np.ascontiguousarray(f(inputs["sg_ln_g"]).reshape(2, 8, 128).transpose(2, 0, 1).reshape(128, 16))
    sg_ln_bT = np.ascontiguousarray(f(inputs["sg_ln_b"]).reshape(2, 8, 128).transpose(2, 0, 1).reshape(128, 16))
    sg_w_sT = np.ascontiguousarray(f(inputs["sg_w_s"]).transpose(0, 3, 1, 2).reshape(2, 128, 8 * 128))
    sg_b_s_b = np.ascontiguousarray(np.tile(f(inputs["sg_b_s"]).reshape(1, 2 * 1024), (128, 1)))
    final_g_b = np.ascontiguousarray(np.tile(f(inputs["final_g"]).reshape(1, D), (128, 1)))
    shared = {
        "w_mod": f(inputs["w_mod"]), "b_modT": b_modT, "norm_gT": norm_gT,
        "w_ffn_gu": f(inputs["w_ffn_gu"]), "w_ffn_down": f(inputs["w_ffn_down"]),
        "da_w_in": da_w_in, "da_w_out": f(inputs["da_w_out"]), "da_lambda_b": da_lambda_b,
        "da_subln_gT": da_subln_gT, "sg_w_in": f(inputs["sg_w_in"]), "sg_ln_gT": sg_ln_gT, "sg_ln_bT": sg_ln_bT,
        "sg_w_sT": sg_w_sT, "sg_b_s_b": sg_b_s_b, "sg_w_out": f(inputs["sg_w_out"]), "final_g_b": final_g_b,
        "ropeC": ropeC, "ropeT": ropeT,
    }
    in_maps = []
    for b in range(NCORES):
        cT = np.concatenate([c[b].reshape(8, 128).T, c_ctx.reshape(8, 128).T], axis=1)
        m = dict(shared)
        m["x"] = x[b]
        m["ctx"] = ctx[b]
        m["cT"] = np.ascontiguousarray(cT)
        in_maps.append(m)
    return in_maps


def kernel(**inputs):
    in_maps = prepare_inputs(inputs)
    if "full" not in _PROG_CACHE:
        _PROG_CACHE["full"] = build_program()
    nc = _PROG_CACHE["full"]
    res = run_bass_kernel_spmd(nc, in_maps, core_ids=list(range(NCORES)))
    return np.stack([np.asarray(r["out"]) for r in res.results], axis=0).astype(np.float32)
```

```python
import math
from contextlib import ExitStack
import numpy as np
import concourse.bass as bass
import concourse.mybir as mybir
from concourse.bass_utils import run_bass_kernel_spmd

F32 = mybir.dt.float32
BF16 = mybir.dt.bfloat16
AF = mybir.ActivationFunctionType
ALU = mybir.AluOpType
AX = mybir.AxisListType

D = 1024
S = 2048
C = 256
T = S + C
DFF = 2816
NFC = DFF // 128
DEPTH = 4
NMOD = 9
NCORES = 8
GROUPS = [(0, 512), (512, 512), (1024, 512), (1536, 512), (2048, 256)]
LAT = [0, 1, 2, 3]
CTXG = 4
RMS_EPS = 1e-6
LN_EPS = 1e-5
NF = 2
NPIECE = NFC // NF
ARENA_WORDS = 32900


class Buf:
    __slots__ = ("name", "lw", "rd")

    def __init__(self, name):
        self.name = name
        self.lw = None
        self.rd = {}


class Op:
    __slots__ = ("eng", "fn", "deps", "needs_inc", "sig", "dsem", "idx")


class Sched:
    def __init__(self, nc, stack):
        self.nc = nc
        self.stack = stack
        self.ops = []
        self.E = {"pe": nc.tensor, "act": nc.scalar, "dve": nc.vector, "pool": nc.gpsimd, "sp": nc.sync}
        self.esem = {e: stack.enter_context(nc.semaphore("es_" + e)) for e in ("pe", "act", "dve", "pool")}
        self.last = {}
        self.pending_bar = {}
        self.dsems = []
        self.free_ds = []
        self.phase_ds = []

    def new_dsem(self, name, persistent=False):
        if self.free_ds:
            s = self.free_ds.pop()
        else:
            s = [self.stack.enter_context(self.nc.semaphore("ds%d" % len(self.dsems))), 0, None]
            self.dsems.append(s)
        if not persistent:
            self.phase_ds.append(s)
        return s

    def op(self, eng, fn, reads=(), writes=(), dsem=None):
        o = Op()
        o.eng = eng
        o.fn = fn
        o.needs_inc = False
        o.dsem = dsem
        o.sig = None
        o.idx = len(self.ops)
        deps = {}
        for b in reads:
            if b.lw is not None:
                deps[b.lw.idx] = b.lw
        for b in writes:
            if b.lw is not None:
                deps[b.lw.idx] = b.lw
            for r in b.rd.values():
                deps[r.idx] = r
        if eng in self.pending_bar:
            for d in self.pending_bar.pop(eng):
                deps[d.idx] = d
        dl = []
        for d in deps.values():
            if d.eng == "pe" and eng == "pe" and d.dsem is None:
                continue
            d.needs_inc = True
            dl.append(d)
        o.deps = dl
        key = eng if dsem is None else ("d", id(dsem))
        for b in reads:
            b.rd[key] = o
        for b in writes:
            b.lw = o
            b.rd = {}
        if dsem is not None:
            dsem[1] += 16
            o.sig = (dsem[0], dsem[1])
            dsem[2] = o
        else:
            self.last[eng] = o
        self.ops.append(o)
        return o

    def barrier(self):
        deps = list(self.last.values()) + [d[2] for d in self.dsems if d[2] is not None]
        for d in deps:
            d.needs_inc = True
        for e in self.E:
            self.pending_bar[e] = list(deps)
        self.free_ds.extend(self.phase_ds)
        self.phase_ds = []

    def emit(self):
        cnt = {e: 0 for e in self.esem}
        for o in self.ops:
            if o.dsem is None and o.needs_inc:
                cnt[o.eng] += 1
                o.sig = (self.esem[o.eng], cnt[o.eng])
        seen = {e: {} for e in self.E}
        nwait = 0
        for o in self.ops:
            need = {}
            for d in o.deps:
                sem, val = d.sig
                k = id(sem)
                if k not in need or need[k][1] < val:
                    need[k] = (sem, val)
            eng = self.E[o.eng]
            sn = seen[o.eng]
            for k, (sem, val) in need.items():
                if sn.get(k, 0) < val:
                    eng.wait_ge(sem, val)
                    sn[k] = val
                    nwait += 1
            ins = o.fn()
            if o.dsem is not None:
                ins.then_inc(o.sig[0], 16)
            elif o.needs_inc:
                ins.then_inc(o.sig[0], 1)
        return nwait

    def final_wait(self, eng, ops):
        e = self.E[eng]
        need = {}
        for d in ops:
            sem, val = d.sig
            if id(sem) not in need or need[id(sem)][1] < val:
                need[id(sem)] = (sem, val)
        for sem, val in need.values():
            e.wait_ge(sem, val)


class Arena:
    def __init__(self, ap, words):
        self.ap = ap
        self.words = words
        self.off = 0

    def reset(self):
        self.off = 0

    def alloc(self, shape, dtype):
        n = 1
        for s in shape:
            n *= s
        w = n if dtype == F32 else (n + 1) // 2
        assert self.off + w <= self.words, ("arena overflow", self.off, w, self.words)
        a = self.ap[:, self.off:self.off + w]
        self.off += w
        if dtype != F32:
            a = a.bitcast(dtype)
        if len(shape) == 2:
            a = a.rearrange("p (a b) -> p a b", a=shape[0])
        elif len(shape) == 3:
            a = a.rearrange("p (a b c) -> p a b c", a=shape[0], b=shape[1])
        return a


def lam_init_of(i):
    return 0.8 - 0.6 * math.exp(-0.3 * i)


def build_program(nlayers=DEPTH, debug_dump=False, stop_stage=None, dbg_attn=False):
    nc = bass.Bass("TRN2", target_bir_lowering=False)
    dt_in = lambda name, shape: nc.dram_tensor(name, list(shape), F32, kind="ExternalInput").ap()
    x_d = dt_in("x", (S, D))
    ctx_d = dt_in("ctx", (C, D))
    cT_d = dt_in("cT", (128, 16))
    wmod_d = dt_in("w_mod", (DEPTH, D, NMOD * D))
    bmodT_d = dt_in("b_modT", (128, DEPTH * 72))
    ngT_d = dt_in("norm_gT", (128, DEPTH * 24))
    wgu_d = dt_in("w_ffn_gu", (DEPTH, 2, D, 2 * DFF))
    wdn_d = dt_in("w_ffn_down", (DEPTH, 2, DFF, D))
    dawin_d = dt_in("da_w_in", (2, D, 3 * D))
    dawout_d = dt_in("da_w_out", (2, D, D))
    dalam_d = dt_in("da_lambda_b", (128, 512))
    dasg_d = dt_in("da_subln_gT", (128, 2))
    sgwin_d = dt_in("sg_w_in", (2, D, 2 * D))
    sglng_d = dt_in("sg_ln_gT", (128, 16))
    sglnb_d = dt_in("sg_ln_bT", (128, 16))
    sgws_d = dt_in("sg_w_sT", (2, 128, 8 * 128))
    sgbs_d = dt_in("sg_b_s_b", (128, 2 * 1024))
    sgwout_d = dt_in("sg_w_out", (2, D, D))
    fg_d = dt_in("final_g_b", (128, D))
    ropeC_d = dt_in("ropeC", (128, S))
    ropeT_d = dt_in("ropeT", (128, S))
    nrows_out = T if debug_dump else S
    out_d = nc.dram_tensor("out", [nrows_out, D], F32, kind="ExternalOutput").ap()

    stack = ExitStack()
    with stack:
        sb = lambda name, shape, dt: stack.enter_context(nc.sbuf_tensor(name, list(shape), dt))
        XT = sb("XT", (128, 8, T), F32)
        MODT = sb("MODT", (128, DEPTH * 72 * 2), F32)
        SMALL = sb("SMALL", (128, 16 + 288 + 96 + 2 + 16 + 16), F32)
        PRM = sb("PRM", (128, 12 * 48), F32)
        CONST = sb("CONST", (128, 8), F32)
        ONES = sb("ONES", (128, 128), BF16)
        IDENT = sb("IDENT", (128, 128), F32)
        SCT = sb("SCT", (128, 16), BF16)
        ARENA_T = sb("ARENA", (128, ARENA_WORDS), F32)
        banks = [stack.enter_context(nc.psum_tensor("bk%d" % i, [128, 512], F32)) for i in range(8)]
        BK = [Buf("bk%d" % i) for i in range(8)]
        SC = Sched(nc, stack)
        AR = Arena(ARENA_T, ARENA_WORDS)

        def MM(out, lhsT, rhs, start, stop, reads, writes):
            return SC.op("pe", lambda: nc.tensor.matmul(out, lhsT, rhs, start=start, stop=stop), reads, writes)

        def TR(out, in_, reads, writes):
            return SC.op("pe", lambda: nc.tensor.transpose(out, in_, IDENT[:]), reads, writes)

        def ACT(out, in_, func, reads, writes, bias=None, scale=None):
            kw = {}
            if bias is not None:
                kw["bias"] = bias
            if scale is not None:
                kw["scale"] = scale
            return SC.op("act", lambda: nc.scalar.activation(out, in_, func, **kw), reads, writes)

        def TT(out, in0, in1, op, reads, writes, eng="dve"):
            e = SC.E[eng]
            return SC.op(eng, lambda: e.tensor_tensor(out, in0, in1, op), reads, writes)

        def TS(out, in0, s1, op0, reads, writes, s2=None, op1=None, eng="dve"):
            e = SC.E[eng]
            if op1 is None:
                return SC.op(eng, lambda: e.tensor_scalar(out, in0, s1, None, op0), reads, writes)
            return SC.op(eng, lambda: e.tensor_scalar(out, in0, s1, s2, op0, op1), reads, writes)

        def STT(out, in0, scalar, in1, op0, op1, reads, writes, eng="dve"):
            e = SC.E[eng]
            return SC.op(eng, lambda: e.scalar_tensor_tensor(out, in0, scalar, in1, op0, op1), reads, writes)

        def RECIP(out, in_, reads, writes):
            return SC.op("dve", lambda: nc.vector.reciprocal(out, in_), reads, writes)

        def RSUM(out, in_, reads, writes):
            return SC.op("dve", lambda: nc.vector.reduce_sum(out, in_, AX.X), reads, writes)

        def COPY(out, in_, reads, writes, eng="dve"):
            if eng == "act":
                return SC.op("act", lambda: nc.scalar.copy(out, in_), reads, writes)
            e = SC.E[eng]
            return SC.op(eng, lambda: e.tensor_copy(out, in_), reads, writes)

        def DMA(queue, out, in_, reads, writes, dsem):
            e = SC.E[queue]
            return SC.op(queue, lambda: e.dma_start(out=out, in_=in_), reads, writes, dsem=dsem)

        class Rot:
            def __init__(self, name, n, shape, dtype, dma=False):
                self.t = [AR.alloc(shape, dtype) for _ in range(n)]
                self.b = [Buf("%s%d" % (name, i)) for i in range(n)]
                self.ds = [SC.new_dsem("%s%d" % (name, i)) for i in range(n)] if dma else None
                self.i = -1
                self.n = n

            def next(self):
                self.i = (self.i + 1) % self.n
                if self.ds:
                    return self.t[self.i], self.b[self.i], self.ds[self.i]
                return self.t[self.i], self.b[self.i]

        XTb = [[Buf("xt%d_%d" % (d, g)) for g in range(5)] for d in range(8)]
        MODb = [Buf("mod%d" % i) for i in range(DEPTH)]
        SMALLb = Buf("small")
        CONSTb = Buf("const")
        LAMb = Buf("lam")
        ONESb = Buf("ones")
        IDENTb = Buf("ident")
        SCTb = Buf("sct")
        PRMb = [Buf("prm%d" % i) for i in range(12)]
        small_ds = SC.new_dsem("small")

        cT = SMALL[:, 0:16]
        bmodT = SMALL[:, 16:304]
        ngT = SMALL[:, 304:400]
        dasg = SMALL[:, 400:402]
        sglng = SMALL[:, 402:418]
        sglnb = SMALL[:, 418:434]
        for dst, src in ((cT, cT_d), (bmodT, bmodT_d), (ngT, ngT_d), (dasg, dasg_d), (sglng, sglng_d), (sglnb, sglnb_d)):
            DMA("sp", dst, src[:, :], (), (SMALLb,), small_ds)
        LAMB = AR.alloc((512,), F32)
        LTMP = AR.alloc((160,), F32)
        lam_ds = SC.new_dsem("lam")
        DMA("sp", LAMB, dalam_d[:, :], (), (LAMb,), lam_ds)

        SC.op("dve", lambda: nc.vector.memset(CONST[:, 0:1], RMS_EPS), (), (CONSTb,))
        SC.op("dve", lambda: nc.vector.memset(CONST[:, 1:2], LN_EPS), (), (CONSTb,))
        SC.op("dve", lambda: nc.vector.memset(ONES[:], 1.0), (), (ONESb,))
        SC.op("pool", lambda: nc.gpsimd.memset(IDENT[:], 0.0), (), (IDENTb,))
        SC.op("pool", lambda: nc.gpsimd.affine_select(out=IDENT[:], in_=IDENT[:], pattern=[[-1, 128]],
                                                      compare_op=ALU.not_equal, fill=1.0, base=0,
                                                      channel_multiplier=1), (), (IDENTb,))
        EPS_RMS = CONST[:, 0:1]
        EPS_LN = CONST[:, 1:2]

        for j in range(2):
            li = lam_init_of(2 * j)
            lb = LAMB[:, j * 256:(j + 1) * 256]
            TT(LTMP[:, 0:64], lb[:, 0:64], lb[:, 64:128], ALU.mult, (LAMb,), (LAMb,))
            TT(LTMP[:, 64:128], lb[:, 128:192], lb[:, 192:256], ALU.mult, (LAMb,), (LAMb,))
            RSUM(LTMP[:, 128:129], LTMP[:, 0:64], (LAMb,), (LAMb,))
            RSUM(LTMP[:, 129:130], LTMP[:, 64:128], (LAMb,), (LAMb,))
            ACT(LTMP[:, 130:132], LTMP[:, 128:130], AF.Exp, (LAMb,), (LAMb,))
            TT(LTMP[:, 132:133], LTMP[:, 131:132], LTMP[:, 130:131], ALU.subtract, (LAMb,), (LAMb,))
            TS(CONST[:, 4 + j:5 + j], LTMP[:, 132:133], -li, ALU.add, (LAMb,), (CONSTb,))
            TS(CONST[:, 6 + j:7 + j], dasg[:, j:j + 1], 1.0 - li, ALU.mult, (SMALLb,), (CONSTb,))

        stage = Rot("stage", 2, (D,), F32, dma=True)
        for c in range(T // 128):
            st, stb, sds = stage.next()
            src = x_d[c * 128:(c + 1) * 128, :] if c < 16 else ctx_d[(c - 16) * 128:(c - 15) * 128, :]
            DMA("sp", st, src, (), (stb,), sds)
            g = min(c // 4, 4)
            for half in range(2):
                bk = (c % 2) * 2 + half
                for dd in range(4):
                    d = half * 4 + dd
                    TR(banks[bk][:, dd * 128:(dd + 1) * 128], st[:, d * 128:(d + 1) * 128], (stb, IDENTb), (BK[bk],))
                COPY(XT[:, half * 4:half * 4 + 4, c * 128:(c + 1) * 128],
                     banks[bk][:, :].rearrange("p (a b) -> p a b", a=4),
                     (BK[bk],), [XTb[half * 4 + dd][g] for dd in range(4)], eng=("act" if half else "dve"))

        ACT(SCT[:, :].rearrange("p (k c) -> p k c", c=2), cT.rearrange("p (c k) -> p k c", c=2), AF.Silu,
            (SMALLb,), (SCTb,))
        def mod_jobs(i, wm, mbk):
            jobs = []

            def piece(q):
                wt, wb, wds = wm.next()
                DMA("pool", wt, wmod_d[i].rearrange("(k p) f -> p k f", p=128)[:, :, q * 512:(q + 1) * 512],
                    (), (wb,), wds)
                for m in range(4):
                    mm_ = q * 4 + m
                    for k in range(8):
                        MM(banks[mbk][:, mm_ * 2:mm_ * 2 + 2], wt[:, k, m * 128:(m + 1) * 128],
                           SCT[:, k * 2:k * 2 + 2], k == 0, k == 7, (wb, SCTb), (BK[mbk],))

            def fin():
                for col in range(2):
                    TT(MODT[:, i * 144:(i + 1) * 144].rearrange("p (m c) -> p m c", c=2)[:, :, col],
                       banks[mbk][:, 0:144].rearrange("p (m c) -> p m c", c=2)[:, :, col],
                       bmodT[:, i * 72:(i + 1) * 72], ALU.add, (BK[mbk], SMALLb), (MODb[i],))

            for q in range(18):
                jobs.append(lambda q=q: piece(q))
            jobs.append(fin)
            return jobs

        wm0 = Rot("wm", 3, (8, 512), BF16, dma=True)
        for job in mod_jobs(0, wm0, 4):
            job()

        def mod_ap(i, kmod, col):
            return MODT[:, i * 144 + kmod * 16:i * 144 + kmod * 16 + 16].rearrange("p (d c) -> p d c", c=2)[:, :, col]

        def prep_params(i, s, gate_mul):
            slot = i * 3 + s
            base = slot * 48
            for col in range(2):
                A = PRM[:, base + col * 8:base + col * 8 + 8]
                SH = PRM[:, base + 16 + col * 8:base + 16 + col * 8 + 8]
                G = PRM[:, base + 32 + col * 8:base + 32 + col * 8 + 8]
                STT(A, mod_ap(i, 3 * s + 1, col), 1.0, ngT[:, i * 24 + s * 8:i * 24 + s * 8 + 8], ALU.add, ALU.mult,
                    (MODb[i], SMALLb), (PRMb[slot],))
                COPY(SH, mod_ap(i, 3 * s, col), (MODb[i],), (PRMb[slot],))
                TS(G, mod_ap(i, 3 * s + 2, col), gate_mul, ALU.mult, (MODb[i],), (PRMb[slot],))
            return slot

        def prm(slot, which, g, d):
            col = 1 if g == CTXG else 0
            o = slot * 48 + which * 16 + col * 8 + d
            return PRM[:, o:o + 1]

        def prenorm(groups, slot, XN, XNb, ssbanks, FR):
            SQ = Rot("sq", 1, (8, 512), BF16)
            RS = Rot("rs", 1, (512,), F32)
            TMP = FR
            for n, g in enumerate(groups):
                g0, gs = GROUPS[g]
                sq, sqb = SQ.next()
                bk = ssbanks[n % len(ssbanks)]
                for d in range(8):
                    ACT(sq[:, d, 0:gs], XT[:, d, g0:g0 + gs], AF.Square, (XTb[d][g],), (sqb,))
                for d in range(8):
                    MM(banks[bk][:, 0:gs], ONES[:], sq[:, d, 0:gs], d == 0, d == 7, (ONESb, sqb), (BK[bk],))
                rs, rsb = RS.next()
                ACT(rs[:, 0:gs], banks[bk][:, 0:gs], AF.Ln, (BK[bk], CONSTb), (rsb,), bias=EPS_RMS, scale=1.0 / D)
                ACT(rs[:, 0:gs], rs[:, 0:gs], AF.Exp, (rsb,), (rsb,), scale=-0.5)
                for d in range(8):
                    tp, tpb = TMP.next()
                    STT(tp[:, 0:gs], XT[:, d, g0:g0 + gs], prm(slot, 0, g, d), rs[:, 0:gs], ALU.mult, ALU.mult,
                        (XTb[d][g], PRMb[slot], rsb), (tpb,))
                    ACT(XN[:, d, g0:g0 + gs], tp[:, 0:gs], AF.Identity, (tpb, PRMb[slot]), (XNb[d][g],),
                        bias=prm(slot, 1, g, d))

        def ffn_phase(i, which, groups):
            SC.barrier()
            AR.reset()
            slot = prep_params(i, 0 if which == 0 else 2, 0.5)
            XN = AR.alloc((8, T), BF16)
            XNb = [[Buf("xn%d_%d" % (d, g)) for g in range(5)] for d in range(8)]
            WG = Rot("wg", 3, (8, NF * 128), BF16, dma=True)
            WU = Rot("wu", 3, (8, NF * 128), BF16, dma=True)
            WD = Rot("wd", 3, (NF, D), BF16, dma=True)
            FR = Rot("fr", 8, (512,), F32)
            SG = FR
            HT = Rot("ht", 4, (512,), BF16)
            wguv = wgu_d[i, which].rearrange("(k p) f -> p k f", p=128)
            wdnv = wdn_d[i, which].rearrange("(f p) d -> p f d", p=128)
            loaded = {}

            def load_piece(p):
                f0 = p * NF * 128
                wg = WG.next()
                wu = WU.next()
                wd = WD.next()
                DMA("pool", wg[0], wguv[:, :, f0:f0 + NF * 128], (), (wg[1],), wg[2])
                DMA("pool", wu[0], wguv[:, :, DFF + f0:DFF + f0 + NF * 128], (), (wu[1],), wu[2])
                DMA("pool", wd[0], wdnv[:, p * NF:(p + 1) * NF, :], (), (wd[1],), wd[2])
                loaded[p] = (wg, wu, wd)

            load_piece(0)
            load_piece(1)
            prenorm(groups, slot, XN, XNb, [7], FR)
            items = [(p, g) for p in range(NPIECE) for g in groups]
            ybanks = [4, 5, 6]
            state = {"y": 0}

            def emit_gu(p, g):
                g0, gs = GROUPS[g]
                wg, wu, wd = loaded[p]
                hts = []
                for fi in range(NF):
                    bg, bu = (0, 1) if fi % 2 == 0 else (2, 3)
                    for k in range(8):
                        MM(banks[bg][:, 0:gs], wg[0][:, k, fi * 128:(fi + 1) * 128], XN[:, k, g0:g0 + gs], k == 0,
                           k == 7, (wg[1], XNb[k][g]), (BK[bg],))
                    for k in range(8):
                        MM(banks[bu][:, 0:gs], wu[0][:, k, fi * 128:(fi + 1) * 128], XN[:, k, g0:g0 + gs], k == 0,
                           k == 7, (wu[1], XNb[k][g]), (BK[bu],))
                    sg, sgb = SG.next()
                    ht, htb = HT.next()
                    ACT(sg[:, 0:gs], banks[bg][:, 0:gs], AF.Silu, (BK[bg],), (sgb,))
                    TT(ht[:, 0:gs], sg[:, 0:gs], banks[bu][:, 0:gs], ALU.mult, (sgb, BK[bu]), (htb,))
                    hts.append((ht, htb))
                return hts

            def emit_down(p, g, hts):
                g0, gs = GROUPS[g]
                wg, wu, wd = loaded[p]
                for d in range(8):
                    yb = ybanks[state["y"] % 3]
                    state["y"] += 1
                    for fi in range(NF):
                        MM(banks[yb][:, 0:gs], wd[0][:, fi, d * 128:(d + 1) * 128], hts[fi][0][:, 0:gs], fi == 0,
                           fi == NF - 1, (wd[1], hts[fi][1]), (BK[yb],))
                    STT(XT[:, d, g0:g0 + gs], banks[yb][:, 0:gs], prm(slot, 2, g, d), XT[:, d, g0:g0 + gs], ALU.mult,
                        ALU.add, (BK[yb], PRMb[slot], XTb[d][g]), (XTb[d][g],))

            mjobs = []
            if which == 0 and i + 1 < nlayers:
                mjobs = mod_jobs(i + 1, Rot("wmx", 2, (8, 512), BF16, dma=True), 7)
            prev = None
            for n, (p, g) in enumerate(items):
                hts = emit_gu(p, g)
                if prev is not None:
                    emit_down(*prev)
                if g == groups[0] and p + 2 < NPIECE:
                    load_piece(p + 2)
                prev = (p, g, hts)
                if mjobs and n % 2 == 1:
                    mjobs.pop(0)()
            emit_down(*prev)
            while mjobs:
                mjobs.pop(0)()

        def attn_phase(i, j, qgroups, kvgroups):
            SC.barrier()
            AR.reset()
            slot = prep_params(i, 1, 1.0)
            XN = AR.alloc((8, T), BF16)
            XNb = [[Buf("xn%d_%d" % (d, g)) for g in range(5)] for d in range(8)]
            ROPC = AR.alloc((S,), F32)
            ROPT = AR.alloc((S,), F32)
            ROPb = Buf("rope")
            rds = SC.new_dsem("rope%d" % i)
            DMA("sp", ROPC, ropeC_d[:, :], (), (ROPb,), rds)
            DMA("sp", ROPT, ropeT_d[:, :], (), (ROPb,), rds)
            WQ = Rot("wq", 2, (8, 128), BF16, dma=True)
            WK = Rot("wk", 2, (8, 128), BF16, dma=True)
            WV = Rot("wv", 2, (8, 128), BF16, dma=True)
            WO = Rot("wo", 3, (D,), BF16, dma=True)
            QTA = AR.alloc((T,), BF16)
            QTB = AR.alloc((T,), BF16)
            QTb = [Buf("qt%d" % g) for g in range(5)]
            SC.op("dve", lambda: nc.vector.memset(QTA[64:128, :], 0.0), (), QTb)
            SC.op("dve", lambda: nc.vector.memset(QTB[0:64, :], 0.0), (), QTb)
            KT = Rot("kt", 2, (T,), BF16)
            VV = Rot("vv", 2, (18, 128), BF16)
            PP = [Rot("pp%d" % r, 2, (512,), BF16) for r in range(2)]
            FR = Rot("fr", 2, (512,), F32)
            NR = Rot("nr", 4, (512,), F32)
            HR = Rot("hr", 5, (512,), BF16)
            OSQ = ON = HR
            winv = dawin_d[j].rearrange("(k p) f -> p k f", p=128)
            woutv = dawout_d[j].rearrange("(h p) d -> p h d", p=128)
            NLAM = CONST[:, 4 + j:5 + j]
            SGC = CONST[:, 6 + j:7 + j]
            misc = [6, 7]
            mstate = {"m": 0}

            def mbank():
                b = misc[mstate["m"] % 2]
                mstate["m"] += 1
                return b

            prenorm(kvgroups, slot, XN, XNb, [6, 7], FR)
            hw = {}

            def load_head(h):
                wq = WQ.next()
                wk = WK.next()
                wv = WV.next()
                wo = WO.next()
                DMA("pool", wq[0], winv[:, :, h * 128:(h + 1) * 128], (), (wq[1],), wq[2])
                DMA("pool", wk[0], winv[:, :, D + h * 128:D + (h + 1) * 128], (), (wk[1],), wk[2])
                DMA("pool", wv[0], winv[:, :, 2 * D + h * 128:2 * D + (h + 1) * 128], (), (wv[1],), wv[2])
                DMA("pool", wo[0], woutv[:, h, :], (), (wo[1],), wo[2])
                hw[h] = (wq, wk, wv, wo)

            def rope_parts(bk, g0, gs):
                qs, qsb = FR.next()
                COPY(qs[:, 0:gs], banks[bk][:, 0:gs], (BK[bk],), (qsb,), eng="dve")
                rb, rbb = FR.next()
                for blk in range(4):
                    sp_ = blk ^ 1
                    TT(rb[blk * 32:(blk + 1) * 32, 0:gs], qs[sp_ * 32:(sp_ + 1) * 32, 0:gs],
                       ROPT[sp_ * 32:(sp_ + 1) * 32, g0:g0 + gs], ALU.mult, (qsb, ROPb), (rbb,))
                TT(qs[:, 0:gs], qs[:, 0:gs], ROPC[:, g0:g0 + gs], ALU.mult, (qsb, ROPb), (qsb,))
                return qs, qsb, rb, rbb

            proj = {}

            def project_q(h):
                wq = hw[h][0]
                for g in qgroups:
                    g0, gs = GROUPS[g]
                    bk = mbank()
                    for k in range(8):
                        MM(banks[bk][:, 0:gs], wq[0][:, k, :], XN[:, k, g0:g0 + gs], k == 0, k == 7,
                           (wq[1], XNb[k][g]), (BK[bk],))
                    if g == CTXG:
                        COPY(QTA[0:64, g0:g0 + gs], banks[bk][0:64, 0:gs], (BK[bk],), (QTb[g],), eng="act")
                        COPY(QTB[64:128, g0:g0 + gs], banks[bk][64:128, 0:gs], (BK[bk],), (QTb[g],), eng="act")
                    else:
                        ra, rab, rb, rbb = rope_parts(bk, g0, gs)
                        TT(QTA[0:64, g0:g0 + gs], ra[0:64, 0:gs], rb[0:64, 0:gs], ALU.add, (rab, rbb), (QTb[g],))
                        TT(QTB[64:128, g0:g0 + gs], ra[64:128, 0:gs], rb[64:128, 0:gs], ALU.add, (rab, rbb),
                           (QTb[g],))

            def project_kv(h):
                wq, wk, wv, wo = hw[h]
                kt, ktb = KT.next()
                vv, vvb = VV.next()
                for g in kvgroups:
                    g0, gs = GROUPS[g]
                    bk = mbank()
                    for k in range(8):
                        MM(banks[bk][:, 0:gs], wk[0][:, k, :], XN[:, k, g0:g0 + gs], k == 0, k == 7,
                           (wk[1], XNb[k][g]), (BK[bk],))
                    if g == CTXG:
                        COPY(kt[:, g0:g0 + gs], banks[bk][:, 0:gs], (BK[bk],), (ktb,), eng="act")
                    else:
                        ra, rab, rb, rbb = rope_parts(bk, g0, gs)
                        TT(kt[:, g0:g0 + gs], ra[:, 0:gs], rb[:, 0:gs], ALU.add, (rab, rbb), (ktb,))
                for g in kvgroups:
                    g0, gs = GROUPS[g]
                    bk = mbank()
                    nch = gs // 128
                    for cc in range(nch):
                        t0 = g0 + cc * 128
                        for k in range(8):
                            MM(banks[bk][:, cc * 128:(cc + 1) * 128], XN[:, k, t0:t0 + 128], wv[0][:, k, :], k == 0,
                               k == 7, (wv[1], XNb[k][g]), (BK[bk],))
                    COPY(vv[:, g0 // 128:g0 // 128 + nch, :], banks[bk][:, 0:gs].rearrange("p (a b) -> p a b", a=nch),
                         (BK[bk],), (vvb,), eng="act")
                proj[h] = (kt, ktb, vv, vvb)

            def key_loop(h, g, inject=()):
                inject = list(inject)
                kt, ktb, vv, vvb = proj[h]
                g0, gs = GROUPS[g]
                chunks = list(range(18)) if g != CTXG else [16, 17]
                pend = None
                nk = len(chunks)

                def pv1(r, ci, c, pt, ptb):
                    MM(banks[2 + r][:, 0:gs], vv[:, c, :], pt[:, 0:gs], ci == 0, ci == nk - 1, (vvb, ptb),
                       (BK[2 + r],))
                    MM(banks[4 + r][:, 0:gs], ONES[:], pt[:, 0:gs], ci == 0, ci == nk - 1, (ONESb, ptb),
                       (BK[4 + r],))

                for ci, c in enumerate(chunks):
                    ps = []
                    for r in range(2):
                        MM(banks[r][:, 0:gs], kt[:, c * 128:(c + 1) * 128], (QTA, QTB)[r][:, g0:g0 + gs], True, True,
                           (ktb, QTb[g]), (BK[r],))
                        pt, ptb = PP[r].next()
                        ACT(pt[:, 0:gs], banks[r][:, 0:gs], AF.Exp, (BK[r],), (ptb,), scale=0.125)
                        ps.append((pt, ptb))
                        if pend is not None:
                            pv1(r, pend[0], pend[1], *pend[2 + r])
                    pend = (ci, c, ps[0], ps[1])
                    while inject and inject[0][0] <= ci:
                        inject.pop(0)[1]()
                for r in range(2):
                    pv1(r, pend[0], pend[1], *pend[2 + r])
                while inject:
                    inject.pop(0)[1]()

            def norm_chain(h, g):
                g0, gs = GROUPS[g]
                r1, r1b = NR.next()
                r2, r2b = NR.next()
                t1, t1b = NR.next()
                t2, t2b = NR.next()
                ACT(r1[:, 0:gs], banks[4][:, 0:gs], AF.Ln, (BK[4],), (r1b,))
                COPY(t1[:, 0:gs], banks[2][:, 0:gs], (BK[2],), (t1b,))
                ACT(r2[:, 0:gs], banks[5][:, 0:gs], AF.Ln, (BK[5],), (r2b,))
                COPY(t2[:, 0:gs], banks[3][:, 0:gs], (BK[3],), (t2b,))
                ACT(r1[:, 0:gs], r1[:, 0:gs], AF.Exp, (r1b,), (r1b,), scale=-1.0)
                ACT(r2[:, 0:gs], r2[:, 0:gs], AF.Exp, (r2b,), (r2b,), scale=-1.0)
                TT(t1[:, 0:gs], t1[:, 0:gs], r1[:, 0:gs], ALU.mult, (t1b, r1b), (t1b,))
                STT(t2[:, 0:gs], t2[:, 0:gs], NLAM, r2[:, 0:gs], ALU.mult, ALU.mult, (t2b, CONSTb, r2b), (t2b,))
                oo, oob = r1, r1b
                TT(oo[:, 0:gs], t1[:, 0:gs], t2[:, 0:gs], ALU.add, (t1b, t2b), (oob,))
                osq, osqb = OSQ.next()
                TT(osq[:, 0:gs], oo[:, 0:gs], oo[:, 0:gs], ALU.mult, (oob,), (osqb,))
                return (g, gs, oo, oob, osq, osqb, r2, r2b)

            def norm_chain_b(st):
                g, gs, oo, oob, osq, osqb, r2, r2b = st
                bk = mbank()
                MM(banks[bk][:, 0:gs], ONES[:], osq[:, 0:gs], True, True, (ONESb, osqb), (BK[bk],))
                zi, zib = r2, r2b
                ACT(zi[:, 0:gs], banks[bk][:, 0:gs], AF.Ln, (BK[bk], CONSTb), (zib,), bias=EPS_RMS, scale=1.0 / 128)
                ACT(zi[:, 0:gs], zi[:, 0:gs], AF.Exp, (zib,), (zib,), scale=-0.5)
                on, onb = ON.next()
                STT(on[:, 0:gs], oo[:, 0:gs], SGC, zi[:, 0:gs], ALU.mult, ALU.mult, (oob, CONSTb, zib), (onb,))
                return on, onb

            def out_proj(h, g, on, onb, ds=range(8)):
                if dbg_attn:
                    return
                g0, gs = GROUPS[g]
                wo = hw[h][3]
                for d in ds:
                    bk = mbank()
                    MM(banks[bk][:, 0:gs], wo[0][:, d * 128:(d + 1) * 128], on[:, 0:gs], True, True, (wo[1], onb),
                       (BK[bk],))
                    STT(XT[:, d, g0:g0 + gs], banks[bk][:, 0:gs], prm(slot, 2, g, d), XT[:, d, g0:g0 + gs], ALU.mult,
                        ALU.add, (BK[bk], PRMb[slot], XTb[d][g]), (XTb[d][g],))

            load_head(0)
            load_head(1)
            project_kv(0)
            prev = None
            box = {}

            def part_b(p):
                box["on"] = norm_chain_b(p[2])

            def part_c(p, ds=range(8)):
                out_proj(p[0], p[1], *box["on"], ds=ds)

            for h in range(8):
                project_q(h)
                if h + 1 < 8:
                    project_kv(h + 1)
                for g in qgroups:
                    inj = []
                    if prev is not None:
                        inj = [(3, (lambda p=prev: part_b(p)))]
                        inj += [(8 + d, (lambda p=prev, d=d: part_c(p, (d,)))) for d in range(8)]
                    key_loop(h, g, inj)
                    st = norm_chain(h, g)
                    prev = (h, g, st)
                if h + 2 < 8:
                    load_head(h + 2)
            part_b(prev)
            part_c(prev)

        def sgmlp_phase(i, j, groups):
            SC.barrier()
            AR.reset()
            slot = prep_params(i, 1, 1.0)
            XN = AR.alloc((8, T), BF16)
            XNb = [[Buf("xn%d_%d" % (d, g)) for g in range(5)] for d in range(8)]
            WIN = AR.alloc((8, 2 * D), BF16)
            WOUT = AR.alloc((8, D), BF16)
            WST = AR.alloc((8, 128), BF16)
            BIAS = AR.alloc((8, 128), F32)
            WINb, WOUTb, WSTb, BIASb = Buf("win"), Buf("wout"), Buf("wst"), Buf("bias")
            ds1, ds2, ds3, ds4 = (SC.new_dsem("sg%d_%d" % (i, n)) for n in range(4))
            winv = sgwin_d[j].rearrange("(k p) f -> p k f", p=128)
            DMA("pool", WIN[:, :, 0:D], winv[:, :, 0:D], (), (WINb,), ds1)
            DMA("pool", WIN[:, :, D:2 * D], winv[:, :, D:2 * D], (), (WINb,), ds1)
            DMA("pool", WOUT, sgwout_d[j].rearrange("(k p) f -> p k f", p=128), (), (WOUTb,), ds2)
            DMA("pool", WST, sgws_d[j].rearrange("p (g t) -> p g t", g=8), (), (WSTb,), ds3)
            DMA("sp", BIAS, sgbs_d[:, j * 1024:(j + 1) * 1024].rearrange("p (g t) -> p g t", g=8), (), (BIASb,), ds4)
            FR = Rot("fr", 2, (512,), F32)
            prenorm(groups, slot, XN, XNb, [7], FR)
            for half in range(2):
                MM(banks[half][:, :], ONES[:], WST[:, half * 4:half * 4 + 4, :].rearrange("p a b -> p (a b)"), True,
                   True, (ONESb, WSTb), (BK[half],))
            for gq in range(8):
                STT(BIAS[:, gq, :], banks[gq // 4][:, (gq % 4) * 128:(gq % 4 + 1) * 128],
                    sglnb[:, j * 8 + gq:j * 8 + gq + 1], BIAS[:, gq, :], ALU.mult, ALU.add,
                    (BK[gq // 4], SMALLb, BIASb), (BIASb,))
            SUB = 256
            UT = Rot("ut", 1, (8, SUB), BF16)
            GT = Rot("gt", 1, (8, SUB), BF16)
            VG = Rot("vg", 2, (D,), F32)
            VSQ = Rot("vsq", 1, (D,), BF16)
            VH = Rot("vh", 2, (D,), BF16)
            ST = Rot("st", 4, (16,), F32)
            MT = Rot("mt", 3, (128,), F32)
            pairs = [(2, 3), (4, 5), (6, 7)]
            subs = []
            for g in groups:
                gg0, ggs = GROUPS[g]
                for sub in range(ggs // SUB):
                    subs.append((g, gg0 + sub * SUB))
            cstate = {"t": 0}

            def emit_v(g, g0):
                st, stb = ST.next()
                chunks = []
                for cpos in range(SUB // 128):
                    t = cstate["t"]
                    cstate["t"] += 1
                    vb = pairs[((t // 2) + cpos) % 3]
                    t0 = g0 + cpos * 128
                    for half in range(2):
                        for k in range(8):
                            MM(banks[vb[half]][:, :], XN[:, k, t0:t0 + 128],
                               WIN[:, k, D + half * 512:D + (half + 1) * 512], k == 0, k == 7,
                               (WINb, XNb[k][g]), (BK[vb[half]],))
                    vg, vgb = VG.next()
                    vsq, vsqb = VSQ.next()
                    for half in range(2):
                        ACT(vg[:, half * 512:(half + 1) * 512], banks[vb[half]][:, :], AF.Gelu_apprx_tanh,
                            (BK[vb[half]],), (vgb,))
                    ACT(vsq[:, :], vg[:, :], AF.Square, (vgb,), (vsqb,))
                    RSUM(st[:, cpos:cpos + 1], vg[:, :], (vgb,), (stb,))
                    RSUM(st[:, 2 + cpos:3 + cpos], vsq[:, :], (vsqb,), (stb,))
                    chunks.append((t, vg, vgb))
                TS(st[:, 4:6], st[:, 0:2], 1.0 / D, ALU.mult, (stb,), (stb,))
                TT(st[:, 6:8], st[:, 4:6], st[:, 4:6], ALU.mult, (stb,), (stb,))
                STT(st[:, 8:10], st[:, 2:4], 1.0 / D, st[:, 6:8], ALU.mult, ALU.subtract, (stb,), (stb,))
                ACT(st[:, 10:12], st[:, 8:10], AF.Sqrt, (stb, CONSTb), (stb,), bias=EPS_LN)
                RECIP(st[:, 12:14], st[:, 10:12], (stb,), (stb,))
                out = []
                for cpos, (t, vg, vgb) in enumerate(chunks):
                    vh, vhb = VH.next()
                    TS(vh[:, :], vg[:, :], st[:, 4 + cpos:5 + cpos], ALU.subtract, (vgb, stb), (vhb,),
                       s2=st[:, 12 + cpos:13 + cpos], op1=ALU.mult)
                    out.append((t, vh, vhb))
                return out

            def emit_u(g, g0):
                ut, utb = UT.next()
                for cc in range(8):
                    bk = cc % 2
                    for k in range(8):
                        MM(banks[bk][:, 0:SUB], WIN[:, k, cc * 128:(cc + 1) * 128], XN[:, k, g0:g0 + SUB], k == 0,
                           k == 7, (WINb, XNb[k][g]), (BK[bk],))
                    ACT(ut[:, cc, 0:SUB], banks[bk][:, 0:SUB], AF.Gelu_apprx_tanh, (BK[bk],), (utb,))
                return ut, utb

            def emit_spatial(vhs, ut, utb):
                gt, gtb = GT.next()
                for cpos, (t, vh, vhb) in enumerate(vhs):
                    mbp = pairs[((t // 2) + (2 if cpos == 0 else 0)) % 3]
                    for gq in range(8):
                        bk = mbp[gq // 4]
                        MM(banks[bk][:, (gq % 4) * 128:(gq % 4 + 1) * 128], vh[:, gq * 128:(gq + 1) * 128],
                           WST[:, gq, :], True, True, (vhb, WSTb), (BK[bk],))
                    for gq in range(8):
                        bk = mbp[gq // 4]
                        mt, mtb = MT.next()
                        STT(mt[:, :], banks[bk][:, (gq % 4) * 128:(gq % 4 + 1) * 128],
                            sglng[:, j * 8 + gq:j * 8 + gq + 1], BIAS[:, gq, :], ALU.mult, ALU.add,
                            (BK[bk], SMALLb, BIASb), (mtb,))
                        TT(gt[:, gq, cpos * 128:(cpos + 1) * 128], mt[:, :],
                           ut[:, gq, cpos * 128:(cpos + 1) * 128], ALU.mult, (mtb, utb), (gtb,))
                return gt, gtb

            def emit_y(g, g0, gt, gtb):
                for d in range(8):
                    bk = d % 2
                    for cc in range(8):
                        MM(banks[bk][:, 0:SUB], WOUT[:, cc, d * 128:(d + 1) * 128], gt[:, cc, 0:SUB], cc == 0,
                           cc == 7, (WOUTb, gtb), (BK[bk],))
                    STT(XT[:, d, g0:g0 + SUB], banks[bk][:, 0:SUB], prm(slot, 2, g, d), XT[:, d, g0:g0 + SUB],
                        ALU.mult, ALU.add, (BK[bk], PRMb[slot], XTb[d][g]), (XTb[d][g],))

            prev_y = None
            for (g, g0) in subs:
                vhs = emit_v(g, g0)
                if prev_y is not None:
                    emit_y(*prev_y)
                ut, utb = emit_u(g, g0)
                gt, gtb = emit_spatial(vhs, ut, utb)
                prev_y = (g, g0, gt, gtb)
            emit_y(*prev_y)

        last_ctx_layer = 2
        done = False
        for i in range(nlayers):
            mode = "full" if i < last_ctx_layer else ("kv" if i == last_ctx_layer else "none")
            j = i // 2
            ffn_phase(i, 0, LAT + ([CTXG] if mode != "none" else []))
            if stop_stage == (i, 0):
                break
            if i % 2 == 0:
                attn_phase(i, j, LAT + ([CTXG] if mode == "full" else []), LAT + [CTXG])
            else:
                sgmlp_phase(i, j, LAT + ([CTXG] if mode == "full" else []))
            if stop_stage == (i, 1):
                break
            ffn_phase(i, 1, LAT + ([CTXG] if mode == "full" else []))

        SC.barrier()
        AR.reset()
        FG = AR.alloc((D,), F32)
        FGb = Buf("fg")
        fds = SC.new_dsem("fg")
        DMA("sp", FG, fg_d[:, :], (), (FGb,), fds)
        OST = Rot("ost", 2, (D,), F32, dma=True)
        FSQ = Rot("fsq", 2, (D,), F32)
        FS = Rot("fs", 4, (4,), F32)
        OUTb = Buf("outdram")
        out_ops = []
        for c in range(nrows_out // 128):
            g = min(c // 4, 4)
            for half in range(2):
                bk = (c % 2) * 2 + half
                for dd in range(4):
                    d = half * 4 + dd
                    TR(banks[bk][:, dd * 128:(dd + 1) * 128], XT[:, d, c * 128:(c + 1) * 128], (XTb[d][g], IDENTb),
                       (BK[bk],))
            ost, ostb, ods = OST.next()
            b0 = (c % 2) * 2
            if debug_dump:
                for half in range(2):
                    COPY(ost[:, half * 512:(half + 1) * 512], banks[b0 + half][:, :], (BK[b0 + half],), (ostb,),
                         eng=("act" if half else "dve"))
            else:
                fsq, fsqb = FSQ.next()
                fs, fsb = FS.next()
                for half in range(2):
                    ACT(fsq[:, half * 512:(half + 1) * 512], banks[b0 + half][:, :], AF.Square, (BK[b0 + half],),
                        (fsqb,))
                RSUM(fs[:, 0:1], fsq[:, :], (fsqb,), (fsb,))
                ACT(fs[:, 1:2], fs[:, 0:1], AF.Sqrt, (fsb, CONSTb), (fsb,), bias=EPS_RMS, scale=1.0 / D)
                RECIP(fs[:, 2:3], fs[:, 1:2], (fsb,), (fsb,))
                for half in range(2):
                    STT(ost[:, half * 512:(half + 1) * 512], banks[b0 + half][:, :], fs[:, 2:3],
                        FG[:, half * 512:(half + 1) * 512], ALU.mult, ALU.mult, (BK[b0 + half], fsb, FGb), (ostb,))
            o = DMA("sp", out_d[c * 128:(c + 1) * 128, :], ost, (ostb,), (), ods)
            out_ops.append(o)

        nwait = SC.emit()
        SC.final_wait("sp", out_ops)
        print("ops", len(SC.ops), "waits", nwait)
    return nc


def _rope_tables():
    n_freq = 16
    inv = (10000.0 ** (-np.arange(n_freq, dtype=np.float32) / n_freq)).astype(np.float32)
    t = np.arange(S)
    pos = np.stack([t // 64, t % 64], axis=-1).astype(np.float32)
    ang = pos[:, :, None] * inv
    cos = np.cos(ang).astype(np.float32).reshape(S, 32).T
    sin = np.sin(ang).astype(np.float32).reshape(S, 32).T
    ropeC = np.tile(cos, (4, 1))
    ropeT = np.concatenate([sin, -sin, sin, -sin], axis=0)
    return np.ascontiguousarray(ropeC), np.ascontiguousarray(ropeT)


def _head_perm():
    perm = np.zeros(128, dtype=np.int64)
    for r in range(2):
        for half in range(2):
            for axis in range(2):
                for f in range(16):
                    perm[r * 64 + half * 32 + axis * 16 + f] = r * 64 + axis * 32 + half * 16 + f
    return perm


def _fm(v):
    return np.ascontiguousarray(v.reshape(-1, 8, 128).transpose(2, 0, 1).reshape(128, -1))


_PROG_CACHE = {}


def prepare_inputs(inputs):
    f = lambda a: np.ascontiguousarray(np.asarray(a, dtype=np.float32))
    x = f(inputs["x"])
    c = f(inputs["c"])
    ctx = f(inputs["ctx"])
    c_ctx = f(inputs["c_ctx"])
    perm = _head_perm()
    da_w_in = f(inputs["da_w_in"]).copy()
    cols = np.arange(3 * D)
    for blk in range(2):
        for h in range(8):
            base = blk * D + h * 128
            cols[base:base + 128] = base + perm
    da_w_in = np.ascontiguousarray(da_w_in[:, :, cols])
    ropeC, ropeT = _rope_tables()
    b_mod = f(inputs["b_mod"])
    b_modT = np.ascontiguousarray(b_mod.reshape(DEPTH, 72, 128).transpose(2, 0, 1).reshape(128, DEPTH * 72))
    norm_g = f(inputs["norm_g"])
    norm_gT = np.ascontiguousarray(norm_g.reshape(DEPTH, 3, 8, 128).transpose(3, 0, 1, 2).reshape(128, DEPTH * 24))
    da_lambda_b = np.ascontiguousarray(np.tile(f(inputs["da_lambda"]).reshape(1, 512), (128, 1)))
    da_subln_gT = np.ascontiguousarray(f(inputs["da_subln_g"]).T)
    sg_ln_gT = np.ascontiguousarray(f(inputs["sg_ln_g"]).reshape(2, 8, 128).transpose(2, 0, 1).reshape(128, 16))
    sg_ln_bT = np.ascontiguousarray(f(inputs["sg_ln_b"]).reshape(2, 8, 128).transpose(2, 0, 1).reshape(128, 16))
    sg_w_sT = np.ascontiguousarray(f(inputs["sg_w_s"]).transpose(0, 3, 1, 2).reshape(2, 128, 8 * 128))
    sg_b_s_b = np.ascontiguousarray(np.tile(f(inputs["sg_b_s"]).reshape(1, 2 * 1024), (128, 1)))
    final_g_b = np.ascontiguousarray(np.tile(f(inputs["final_g"]).reshape(1, D), (128, 1)))
    shared = {
        "w_mod": f(inputs["w_mod"]), "b_modT": b_modT, "norm_gT": norm_gT,
        "w_ffn_gu": f(inputs["w_ffn_gu"]), "w_ffn_down": f(inputs["w_ffn_down"]),
        "da_w_in": da_w_in, "da_w_out": f(inputs["da_w_out"]), "da_lambda_b": da_lambda_b,
        "da_subln_gT": da_subln_gT, "sg_w_in": f(inputs["sg_w_in"]), "sg_ln_gT": sg_ln_gT, "sg_ln_bT": sg_ln_bT,
        "sg_w_sT": sg_w_sT, "sg_b_s_b": sg_b_s_b, "sg_w_out": f(inputs["sg_w_out"]), "final_g_b": final_g_b,
        "ropeC": ropeC, "ropeT": ropeT,
    }
    in_maps = []
    for b in range(NCORES):
        cT = np.concatenate([c[b].reshape(8, 128).T, c_ctx.reshape(8, 128).T], axis=1)
        m = dict(shared)
        m["x"] = x[b]
        m["ctx"] = ctx[b]
        m["cT"] = np.ascontiguousarray(cT)
        in_maps.append(m)
    return in_maps


def kernel(**inputs):
    in_maps = prepare_inputs(inputs)
    if "full" not in _PROG_CACHE:
        _PROG_CACHE["full"] = build_program()
    nc = _PROG_CACHE["full"]
    res = run_bass_kernel_spmd(nc, in_maps, core_ids=list(range(NCORES)))
    return np.stack([np.asarray(r["out"]) for r in res.results], axis=0).astype(np.float32)
```

```python
import math
from contextlib import ExitStack
import numpy as np
import concourse.bass as bass
import concourse.mybir as mybir
from concourse.bass_utils import run_bass_kernel_spmd

F32 = mybir.dt.float32
BF16 = mybir.dt.bfloat16
AF = mybir.ActivationFunctionType
ALU = mybir.AluOpType
AX = mybir.AxisListType

D = 1024
S = 2048
C = 256
T = S + C
DFF = 2816
NFC = DFF // 128
DEPTH = 4
NMOD = 9
NCORES = 8
GROUPS = [(0, 512), (512, 512), (1024, 512), (1536, 512), (2048, 256)]
LAT = [0, 1, 2, 3]
CTXG = 4
RMS_EPS = 1e-6
LN_EPS = 1e-5
NF = 2
NPIECE = NFC // NF
ARENA_WORDS = 32900


class Buf:
    __slots__ = ("name", "lw", "rd")

    def __init__(self, name):
        self.name = name
        self.lw = None
        self.rd = {}


class Op:
    __slots__ = ("eng", "fn", "deps", "needs_inc", "sig", "dsem", "idx")


class Sched:
    def __init__(self, nc, stack):
        self.nc = nc
        self.stack = stack
        self.ops = []
        self.E = {"pe": nc.tensor, "act": nc.scalar, "dve": nc.vector, "pool": nc.gpsimd, "sp": nc.sync}
        self.esem = {e: stack.enter_context(nc.semaphore("es_" + e)) for e in ("pe", "act", "dve", "pool")}
        self.last = {}
        self.pending_bar = {}
        self.dsems = []
        self.free_ds = []
        self.phase_ds = []

    def new_dsem(self, name, persistent=False):
        if self.free_ds:
            s = self.free_ds.pop()
        else:
            s = [self.stack.enter_context(self.nc.semaphore("ds%d" % len(self.dsems))), 0, None]
            self.dsems.append(s)
        if not persistent:
            self.phase_ds.append(s)
        return s

    def op(self, eng, fn, reads=(), writes=(), dsem=None):
        o = Op()
        o.eng = eng
        o.fn = fn
        o.needs_inc = False
        o.dsem = dsem
        o.sig = None
        o.idx = len(self.ops)
        deps = {}
        for b in reads:
            if b.lw is not None:
                deps[b.lw.idx] = b.lw
        for b in writes:
            if b.lw is not None:
                deps[b.lw.idx] = b.lw
            for r in b.rd.values():
                deps[r.idx] = r
        if eng in self.pending_bar:
            for d in self.pending_bar.pop(eng):
                deps[d.idx] = d
        dl = []
        for d in deps.values():
            if d.eng == "pe" and eng == "pe" and d.dsem is None:
                continue
            d.needs_inc = True
            dl.append(d)
        o.deps = dl
        key = eng if dsem is None else ("d", id(dsem))
        for b in reads:
            b.rd[key] = o
        for b in writes:
            b.lw = o
            b.rd = {}
        if dsem is not None:
            dsem[1] += 16
            o.sig = (dsem[0], dsem[1])
            dsem[2] = o
        else:
            self.last[eng] = o
        self.ops.append(o)
        return o

    def barrier(self):
        deps = list(self.last.values()) + [d[2] for d in self.dsems if d[2] is not None]
        for d in deps:
            d.needs_inc = True
        for e in self.E:
            self.pending_bar[e] = list(deps)
        self.free_ds.extend(self.phase_ds)
        self.phase_ds = []

    def emit(self):
        cnt = {e: 0 for e in self.esem}
        for o in self.ops:
            if o.dsem is None and o.needs_inc:
                cnt[o.eng] += 1
                o.sig = (self.esem[o.eng], cnt[o.eng])
        seen = {e: {} for e in self.E}
        nwait = 0
        for o in self.ops:
            need = {}
            for d in o.deps:
                sem, val = d.sig
                k = id(sem)
                if k not in need or need[k][1] < val:
                    need[k] = (sem, val)
            eng = self.E[o.eng]
            sn = seen[o.eng]
            for k, (sem, val) in need.items():
                if sn.get(k, 0) < val:
                    eng.wait_ge(sem, val)
                    sn[k] = val
                    nwait += 1
            ins = o.fn()
            if o.dsem is not None:
                ins.then_inc(o.sig[0], 16)
            elif o.needs_inc:
                ins.then_inc(o.sig[0], 1)
        return nwait

    def final_wait(self, eng, ops):
        e = self.E[eng]
        need = {}
        for d in ops:
            sem, val = d.sig
            if id(sem) not in need or need[id(sem)][1] < val:
                need[id(sem)] = (sem, val)
        for sem, val in need.values():
            e.wait_ge(sem, val)


class Arena:
    def __init__(self, ap, words):
        self.ap = ap
        self.words = words
        self.off = 0

    def reset(self):
        self.off = 0

    def alloc(self, shape, dtype):
        n = 1
        for s in shape:
            n *= s
        w = n if dtype == F32 else (n + 1) // 2
        assert self.off + w <= self.words, ("arena overflow", self.off, w, self.words)
        a = self.ap[:, self.off:self.off + w]
        self.off += w
        if dtype != F32:
            a = a.bitcast(dtype)
        if len(shape) == 2:
            a = a.rearrange("p (a b) -> p a b", a=shape[0])
        elif len(shape) == 3:
            a = a.rearrange("p (a b c) -> p a b c", a=shape[0], b=shape[1])
        return a


def lam_init_of(i):
    return 0.8 - 0.6 * math.exp(-0.3 * i)


def build_program(nlayers=DEPTH, debug_dump=False, stop_stage=None, dbg_attn=False):
    nc = bass.Bass("TRN2", target_bir_lowering=False)
    dt_in = lambda name, shape: nc.dram_tensor(name, list(shape), F32, kind="ExternalInput").ap()
    x_d = dt_in("x", (S, D))
    ctx_d = dt_in("ctx", (C, D))
    cT_d = dt_in("cT", (128, 16))
    wmod_d = dt_in("w_mod", (DEPTH, D, NMOD * D))
    bmodT_d = dt_in("b_modT", (128, DEPTH * 72))
    ngT_d = dt_in("norm_gT", (128, DEPTH * 24))
    wgu_d = dt_in("w_ffn_gu", (DEPTH, 2, D, 2 * DFF))
    wdn_d = dt_in("w_ffn_down", (DEPTH, 2, DFF, D))
    dawin_d = dt_in("da_w_in", (2, D, 3 * D))
    dawout_d = dt_in("da_w_out", (2, D, D))
    dalam_d = dt_in("da_lambda_b", (128, 512))
    dasg_d = dt_in("da_subln_gT", (128, 2))
    sgwin_d = dt_in("sg_w_in", (2, D, 2 * D))
    sglng_d = dt_in("sg_ln_gT", (128, 16))
    sglnb_d = dt_in("sg_ln_bT", (128, 16))
    sgws_d = dt_in("sg_w_sT", (2, 128, 8 * 128))
    sgbs_d = dt_in("sg_b_s_b", (128, 2 * 1024))
    sgwout_d = dt_in("sg_w_out", (2, D, D))
    fg_d = dt_in("final_g_b", (128, D))
    ropeC_d = dt_in("ropeC", (128, S))
    ropeT_d = dt_in("ropeT", (128, S))
    nrows_out = T if debug_dump else S
    out_d = nc.dram_tensor("out", [nrows_out, D], F32, kind="ExternalOutput").ap()

    stack = ExitStack()
    with stack:
        sb = lambda name, shape, dt: stack.enter_context(nc.sbuf_tensor(name, list(shape), dt))
        XT = sb("XT", (128, 8, T), F32)
        MODT = sb("MODT", (128, DEPTH * 72 * 2), F32)
        SMALL = sb("SMALL", (128, 16 + 288 + 96 + 2 + 16 + 16), F32)
        PRM = sb("PRM", (128, 12 * 48), F32)
        CONST = sb("CONST", (128, 8), F32)
        ONES = sb("ONES", (128, 128), BF16)
        IDENT = sb("IDENT", (128, 128), F32)
        SCT = sb("SCT", (128, 16), BF16)
        ARENA_T = sb("ARENA", (128, ARENA_WORDS), F32)
        banks = [stack.enter_context(nc.psum_tensor("bk%d" % i, [128, 512], F32)) for i in range(8)]
        BK = [Buf("bk%d" % i) for i in range(8)]
        SC = Sched(nc, stack)
        AR = Arena(ARENA_T, ARENA_WORDS)

        def MM(out, lhsT, rhs, start, stop, reads, writes):
            return SC.op("pe", lambda: nc.tensor.matmul(out, lhsT, rhs, start=start, stop=stop), reads, writes)

        def TR(out, in_, reads, writes):
            return SC.op("pe", lambda: nc.tensor.transpose(out, in_, IDENT[:]), reads, writes)

        def ACT(out, in_, func, reads, writes, bias=None, scale=None):
            kw = {}
            if bias is not None:
                kw["bias"] = bias
            if scale is not None:
                kw["scale"] = scale
            return SC.op("act", lambda: nc.scalar.activation(out, in_, func, **kw), reads, writes)

        def TT(out, in0, in1, op, reads, writes, eng="dve"):
            e = SC.E[eng]
            return SC.op(eng, lambda: e.tensor_tensor(out, in0, in1, op), reads, writes)

        def TS(out, in0, s1, op0, reads, writes, s2=None, op1=None, eng="dve"):
            e = SC.E[eng]
            if op1 is None:
                return SC.op(eng, lambda: e.tensor_scalar(out, in0, s1, None, op0), reads, writes)
            return SC.op(eng, lambda: e.tensor_scalar(out, in0, s1, s2, op0, op1), reads, writes)

        def STT(out, in0, scalar, in1, op0, op1, reads, writes, eng="dve"):
            e = SC.E[eng]
            return SC.op(eng, lambda: e.scalar_tensor_tensor(out, in0, scalar, in1, op0, op1), reads, writes)

        def RECIP(out, in_, reads, writes):
            return SC.op("dve", lambda: nc.vector.reciprocal(out, in_), reads, writes)

        def RSUM(out, in_, reads, writes):
            return SC.op("dve", lambda: nc.vector.reduce_sum(out, in_, AX.X), reads, writes)

        def COPY(out, in_, reads, writes, eng="dve"):
            if eng == "act":
                return SC.op("act", lambda: nc.scalar.copy(out, in_), reads, writes)
            e = SC.E[eng]
            return SC.op(eng, lambda: e.tensor_copy(out, in_), reads, writes)

        def DMA(queue, out, in_, reads, writes, dsem):
            e = SC.E[queue]
            return SC.op(queue, lambda: e.dma_start(out=out, in_=in_), reads, writes, dsem=dsem)

        class Rot:
            def __init__(self, name, n, shape, dtype, dma=False):
                self.t = [AR.alloc(shape, dtype) for _ in range(n)]
                self.b = [Buf("%s%d" % (name, i)) for i in range(n)]
                self.ds = [SC.new_dsem("%s%d" % (name, i)) for i in range(n)] if dma else None
                self.i = -1
                self.n = n

            def next(self):
                self.i = (self.i + 1) % self.n
                if self.ds:
                    return self.t[self.i], self.b[self.i], self.ds[self.i]
                return self.t[self.i], self.b[self.i]

        XTb = [[Buf("xt%d_%d" % (d, g)) for g in range(5)] for d in range(8)]
        MODb = [Buf("mod%d" % i) for i in range(DEPTH)]
        SMALLb = Buf("small")
        CONSTb = Buf("const")
        LAMb = Buf("lam")
        ONESb = Buf("ones")
        IDENTb = Buf("ident")
        SCTb = Buf("sct")
        PRMb = [Buf("prm%d" % i) for i in range(12)]
        small_ds = SC.new_dsem("small")

        cT = SMALL[:, 0:16]
        bmodT = SMALL[:, 16:304]
        ngT = SMALL[:, 304:400]
        dasg = SMALL[:, 400:402]
        sglng = SMALL[:, 402:418]
        sglnb = SMALL[:, 418:434]
        for dst, src in ((cT, cT_d), (bmodT, bmodT_d), (ngT, ngT_d), (dasg, dasg_d), (sglng, sglng_d), (sglnb, sglnb_d)):
            DMA("sp", dst, src[:, :], (), (SMALLb,), small_ds)
        LAMB = AR.alloc((512,), F32)
        LTMP = AR.alloc((160,), F32)
        lam_ds = SC.new_dsem("lam")
        DMA("sp", LAMB, dalam_d[:, :], (), (LAMb,), lam_ds)

        SC.op("dve", lambda: nc.vector.memset(CONST[:, 0:1], RMS_EPS), (), (CONSTb,))
        SC.op("dve", lambda: nc.vector.memset(CONST[:, 1:2], LN_EPS), (), (CONSTb,))
        SC.op("dve", lambda: nc.vector.memset(ONES[:], 1.0), (), (ONESb,))
        SC.op("pool", lambda: nc.gpsimd.memset(IDENT[:], 0.0), (), (IDENTb,))
        SC.op("pool", lambda: nc.gpsimd.affine_select(out=IDENT[:], in_=IDENT[:], pattern=[[-1, 128]],
                                                      compare_op=ALU.not_equal, fill=1.0, base=0,
                                                      channel_multiplier=1), (), (IDENTb,))
        EPS_RMS = CONST[:, 0:1]
        EPS_LN = CONST[:, 1:2]

        for j in range(2):
            li = lam_init_of(2 * j)
            lb = LAMB[:, j * 256:(j + 1) * 256]
            TT(LTMP[:, 0:64], lb[:, 0:64], lb[:, 64:128], ALU.mult, (LAMb,), (LAMb,))
            TT(LTMP[:, 64:128], lb[:, 128:192], lb[:, 192:256], ALU.mult, (LAMb,), (LAMb,))
            RSUM(LTMP[:, 128:129], LTMP[:, 0:64], (LAMb,), (LAMb,))
            RSUM(LTMP[:, 129:130], LTMP[:, 64:128], (LAMb,), (LAMb,))
            ACT(LTMP[:, 130:132], LTMP[:, 128:130], AF.Exp, (LAMb,), (LAMb,))
            TT(LTMP[:, 132:133], LTMP[:, 131:132], LTMP[:, 130:131], ALU.subtract, (LAMb,), (LAMb,))
            TS(CONST[:, 4 + j:5 + j], LTMP[:, 132:133], -li, ALU.add, (LAMb,), (CONSTb,))
            TS(CONST[:, 6 + j:7 + j], dasg[:, j:j + 1], 1.0 - li, ALU.mult, (SMALLb,), (CONSTb,))

        stage = Rot("stage", 2, (D,), F32, dma=True)
        for c in range(T // 128):
            st, stb, sds = stage.next()
            src = x_d[c * 128:(c + 1) * 128, :] if c < 16 else ctx_d[(c - 16) * 128:(c - 15) * 128, :]
            DMA("sp", st, src, (), (stb,), sds)
            g = min(c // 4, 4)
            for half in range(2):
                bk = (c % 2) * 2 + half
                for dd in range(4):
                    d = half * 4 + dd
                    TR(banks[bk][:, dd * 128:(dd + 1) * 128], st[:, d * 128:(d + 1) * 128], (stb, IDENTb), (BK[bk],))
                COPY(XT[:, half * 4:half * 4 + 4, c * 128:(c + 1) * 128],
                     banks[bk][:, :].rearrange("p (a b) -> p a b", a=4),
                     (BK[bk],), [XTb[half * 4 + dd][g] for dd in range(4)], eng=("act" if half else "dve"))

        ACT(SCT[:, :].rearrange("p (k c) -> p k c", c=2), cT.rearrange("p (c k) -> p k c", c=2), AF.Silu,
            (SMALLb,), (SCTb,))
        def mod_jobs(i, wm, mbk):
            jobs = []

            def piece(q):
                wt, wb, wds = wm.next()
                DMA("pool", wt, wmod_d[i].rearrange("(k p) f -> p k f", p=128)[:, :, q * 512:(q + 1) * 512],
                    (), (wb,), wds)
                for m in range(4):
                    mm_ = q * 4 + m
                    for k in range(8):
                        MM(banks[mbk][:, mm_ * 2:mm_ * 2 + 2], wt[:, k, m * 128:(m + 1) * 128],
                           SCT[:, k * 2:k * 2 + 2], k == 0, k == 7, (wb, SCTb), (BK[mbk],))

            def fin():
                for col in range(2):
                    TT(MODT[:, i * 144:(i + 1) * 144].rearrange("p (m c) -> p m c", c=2)[:, :, col],
                       banks[mbk][:, 0:144].rearrange("p (m c) -> p m c", c=2)[:, :, col],
                       bmodT[:, i * 72:(i + 1) * 72], ALU.add, (BK[mbk], SMALLb), (MODb[i],))

            for q in range(18):
                jobs.append(lambda q=q: piece(q))
            jobs.append(fin)
            return jobs

        wm0 = Rot("wm", 3, (8, 512), BF16, dma=True)
        for job in mod_jobs(0, wm0, 4):
            job()

        def mod_ap(i, kmod, col):
            return MODT[:, i * 144 + kmod * 16:i * 144 + kmod * 16 + 16].rearrange("p (d c) -> p d c", c=2)[:, :, col]

        def prep_params(i, s, gate_mul):
            slot = i * 3 + s
            base = slot * 48
            for col in range(2):
                A = PRM[:, base + col * 8:base + col * 8 + 8]
                SH = PRM[:, base + 16 + col * 8:base + 16 + col * 8 + 8]
                G = PRM[:, base + 32 + col * 8:base + 32 + col * 8 + 8]
                STT(A, mod_ap(i, 3 * s + 1, col), 1.0, ngT[:, i * 24 + s * 8:i * 24 + s * 8 + 8], ALU.add, ALU.mult,
                    (MODb[i], SMALLb), (PRMb[slot],))
                COPY(SH, mod_ap(i, 3 * s, col), (MODb[i],), (PRMb[slot],))
                TS(G, mod_ap(i, 3 * s + 2, col), gate_mul, ALU.mult, (MODb[i],), (PRMb[slot],))
            return slot

        def prm(slot, which, g, d):
            col = 1 if g == CTXG else 0
            o = slot * 48 + which * 16 + col * 8 + d
            return PRM[:, o:o + 1]

        def prenorm(groups, slot, XN, XNb, ssbanks, FR):
            sqw = AR.alloc((2048,), F32)
            sq = sqw.bitcast(BF16).rearrange("p (a b) -> p a b", a=8)
            sqq = [(sqw[:, q * 512:(q + 1) * 512], Buf("sqq%d" % q)) for q in range(4)]
            RS = Rot("rs", 1, (512,), F32)
            TMP = FR
            for n, g in enumerate(groups):
                g0, gs = GROUPS[g]
                bk = ssbanks[n % len(ssbanks)]
                for d in range(8):
                    ACT(sq[:, d, 0:gs], XT[:, d, g0:g0 + gs], AF.Square, (XTb[d][g],), (sqq[d // 2][1],))
                for d in range(8):
                    MM(banks[bk][:, 0:gs], ONES[:], sq[:, d, 0:gs], d == 0, d == 7, (ONESb, sqq[d // 2][1]),
                       (BK[bk],))
                rs, rsb = RS.next()
                ACT(rs[:, 0:gs], banks[bk][:, 0:gs], AF.Ln, (BK[bk], CONSTb), (rsb,), bias=EPS_RMS, scale=1.0 / D)
                ACT(rs[:, 0:gs], rs[:, 0:gs], AF.Exp, (rsb,), (rsb,), scale=-0.5)
                for d in range(8):
                    tp, tpb = TMP.next()
                    STT(tp[:, 0:gs], XT[:, d, g0:g0 + gs], prm(slot, 0, g, d), rs[:, 0:gs], ALU.mult, ALU.mult,
                        (XTb[d][g], PRMb[slot], rsb), (tpb,))
                    ACT(XN[:, d, g0:g0 + gs], tp[:, 0:gs], AF.Identity, (tpb, PRMb[slot]), (XNb[d][g],),
                        bias=prm(slot, 1, g, d))
            return sqq

        def ffn_phase(i, which, groups):
            SC.barrier()
            AR.reset()
            slot = prep_params(i, 0 if which == 0 else 2, 0.5)
            XN = AR.alloc((8, T), BF16)
            XNb = [[Buf("xn%d_%d" % (d, g)) for g in range(5)] for d in range(8)]
            WG = Rot("wg", 3, (8, NF * 128), BF16, dma=True)
            WU = Rot("wu", 3, (8, NF * 128), BF16, dma=True)
            WD = Rot("wd", 3, (NF, D), BF16, dma=True)
            FR = Rot("fr", 8, (512,), F32)
            SG = FR
            HT = Rot("ht", 4, (512,), BF16)
            wguv = wgu_d[i, which].rearrange("(k p) f -> p k f", p=128)
            wdnv = wdn_d[i, which].rearrange("(f p) d -> p f d", p=128)
            loaded = {}

            def load_piece(p):
                f0 = p * NF * 128
                wg = WG.next()
                wu = WU.next()
                wd = WD.next()
                DMA("pool", wg[0], wguv[:, :, f0:f0 + NF * 128], (), (wg[1],), wg[2])
                DMA("pool", wu[0], wguv[:, :, DFF + f0:DFF + f0 + NF * 128], (), (wu[1],), wu[2])
                DMA("pool", wd[0], wdnv[:, p * NF:(p + 1) * NF, :], (), (wd[1],), wd[2])
                loaded[p] = (wg, wu, wd)

            load_piece(0)
            load_piece(1)
            prenorm(groups, slot, XN, XNb, [7], FR)
            items = [(p, g) for p in range(NPIECE) for g in groups]
            ybanks = [4, 5, 6]
            state = {"y": 0}

            def emit_gu(p, g):
                g0, gs = GROUPS[g]
                wg, wu, wd = loaded[p]
                hts = []
                for fi in range(NF):
                    bg, bu = (0, 1) if fi % 2 == 0 else (2, 3)
                    for k in range(8):
                        MM(banks[bg][:, 0:gs], wg[0][:, k, fi * 128:(fi + 1) * 128], XN[:, k, g0:g0 + gs], k == 0,
                           k == 7, (wg[1], XNb[k][g]), (BK[bg],))
                    for k in range(8):
                        MM(banks[bu][:, 0:gs], wu[0][:, k, fi * 128:(fi + 1) * 128], XN[:, k, g0:g0 + gs], k == 0,
                           k == 7, (wu[1], XNb[k][g]), (BK[bu],))
                    sg, sgb = SG.next()
                    ht, htb = HT.next()
                    ACT(sg[:, 0:gs], banks[bg][:, 0:gs], AF.Silu, (BK[bg],), (sgb,))
                    TT(ht[:, 0:gs], sg[:, 0:gs], banks[bu][:, 0:gs], ALU.mult, (sgb, BK[bu]), (htb,))
                    hts.append((ht, htb))
                return hts

            def emit_down(p, g, hts):
                g0, gs = GROUPS[g]
                wg, wu, wd = loaded[p]
                for d in range(8):
                    yb = ybanks[state["y"] % 3]
                    state["y"] += 1
                    for fi in range(NF):
                        MM(banks[yb][:, 0:gs], wd[0][:, fi, d * 128:(d + 1) * 128], hts[fi][0][:, 0:gs], fi == 0,
                           fi == NF - 1, (wd[1], hts[fi][1]), (BK[yb],))
                    STT(XT[:, d, g0:g0 + gs], banks[yb][:, 0:gs], prm(slot, 2, g, d), XT[:, d, g0:g0 + gs], ALU.mult,
                        ALU.add, (BK[yb], PRMb[slot], XTb[d][g]), (XTb[d][g],))

            mjobs = []
            if which == 0 and i + 1 < nlayers:
                mjobs = mod_jobs(i + 1, Rot("wmx", 2, (8, 512), BF16, dma=True), 7)
            prev = None
            for n, (p, g) in enumerate(items):
                hts = emit_gu(p, g)
                if prev is not None:
                    emit_down(*prev)
                if g == groups[0] and p + 2 < NPIECE:
                    load_piece(p + 2)
                prev = (p, g, hts)
                if mjobs and n % 2 == 1:
                    mjobs.pop(0)()
            emit_down(*prev)
            while mjobs:
                mjobs.pop(0)()

        def attn_phase(i, j, qgroups, kvgroups):
            SC.barrier()
            AR.reset()
            slot = prep_params(i, 1, 1.0)
            XN = AR.alloc((8, T), BF16)
            XNb = [[Buf("xn%d_%d" % (d, g)) for g in range(5)] for d in range(8)]
            ROPC = AR.alloc((S,), F32)
            ROPT = AR.alloc((S,), F32)
            ROPb = Buf("rope")
            rds = SC.new_dsem("rope%d" % i)
            DMA("sp", ROPC, ropeC_d[:, :], (), (ROPb,), rds)
            DMA("sp", ROPT, ropeT_d[:, :], (), (ROPb,), rds)
            WQ = Rot("wq", 2, (8, 128), BF16, dma=True)
            WK = Rot("wk", 2, (8, 128), BF16, dma=True)
            WV = Rot("wv", 2, (8, 128), BF16, dma=True)
            WO = Rot("wo", 3, (D,), BF16, dma=True)
            QTA = AR.alloc((T,), BF16)
            QTB = AR.alloc((T,), BF16)
            QTb = [Buf("qt%d" % g) for g in range(5)]
            SC.op("dve", lambda: nc.vector.memset(QTA[64:128, :], 0.0), (), QTb)
            SC.op("dve", lambda: nc.vector.memset(QTB[0:64, :], 0.0), (), QTb)
            KT = Rot("kt", 2, (T,), BF16)
            VV = Rot("vv", 2, (18, 128), BF16)
            PP = [Rot("pp%d" % r, 2, (512,), BF16) for r in range(2)]
            FR = Rot("fr", 2, (512,), F32)
            NR = Rot("nr", 4, (512,), F32)
            HR = Rot("hr", 5, (512,), BF16)
            OSQ = ON = HR
            winv = dawin_d[j].rearrange("(k p) f -> p k f", p=128)
            woutv = dawout_d[j].rearrange("(h p) d -> p h d", p=128)
            NLAM = CONST[:, 4 + j:5 + j]
            SGC = CONST[:, 6 + j:7 + j]
            misc = [6, 7]
            mstate = {"m": 0}

            def mbank():
                b = misc[mstate["m"] % 2]
                mstate["m"] += 1
                return b

            sqq = prenorm(kvgroups, slot, XN, XNb, [6, 7], FR)

            class Ring:
                def __init__(self, items):
                    self.items = items
                    self.i = -1

                def next(self):
                    self.i = (self.i + 1) % len(self.items)
                    return self.items[self.i]

            RR = Ring(list(zip(FR.t, FR.b)) + sqq)
            hw = {}

            def load_head(h):
                wq = WQ.next()
                wk = WK.next()
                wv = WV.next()
                wo = WO.next()
                DMA("pool", wq[0], winv[:, :, h * 128:(h + 1) * 128], (), (wq[1],), wq[2])
                DMA("pool", wk[0], winv[:, :, D + h * 128:D + (h + 1) * 128], (), (wk[1],), wk[2])
                DMA("pool", wv[0], winv[:, :, 2 * D + h * 128:2 * D + (h + 1) * 128], (), (wv[1],), wv[2])
                DMA("pool", wo[0], woutv[:, h, :], (), (wo[1],), wo[2])
                hw[h] = (wq, wk, wv, wo)

            def rope_parts(bk, g0, gs):
                qs, qsb = RR.next()
                COPY(qs[:, 0:gs], banks[bk][:, 0:gs], (BK[bk],), (qsb,), eng="act")
                rb, rbb = RR.next()
                for blk in range(4):
                    sp_ = blk ^ 1
                    TT(rb[blk * 32:(blk + 1) * 32, 0:gs], qs[sp_ * 32:(sp_ + 1) * 32, 0:gs],
                       ROPT[sp_ * 32:(sp_ + 1) * 32, g0:g0 + gs], ALU.mult, (qsb, ROPb), (rbb,))
                TT(qs[:, 0:gs], qs[:, 0:gs], ROPC[:, g0:g0 + gs], ALU.mult, (qsb, ROPb), (qsb,))
                return qs, qsb, rb, rbb

            proj = {}

            def project_q_g(h, g):
                wq = hw[h][0]
                g0, gs = GROUPS[g]
                bk = mbank()
                for k in range(8):
                    MM(banks[bk][:, 0:gs], wq[0][:, k, :], XN[:, k, g0:g0 + gs], k == 0, k == 7,
                       (wq[1], XNb[k][g]), (BK[bk],))
                if g == CTXG:
                    COPY(QTA[0:64, g0:g0 + gs], banks[bk][0:64, 0:gs], (BK[bk],), (QTb[g],), eng="act")
                    COPY(QTB[64:128, g0:g0 + gs], banks[bk][64:128, 0:gs], (BK[bk],), (QTb[g],), eng="act")
                else:
                    ra, rab, rb, rbb = rope_parts(bk, g0, gs)
                    TT(QTA[0:64, g0:g0 + gs], ra[0:64, 0:gs], rb[0:64, 0:gs], ALU.add, (rab, rbb), (QTb[g],))
                    TT(QTB[64:128, g0:g0 + gs], ra[64:128, 0:gs], rb[64:128, 0:gs], ALU.add, (rab, rbb),
                       (QTb[g],))

            def kv_begin(h):
                kt, ktb = KT.next()
                vv, vvb = VV.next()
                proj[h] = (kt, ktb, vv, vvb)

            def project_kv_g(h, g):
                wq, wk, wv, wo = hw[h]
                kt, ktb, vv, vvb = proj[h]
                g0, gs = GROUPS[g]
                bk = mbank()
                for k in range(8):
                    MM(banks[bk][:, 0:gs], wk[0][:, k, :], XN[:, k, g0:g0 + gs], k == 0, k == 7,
                       (wk[1], XNb[k][g]), (BK[bk],))
                if g == CTXG:
                    COPY(kt[:, g0:g0 + gs], banks[bk][:, 0:gs], (BK[bk],), (ktb,), eng="act")
                else:
                    ra, rab, rb, rbb = rope_parts(bk, g0, gs)
                    TT(kt[:, g0:g0 + gs], ra[:, 0:gs], rb[:, 0:gs], ALU.add, (rab, rbb), (ktb,))
                bk = mbank()
                nch = gs // 128
                for cc in range(nch):
                    t0 = g0 + cc * 128
                    for k in range(8):
                        MM(banks[bk][:, cc * 128:(cc + 1) * 128], XN[:, k, t0:t0 + 128], wv[0][:, k, :], k == 0,
                           k == 7, (wv[1], XNb[k][g]), (BK[bk],))
                COPY(vv[:, g0 // 128:g0 // 128 + nch, :], banks[bk][:, 0:gs].rearrange("p (a b) -> p a b", a=nch),
                     (BK[bk],), (vvb,), eng="act")

            def key_loop(h, g, inject=()):
                inject = list(inject)
                kt, ktb, vv, vvb = proj[h]
                g0, gs = GROUPS[g]
                chunks = list(range(18)) if g != CTXG else [16, 17]
                pend = None
                nk = len(chunks)

                def pv1(r, ci, c, pt, ptb):
                    MM(banks[2 + r][:, 0:gs], vv[:, c, :], pt[:, 0:gs], ci == 0, ci == nk - 1, (vvb, ptb),
                       (BK[2 + r],))
                    MM(banks[4 + r][:, 0:gs], ONES[:], pt[:, 0:gs], ci == 0, ci == nk - 1, (ONESb, ptb),
                       (BK[4 + r],))

                for ci, c in enumerate(chunks):
                    ps = []
                    for r in range(2):
                        MM(banks[r][:, 0:gs], kt[:, c * 128:(c + 1) * 128], (QTA, QTB)[r][:, g0:g0 + gs], True, True,
                           (ktb, QTb[g]), (BK[r],))
                        pt, ptb = PP[r].next()
                        ACT(pt[:, 0:gs], banks[r][:, 0:gs], AF.Exp, (BK[r],), (ptb,), scale=0.125)
                        ps.append((pt, ptb))
                        if pend is not None:
                            pv1(r, pend[0], pend[1], *pend[2 + r])
                    pend = (ci, c, ps[0], ps[1])
                    while inject and inject[0][0] <= ci:
                        inject.pop(0)[1]()
                for r in range(2):
                    pv1(r, pend[0], pend[1], *pend[2 + r])
                while inject:
                    inject.pop(0)[1]()

            def norm_chain(h, g):
                g0, gs = GROUPS[g]
                r1, r1b = NR.next()
                r2, r2b = NR.next()
                t1, t1b = NR.next()
                t2, t2b = NR.next()
                ACT(r1[:, 0:gs], banks[4][:, 0:gs], AF.Ln, (BK[4],), (r1b,))
                COPY(t1[:, 0:gs], banks[2][:, 0:gs], (BK[2],), (t1b,))
                ACT(r2[:, 0:gs], banks[5][:, 0:gs], AF.Ln, (BK[5],), (r2b,))
                COPY(t2[:, 0:gs], banks[3][:, 0:gs], (BK[3],), (t2b,))
                ACT(r1[:, 0:gs], r1[:, 0:gs], AF.Exp, (r1b,), (r1b,), scale=-1.0)
                ACT(r2[:, 0:gs], r2[:, 0:gs], AF.Exp, (r2b,), (r2b,), scale=-1.0)
                TT(t1[:, 0:gs], t1[:, 0:gs], r1[:, 0:gs], ALU.mult, (t1b, r1b), (t1b,))
                STT(t2[:, 0:gs], t2[:, 0:gs], NLAM, r2[:, 0:gs], ALU.mult, ALU.mult, (t2b, CONSTb, r2b), (t2b,))
                oo, oob = r1, r1b
                TT(oo[:, 0:gs], t1[:, 0:gs], t2[:, 0:gs], ALU.add, (t1b, t2b), (oob,))
                osq, osqb = OSQ.next()
                TT(osq[:, 0:gs], oo[:, 0:gs], oo[:, 0:gs], ALU.mult, (oob,), (osqb,))
                return (g, gs, oo, oob, osq, osqb, r2, r2b)

            def norm_chain_b(st):
                g, gs, oo, oob, osq, osqb, r2, r2b = st
                bk = mbank()
                MM(banks[bk][:, 0:gs], ONES[:], osq[:, 0:gs], True, True, (ONESb, osqb), (BK[bk],))
                zi, zib = r2, r2b
                ACT(zi[:, 0:gs], banks[bk][:, 0:gs], AF.Ln, (BK[bk], CONSTb), (zib,), bias=EPS_RMS, scale=1.0 / 128)
                ACT(zi[:, 0:gs], zi[:, 0:gs], AF.Exp, (zib,), (zib,), scale=-0.5)
                on, onb = ON.next()
                STT(on[:, 0:gs], oo[:, 0:gs], SGC, zi[:, 0:gs], ALU.mult, ALU.mult, (oob, CONSTb, zib), (onb,))
                return on, onb

            def out_proj(h, g, on, onb, ds=range(8)):
                if dbg_attn:
                    return
                g0, gs = GROUPS[g]
                wo = hw[h][3]
                for d in ds:
                    bk = mbank()
                    MM(banks[bk][:, 0:gs], wo[0][:, d * 128:(d + 1) * 128], on[:, 0:gs], True, True, (wo[1], onb),
                       (BK[bk],))
                    STT(XT[:, d, g0:g0 + gs], banks[bk][:, 0:gs], prm(slot, 2, g, d), XT[:, d, g0:g0 + gs], ALU.mult,
                        ALU.add, (BK[bk], PRMb[slot], XTb[d][g]), (XTb[d][g],))

            load_head(0)
            load_head(1)
            kv_begin(0)
            for g in kvgroups:
                project_kv_g(0, g)
            prev = None
            box = {}

            def part_b(p):
                box["on"] = norm_chain_b(p[2])

            def part_c(p, ds=range(8)):
                out_proj(p[0], p[1], *box["on"], ds=ds)

            items = [(h, g) for h in range(8) for g in qgroups]
            project_q_g(*items[0])
            nq = len(qgroups)
            for h in range(8):
                for gi, g in enumerate(qgroups):
                    n_it = h * nq + gi
                    if n_it + 1 < len(items):
                        project_q_g(*items[n_it + 1])
                    if h + 1 < 8:
                        if gi == 0:
                            kv_begin(h + 1)
                        share = [kvgroups[gi]] + (kvgroups[nq:] if gi == nq - 1 else [])
                        for kg in share:
                            project_kv_g(h + 1, kg)
                    inj = []
                    if prev is not None:
                        inj = [(3, (lambda p=prev: part_b(p)))]
                        inj += [(8 + d, (lambda p=prev, d=d: part_c(p, (d,)))) for d in range(8)]
                    key_loop(h, g, inj)
                    st = norm_chain(h, g)
                    prev = (h, g, st)
                if h + 2 < 8:
                    load_head(h + 2)
            part_b(prev)
            part_c(prev)

        def sgmlp_phase(i, j, groups):
            SC.barrier()
            AR.reset()
            slot = prep_params(i, 1, 1.0)
            XN = AR.alloc((8, T), BF16)
            XNb = [[Buf("xn%d_%d" % (d, g)) for g in range(5)] for d in range(8)]
            WIN = AR.alloc((8, 2 * D), BF16)
            WOUT = AR.alloc((8, D), BF16)
            WST = AR.alloc((8, 128), BF16)
            BIAS = AR.alloc((8, 128), F32)
            WINb, WOUTb, WSTb, BIASb = Buf("win"), Buf("wout"), Buf("wst"), Buf("bias")
            ds1, ds2, ds3, ds4 = (SC.new_dsem("sg%d_%d" % (i, n)) for n in range(4))
            winv = sgwin_d[j].rearrange("(k p) f -> p k f", p=128)
            DMA("pool", WIN[:, :, 0:D], winv[:, :, 0:D], (), (WINb,), ds1)
            DMA("pool", WIN[:, :, D:2 * D], winv[:, :, D:2 * D], (), (WINb,), ds1)
            DMA("pool", WOUT, sgwout_d[j].rearrange("(k p) f -> p k f", p=128), (), (WOUTb,), ds2)
            DMA("pool", WST, sgws_d[j].rearrange("p (g t) -> p g t", g=8), (), (WSTb,), ds3)
            DMA("sp", BIAS, sgbs_d[:, j * 1024:(j + 1) * 1024].rearrange("p (g t) -> p g t", g=8), (), (BIASb,), ds4)
            FR = Rot("fr", 2, (512,), F32)
            prenorm(groups, slot, XN, XNb, [7], FR)
            for half in range(2):
                MM(banks[half][:, :], ONES[:], WST[:, half * 4:half * 4 + 4, :].rearrange("p a b -> p (a b)"), True,
                   True, (ONESb, WSTb), (BK[half],))
            for gq in range(8):
                STT(BIAS[:, gq, :], banks[gq // 4][:, (gq % 4) * 128:(gq % 4 + 1) * 128],
                    sglnb[:, j * 8 + gq:j * 8 + gq + 1], BIAS[:, gq, :], ALU.mult, ALU.add,
                    (BK[gq // 4], SMALLb, BIASb), (BIASb,))
            SUB = 256
            UT = Rot("ut", 1, (8, SUB), BF16)
            GT = Rot("gt", 1, (8, SUB), BF16)
            VG = Rot("vg", 2, (D,), F32)
            VSQ = Rot("vsq", 1, (D,), BF16)
            VH = Rot("vh", 2, (D,), BF16)
            ST = Rot("st", 4, (16,), F32)
            MT = Rot("mt", 3, (128,), F32)
            pairs = [(2, 3), (4, 5), (6, 7)]
            subs = []
            for g in groups:
                gg0, ggs = GROUPS[g]
                for sub in range(ggs // SUB):
                    subs.append((g, gg0 + sub * SUB))
            cstate = {"t": 0}

            def emit_v(g, g0):
                st, stb = ST.next()
                chunks = []
                for cpos in range(SUB // 128):
                    t = cstate["t"]
                    cstate["t"] += 1
                    vb = pairs[((t // 2) + cpos) % 3]
                    t0 = g0 + cpos * 128
                    for half in range(2):
                        for k in range(8):
                            MM(banks[vb[half]][:, :], XN[:, k, t0:t0 + 128],
                               WIN[:, k, D + half * 512:D + (half + 1) * 512], k == 0, k == 7,
                               (WINb, XNb[k][g]), (BK[vb[half]],))
                    vg, vgb = VG.next()
                    vsq, vsqb = VSQ.next()
                    for half in range(2):
                        ACT(vg[:, half * 512:(half + 1) * 512], banks[vb[half]][:, :], AF.Gelu_apprx_tanh,
                            (BK[vb[half]],), (vgb,))
                    ACT(vsq[:, :], vg[:, :], AF.Square, (vgb,), (vsqb,))
                    RSUM(st[:, cpos:cpos + 1], vg[:, :], (vgb,), (stb,))
                    RSUM(st[:, 2 + cpos:3 + cpos], vsq[:, :], (vsqb,), (stb,))
                    chunks.append((t, vg, vgb))
                TS(st[:, 4:6], st[:, 0:2], 1.0 / D, ALU.mult, (stb,), (stb,))
                TT(st[:, 6:8], st[:, 4:6], st[:, 4:6], ALU.mult, (stb,), (stb,))
                STT(st[:, 8:10], st[:, 2:4], 1.0 / D, st[:, 6:8], ALU.mult, ALU.subtract, (stb,), (stb,))
                ACT(st[:, 10:12], st[:, 8:10], AF.Sqrt, (stb, CONSTb), (stb,), bias=EPS_LN)
                RECIP(st[:, 12:14], st[:, 10:12], (stb,), (stb,))
                out = []
                for cpos, (t, vg, vgb) in enumerate(chunks):
                    vh, vhb = VH.next()
                    TS(vh[:, :], vg[:, :], st[:, 4 + cpos:5 + cpos], ALU.subtract, (vgb, stb), (vhb,),
                       s2=st[:, 12 + cpos:13 + cpos], op1=ALU.mult)
                    out.append((t, vh, vhb))
                return out

            def emit_u(g, g0):
                ut, utb = UT.next()
                for cc in range(8):
                    bk = cc % 2
                    for k in range(8):
                        MM(banks[bk][:, 0:SUB], WIN[:, k, cc * 128:(cc + 1) * 128], XN[:, k, g0:g0 + SUB], k == 0,
                           k == 7, (WINb, XNb[k][g]), (BK[bk],))
                    ACT(ut[:, cc, 0:SUB], banks[bk][:, 0:SUB], AF.Gelu_apprx_tanh, (BK[bk],), (utb,))
                return ut, utb

            def emit_spatial(vhs, ut, utb):
                gt, gtb = GT.next()
                for cpos, (t, vh, vhb) in enumerate(vhs):
                    mbp = pairs[((t // 2) + (2 if cpos == 0 else 0)) % 3]
                    for gq in range(8):
                        bk = mbp[gq // 4]
                        MM(banks[bk][:, (gq % 4) * 128:(gq % 4 + 1) * 128], vh[:, gq * 128:(gq + 1) * 128],
                           WST[:, gq, :], True, True, (vhb, WSTb), (BK[bk],))
                    for gq in range(8):
                        bk = mbp[gq // 4]
                        mt, mtb = MT.next()
                        STT(mt[:, :], banks[bk][:, (gq % 4) * 128:(gq % 4 + 1) * 128],
                            sglng[:, j * 8 + gq:j * 8 + gq + 1], BIAS[:, gq, :], ALU.mult, ALU.add,
                            (BK[bk], SMALLb, BIASb), (mtb,))
                        TT(gt[:, gq, cpos * 128:(cpos + 1) * 128], mt[:, :],
                           ut[:, gq, cpos * 128:(cpos + 1) * 128], ALU.mult, (mtb, utb), (gtb,))
                return gt, gtb

            def emit_y(g, g0, gt, gtb):
                for d in range(8):
                    bk = d % 2
                    for cc in range(8):
                        MM(banks[bk][:, 0:SUB], WOUT[:, cc, d * 128:(d + 1) * 128], gt[:, cc, 0:SUB], cc == 0,
                           cc == 7, (WOUTb, gtb), (BK[bk],))
                    STT(XT[:, d, g0:g0 + SUB], banks[bk][:, 0:SUB], prm(slot, 2, g, d), XT[:, d, g0:g0 + SUB],
                        ALU.mult, ALU.add, (BK[bk], PRMb[slot], XTb[d][g]), (XTb[d][g],))

            prev_y = None
            for (g, g0) in subs:
                vhs = emit_v(g, g0)
                if prev_y is not None:
                    emit_y(*prev_y)
                ut, utb = emit_u(g, g0)
                gt, gtb = emit_spatial(vhs, ut, utb)
                prev_y = (g, g0, gt, gtb)
            emit_y(*prev_y)

        last_ctx_layer = 2
        done = False
        for i in range(nlayers):
            mode = "full" if i < last_ctx_layer else ("kv" if i == last_ctx_layer else "none")
            j = i // 2
            ffn_phase(i, 0, LAT + ([CTXG] if mode != "none" else []))
            if stop_stage == (i, 0):
                break
            if i % 2 == 0:
                attn_phase(i, j, LAT + ([CTXG] if mode == "full" else []), LAT + [CTXG])
            else:
                sgmlp_phase(i, j, LAT + ([CTXG] if mode == "full" else []))
            if stop_stage == (i, 1):
                break
            ffn_phase(i, 1, LAT + ([CTXG] if mode == "full" else []))

        SC.barrier()
        AR.reset()
        FG = AR.alloc((D,), F32)
        FGb = Buf("fg")
        fds = SC.new_dsem("fg")
        DMA("sp", FG, fg_d[:, :], (), (FGb,), fds)
        OST = Rot("ost", 2, (D,), F32, dma=True)
        FSQ = Rot("fsq", 2, (D,), F32)
        FS = Rot("fs", 4, (4,), F32)
        OUTb = Buf("outdram")
        out_ops = []
        for c in range(nrows_out // 128):
            g = min(c // 4, 4)
            for half in range(2):
                bk = (c % 2) * 2 + half
                for dd in range(4):
                    d = half * 4 + dd
                    TR(banks[bk][:, dd * 128:(dd + 1) * 128], XT[:, d, c * 128:(c + 1) * 128], (XTb[d][g], IDENTb),
                       (BK[bk],))
            ost, ostb, ods = OST.next()
            b0 = (c % 2) * 2
            if debug_dump:
                for half in range(2):
                    COPY(ost[:, half * 512:(half + 1) * 512], banks[b0 + half][:, :], (BK[b0 + half],), (ostb,),
                         eng=("act" if half else "dve"))
            else:
                fsq, fsqb = FSQ.next()
                fs, fsb = FS.next()
                for half in range(2):
                    ACT(fsq[:, half * 512:(half + 1) * 512], banks[b0 + half][:, :], AF.Square, (BK[b0 + half],),
                        (fsqb,))
                RSUM(fs[:, 0:1], fsq[:, :], (fsqb,), (fsb,))
                ACT(fs[:, 1:2], fs[:, 0:1], AF.Sqrt, (fsb, CONSTb), (fsb,), bias=EPS_RMS, scale=1.0 / D)
                RECIP(fs[:, 2:3], fs[:, 1:2], (fsb,), (fsb,))
                for half in range(2):
                    STT(ost[:, half * 512:(half + 1) * 512], banks[b0 + half][:, :], fs[:, 2:3],
                        FG[:, half * 512:(half + 1) * 512], ALU.mult, ALU.mult, (BK[b0 + half], fsb, FGb), (ostb,))
            o = DMA("sp", out_d[c * 128:(c + 1) * 128, :], ost, (ostb,), (), ods)
            out_ops.append(o)

        nwait = SC.emit()
        SC.final_wait("sp", out_ops)
        print("ops", len(SC.ops), "waits", nwait)
    return nc


def _rope_tables():
    n_freq = 16
    inv = (10000.0 ** (-np.arange(n_freq, dtype=np.float32) / n_freq)).astype(np.float32)
    t = np.arange(S)
    pos = np.stack([t // 64, t % 64], axis=-1).astype(np.float32)
    ang = pos[:, :, None] * inv
    cos = np.cos(ang).astype(np.float32).reshape(S, 32).T
    sin = np.sin(ang).astype(np.float32).reshape(S, 32).T
    ropeC = np.tile(cos, (4, 1))
    ropeT = np.concatenate([sin, -sin, sin, -sin], axis=0)
    return np.ascontiguousarray(ropeC), np.ascontiguousarray(ropeT)


def _head_perm():
    perm = np.zeros(128, dtype=np.int64)
    for r in range(2):
        for half in range(2):
            for axis in range(2):
                for f in range(16):
                    perm[r * 64 + half * 32 + axis * 16 + f] = r * 64 + axis * 32 + half * 16 + f
    return perm


def _fm(v):
    return np.ascontiguousarray(v.reshape(-1, 8, 128).transpose(2, 0, 1).reshape(128, -1))


_PROG_CACHE = {}


def prepare_inputs(inputs):
    f = lambda a: np.ascontiguousarray(np.asarray(a, dtype=np.float32))
    x = f(inputs["x"])
    c = f(inputs["c"])
    ctx = f(inputs["ctx"])
    c_ctx = f(inputs["c_ctx"])
    perm = _head_perm()
    da_w_in = f(inputs["da_w_in"]).copy()
    cols = np.arange(3 * D)
    for blk in range(2):
        for h in range(8):
            base = blk * D + h * 128
            cols[base:base + 128] = base + perm
    da_w_in = np.ascontiguousarray(da_w_in[:, :, cols])
    ropeC, ropeT = _rope_tables()
    b_mod = f(inputs["b_mod"])
    b_modT = np.ascontiguousarray(b_mod.reshape(DEPTH, 72, 128).transpose(2, 0, 1).reshape(128, DEPTH * 72))
    norm_g = f(inputs["norm_g"])
    norm_gT = np.ascontiguousarray(norm_g.reshape(DEPTH, 3, 8, 128).transpose(3, 0, 1, 2).reshape(128, DEPTH * 24))
    da_lambda_b = np.ascontiguousarray(np.tile(f(inputs["da_lambda"]).reshape(1, 512), (128, 1)))
    da_subln_gT = np.ascontiguousarray(f(inputs["da_subln_g"]).T)
    sg_ln_gT = np.ascontiguousarray(f(inputs["sg_ln_g"]).reshape(2, 8, 128).transpose(2, 0, 1).reshape(128, 16))
    sg_ln_bT = np.ascontiguousarray(f(inputs["sg_ln_b"]).reshape(2, 8, 128).transpose(2, 0, 1).reshape(128, 16))
    sg_w_sT = np.ascontiguousarray(f(inputs["sg_w_s"]).transpose(0, 3, 1, 2).reshape(2, 128, 8 * 128))
    sg_b_s_b = np.ascontiguousarray(np.tile(f(inputs["sg_b_s"]).reshape(1, 2 * 1024), (128, 1)))
    final_g_b = np.ascontiguousarray(np.tile(f(inputs["final_g"]).reshape(1, D), (128, 1)))
    shared = {
        "w_mod": f(inputs["w_mod"]), "b_modT": b_modT, "norm_gT": norm_gT,
        "w_ffn_gu": f(inputs["w_ffn_gu"]), "w_ffn_down": f(inputs["w_ffn_down"]),
        "da_w_in": da_w_in, "da_w_out": f(inputs["da_w_out"]), "da_lambda_b": da_lambda_b,
        "da_subln_gT": da_subln_gT, "sg_w_in": f(inputs["sg_w_in"]), "sg_ln_gT": sg_ln_gT, "sg_ln_bT": sg_ln_bT,
        "sg_w_sT": sg_w_sT, "sg_b_s_b": sg_b_s_b, "sg_w_out": f(inputs["sg_w_out"]), "final_g_b": final_g_b,
        "ropeC": ropeC, "ropeT": ropeT,
    }
    in_maps = []
    for b in range(NCORES):
        cT = np.concatenate([c[b].reshape(8, 128).T, c_ctx.reshape(8, 128).T], axis=1)
        m = dict(shared)
        m["x"] = x[b]
        m["ctx"] = ctx[b]
        m["cT"] = np.ascontiguousarray(cT)
        in_maps.append(m)
    return in_maps


def kernel(**inputs):
    in_maps = prepare_inputs(inputs)
    if "full" not in _PROG_CACHE:
        _PROG_CACHE["full"] = build_program()
    nc = _PROG_CACHE["full"]
    res = run_bass_kernel_spmd(nc, in_maps, core_ids=list(range(NCORES)))
    return np.stack([np.asarray(r["out"]) for r in res.results], axis=0).astype(np.float32)
```
